# Optimizing a Trainium2 kernel written in Bass

```python
import math
import jax
import jax.numpy as jnp
from jax import lax
import numpy as np

D_MODEL = 1024
BATCH = 16
SEQ = 2048
DEPTH = 2
DEC_BATCH = 128
DEC_SEQ = 1
PAST_LEN = 16384
PAGE_SIZE = 128

HEAD_DIM = 64
ROT_DIM = HEAD_DIM // 4
ROPE_THETA = 500000.0
BRANCH_W = 384
N_HEADS = BRANCH_W // HEAD_DIM
N_KV = 2
QW = N_HEADS * HEAD_DIM
KVW = N_KV * HEAD_DIM
WIN_A = 128
DIL_GROUPS = ((128, 1), (512, 4), (2048, 16))
N_DIL = len(DIL_GROUPS)
CONV_W = 3
CHUNK = 128
N_SG = 6
N_BRANCH = 4
N_IN = (QW + 2 * KVW) * (1 + N_DIL) + 5 * BRANCH_W
D_FF = -(-(8 * D_MODEL) // (3 * 256)) * 256
Q_BLOCK = 128
EPS = 1e-6

kernel_name = 'hybrid_gated_parallel_mixer_decode_step'


def rms_norm(x, g):
    xf = x.astype(jnp.float32)
    xf = xf * lax.rsqrt(jnp.mean(xf * xf, axis=-1, keepdims=True) + EPS)
    return (xf * g.astype(jnp.float32)).astype(x.dtype)


def partial_rope(x, pos):
    half = ROT_DIM // 2
    inv_freq = ROPE_THETA ** (-2.0 * jnp.arange(half, dtype=jnp.float32) / ROT_DIM)
    ang = pos[:, None] * inv_freq[None, :]
    cos = jnp.cos(ang)[None, :, None, :]
    sin = jnp.sin(ang)[None, :, None, :]
    xf = x.astype(jnp.float32)
    x1 = xf[..., :half]
    x2 = xf[..., half:ROT_DIM]
    out = jnp.concatenate([x1 * cos - x2 * sin, x2 * cos + x1 * sin, xf[..., ROT_DIM:]], axis=-1)
    return out.astype(x.dtype)


def split_columns(z):
    sizes = [QW, KVW, KVW] * (1 + N_DIL) + [BRANCH_W] * 5
    out, o = [], 0
    for s in sizes:
        out.append(z[..., o:o + s])
        o += s
    return out


def attn_heads(q, k, v, qn, kn, pos):
    B, S, _ = q.shape
    q = partial_rope(rms_norm(q.reshape(B, S, N_HEADS, HEAD_DIM), qn), pos)
    k = partial_rope(rms_norm(k.reshape(B, S, N_KV, HEAD_DIM), kn), pos)
    return q, k, v.reshape(B, S, N_KV, HEAD_DIM)


def sliding_window_attn(q, k, v, n_buf, sink):
    B, S, H, dh = q.shape
    G = H // N_KV
    pad = WIN_A - n_buf
    kp = jnp.pad(k, ((0, 0), (pad, 0), (0, 0), (0, 0)))
    vp = jnp.pad(v, ((0, 0), (pad, 0), (0, 0), (0, 0)))
    qb = math.gcd(S, Q_BLOCK)
    nb = S // qb
    span = WIN_A + qb
    qg = q.reshape(B, nb, qb, N_KV, G, dh)
    sink_l = jnp.broadcast_to(sink.astype(jnp.float32).reshape(1, N_KV, G, 1, 1), (B, N_KV, G, qb, 1))
    rel = WIN_A + jnp.arange(qb)[:, None] - jnp.arange(span)[None, :]
    in_band = (rel >= 0) & (rel <= WIN_A)
    scale = 1.0 / math.sqrt(dh)

    def block(i):
        start = i * qb
        kb = lax.dynamic_slice_in_dim(kp, start, span, axis=1)
        vb = lax.dynamic_slice_in_dim(vp, start, span, axis=1)
        s = jnp.einsum('bqkgd,bskd->bkgqs', qg[:, i], kb, preferred_element_type=jnp.float32) * scale
        real = (start + jnp.arange(span)) >= pad
        s = jnp.where(in_band & real[None, :], s, -jnp.inf)
        p = jax.nn.softmax(jnp.concatenate([s, sink_l], axis=-1), axis=-1)[..., :span]
        return jnp.einsum('bkgqs,bskd->bqkgd', p.astype(v.dtype), vb)

    o = lax.map(block, jnp.arange(nb))
    return jnp.moveaxis(o, 0, 1).reshape(B, S, H, dh)


def dilated_attn(q, k, v, n_buf, window, dil):
    B, S, H, dh = q.shape
    G = H // N_KV
    pad = window - n_buf
    kp = jnp.pad(k, ((0, 0), (pad, 0), (0, 0), (0, 0)))
    vp = jnp.pad(v, ((0, 0), (pad, 0), (0, 0), (0, 0)))
    nk = window // dil + 1
    qb = math.gcd(S, Q_BLOCK)
    nb = S // qb
    qg = q.reshape(B, nb, qb, N_KV, G, dh)
    steps = dil * jnp.arange(nk)
    scale = 1.0 / math.sqrt(dh)

    def block(i):
        kidx = i * qb + window + jnp.arange(qb)[:, None] - steps[None, :]
        kb = jnp.take(kp, kidx, axis=1)
        vb = jnp.take(vp, kidx, axis=1)
        s = jnp.einsum('bqkgd,bqnkd->bkgqn', qg[:, i], kb, preferred_element_type=jnp.float32) * scale
        s = jnp.where(kidx >= pad, s, -jnp.inf)
        lse = jax.nn.logsumexp(s, axis=-1)
        p = jnp.exp(s - lse[..., None])
        o = jnp.einsum('bkgqn,bqnkd->bqkgd', p.astype(v.dtype), vb)
        return o, jnp.moveaxis(lse, 3, 1)

    o, lse = lax.map(block, jnp.arange(nb))
    o = jnp.moveaxis(o, 0, 1).reshape(B, S, H, dh)
    lse = jnp.moveaxis(lse, 0, 1).reshape(B, S, H)
    return o, lse


def chunk_spatial_gate(u, v, ws, bs):
    B, S, C = v.shape
    sp = -(-S // CHUNK) * CHUNK
    vp = jnp.pad(v, ((0, 0), (0, sp - S), (0, 0))).reshape(B, sp // CHUNK, CHUNK, N_SG, C // N_SG)
    tri = jnp.tril(jnp.ones((CHUNK, CHUNK), dtype=bool))
    wm = jnp.where(tri[None], ws, 0.0).astype(v.dtype)
    sv = jnp.einsum('gij,bcjge->bcige', wm, vp) + bs.T.astype(v.dtype)[None, None, :, :, None]
    return u * sv.reshape(B, sp, C)[:, :S]


def trunk_layer(x, pos, buf_a, bufs_b, buf_c, ln1, w_in, q_norm_a, k_norm_a, sink_a, q_norm_b, k_norm_b,
                conv_c, ws_d, bs_d, w_gate, b_gate, w_branch, w_o, ln2, w_ffn_in, w_ffn_out):
    B, S, _ = x.shape
    h = rms_norm(x, ln1)
    cols = split_columns(h @ w_in)
    q, k, v = attn_heads(cols[0], cols[1], cols[2], q_norm_a, k_norm_a, pos)
    kv = jnp.concatenate([buf_a, jnp.stack([k, v], axis=1)], axis=2)
    o_a = sliding_window_attn(q, kv[:, 0], kv[:, 1], buf_a.shape[2], sink_a)
    new_a = kv[:, :, kv.shape[2] - min(WIN_A, kv.shape[2]):]
    outs, lses, new_b = [], [], []
    for g, (window, dil) in enumerate(DIL_GROUPS):
        c = 3 + 3 * g
        q, k, v = attn_heads(cols[c], cols[c + 1], cols[c + 2], q_norm_b[g], k_norm_b[g], pos)
        kv = jnp.concatenate([bufs_b[g], jnp.stack([k, v], axis=1)], axis=2)
        o, lse = dilated_attn(q, kv[:, 0], kv[:, 1], bufs_b[g].shape[2], window, dil)
        outs.append(o)
        lses.append(lse)
        new_b.append(kv[:, :, kv.shape[2] - min(window, kv.shape[2]):])
    wts = jax.nn.softmax(jnp.stack(lses), axis=0)
    o_b = jnp.einsum('gbsh,gbshd->bshd', wts, jnp.stack(outs).astype(jnp.float32)).astype(x.dtype)
    c = 3 + 3 * N_DIL
    gate_b, gate_c, x_c = cols[c], cols[c + 1], cols[c + 2]
    zc = jnp.concatenate([buf_c, gate_c * x_c], axis=1)
    conv = sum(conv_c[j] * zc[:, j:j + S] for j in range(CONV_W))
    o_c = gate_b * conv
    new_c = zc[:, S:]
    u_d, v_d = cols[c + 3], cols[c + 4]
    o_d = chunk_spatial_gate(u_d, v_d, ws_d, bs_d)
    branches = (o_a.reshape(B, S, BRANCH_W), o_b.reshape(B, S, BRANCH_W), o_c, o_d)
    merged = sum(jax.nn.sigmoid((h @ w_gate[n] + b_gate[n]).astype(jnp.float32)).astype(x.dtype) * (br @ w_branch[n])
                 for n, br in enumerate(branches))
    x = x + merged @ w_o
    h2 = rms_norm(x, ln2)
    gt, up = jnp.split(h2 @ w_ffn_in, 2, axis=-1)
    x = x + (jax.nn.silu(gt) * up) @ w_ffn_out
    return x, new_a, new_b, new_c, v_d


def setup_inputs(seed: int = 0) -> dict:
    key = jax.random.key(seed)
    ks = jax.random.split(key, 24)
    f32 = jnp.float32

    def normal(k, shape, scale):
        return scale * jax.random.normal(k, shape, f32)

    def gain(k, shape):
        return 1.0 + 0.1 * jax.random.normal(k, shape, f32)

    la = min(WIN_A, PAST_LEN)
    lb = [min(w, PAST_LEN) for w, _ in DIL_GROUPS]
    return {
        'x_prompt': normal(ks[0], (BATCH, SEQ, D_MODEL), 1.0),
        'x_sample': normal(ks[1], (DEC_BATCH, DEC_SEQ, D_MODEL), 1.0),
        'cache_a': normal(ks[2], (DEPTH, DEC_BATCH, 2, la, N_KV, HEAD_DIM), 1.0),
        'cache_b1': normal(ks[3], (DEPTH, DEC_BATCH, 2, lb[0], N_KV, HEAD_DIM), 1.0),
        'cache_b2': normal(ks[4], (DEPTH, DEC_BATCH, 2, lb[1], N_KV, HEAD_DIM), 1.0),
        'cache_b3': normal(ks[5], (DEPTH, DEC_BATCH, 2, lb[2], N_KV, HEAD_DIM), 1.0),
        'state_c': normal(ks[6], (DEPTH, DEC_BATCH, CONV_W - 1, BRANCH_W), 1.0),
        'ln1': gain(ks[7], (DEPTH, D_MODEL)),
        'w_in': normal(ks[8], (DEPTH, D_MODEL, N_IN), D_MODEL ** -0.5),
        'q_norm_a': gain(ks[9], (DEPTH, HEAD_DIM)),
        'k_norm_a': gain(ks[10], (DEPTH, HEAD_DIM)),
        'sink_a': normal(ks[11], (DEPTH, N_HEADS), 1.0),
        'q_norm_b': gain(ks[12], (DEPTH, N_DIL, HEAD_DIM)),
        'k_norm_b': gain(ks[13], (DEPTH, N_DIL, HEAD_DIM)),
        'conv_c': normal(ks[14], (DEPTH, CONV_W, BRANCH_W), CONV_W ** -0.5),
        'ws_d': normal(ks[15], (DEPTH, N_SG, CHUNK, CHUNK), CHUNK ** -0.5),
        'bs_d': 1.0 + normal(ks[16], (DEPTH, N_SG, CHUNK), 0.1),
        'w_gate': normal(ks[17], (DEPTH, N_BRANCH, D_MODEL, D_MODEL), D_MODEL ** -0.5),
        'b_gate': normal(ks[18], (DEPTH, N_BRANCH, D_MODEL), 0.02),
        'w_branch': normal(ks[19], (DEPTH, N_BRANCH, BRANCH_W, D_MODEL), BRANCH_W ** -0.5),
        'w_o': normal(ks[20], (DEPTH, D_MODEL, D_MODEL), D_MODEL ** -0.5),
        'ln2': gain(ks[21], (DEPTH, D_MODEL)),
        'w_ffn_in': normal(ks[22], (DEPTH, D_MODEL, 2 * D_FF), D_MODEL ** -0.5),
        'w_ffn_out': normal(ks[23], (DEPTH, D_FF, D_MODEL), D_FF ** -0.5),
    }


def reference(x_prompt, x_sample, cache_a, cache_b1, cache_b2, cache_b3, state_c, ln1, w_in, q_norm_a, k_norm_a,
              sink_a, q_norm_b, k_norm_b, conv_c, ws_d, bs_d, w_gate, b_gate, w_branch, w_o, ln2, w_ffn_in,
              w_ffn_out):
    bp, sp, _ = x_prompt.shape
    ds = x_sample.shape[1]
    dt = x_prompt.dtype
    pos_p = jnp.arange(sp, dtype=jnp.float32)
    pos_s = PAST_LEN + jnp.arange(ds, dtype=jnp.float32)
    empty_kv = jnp.zeros((bp, 2, 0, N_KV, HEAD_DIM), dt)
    zero_conv = jnp.zeros((bp, CONV_W - 1, BRANCH_W), dt)
    caches_b = (cache_b1, cache_b2, cache_b3)
    y_p, y_s = x_prompt, x_sample
    a_p, a_s, c_p, c_s, d_s = [], [], [], [], []
    b_p = [[] for _ in DIL_GROUPS]
    b_s = [[] for _ in DIL_GROUPS]
    for l in range(DEPTH):
        lw = (ln1[l], w_in[l], q_norm_a[l], k_norm_a[l], sink_a[l], q_norm_b[l], k_norm_b[l], conv_c[l],
              ws_d[l], bs_d[l], w_gate[l], b_gate[l], w_branch[l], w_o[l], ln2[l], w_ffn_in[l], w_ffn_out[l])
        y_p, na, nb, nc, _ = trunk_layer(y_p, pos_p, empty_kv, (empty_kv,) * N_DIL, zero_conv, *lw)
        y_s, na2, nb2, nc2, vd = trunk_layer(y_s, pos_s, cache_a[l], tuple(cb[l] for cb in caches_b),
                                             state_c[l], *lw)
        a_p.append(na)
        a_s.append(na2)
        for g in range(N_DIL):
            b_p[g].append(nb[g])
            b_s[g].append(nb2[g])
        c_p.append(nc)
        c_s.append(nc2)
        d_s.append(vd)
    return (y_p, y_s, jnp.stack(a_p), jnp.stack(a_s), jnp.stack(b_p[0]), jnp.stack(b_s[0]),
            jnp.stack(b_p[1]), jnp.stack(b_s[1]), jnp.stack(b_p[2]), jnp.stack(b_s[2]),
            jnp.stack(c_p), jnp.stack(c_s), jnp.stack(d_s))
```

```python
import math
from contextlib import ExitStack
from functools import partial

import numpy as np
import concourse.bass as bass
import concourse.mybir as mybir
from concourse.bass_utils import run_bass_kernel_spmd

F32 = mybir.dt.float32
BF16 = mybir.dt.bfloat16
AF = mybir.ActivationFunctionType
ALU = mybir.AluOpType
AX = mybir.AxisListType

D = 1024
SEQ = 2048
DEPTH = 2
NSEQ = 2
NS = 16
PAST = 16384
T = 512
NT = SEQ // T
NIN = 4480
DFF = 2816
EPS = 1e-6
WSLOT = 5632
NWS = 4
COPY_START_TL = 4
STRICT_SAME_ENG = True
EW2 = "dve"
DILS = (1, 1, 4, 16)
CROWS = (128, 128, 512, 2048)


class Res:
    __slots__ = ("name", "w", "r_eng", "r_dma")

    def __init__(self, name):
        self.name = name
        self.w = None
        self.r_eng = {}
        self.r_dma = []


class DSem:
    __slots__ = ("sem", "total")

    def __init__(self, sem):
        self.sem = sem
        self.total = 0


class Op:
    __slots__ = ("eng", "fn", "deps", "dsem", "val", "need_inc", "cnt")


class Sched:
    ENGS = ("pe", "act", "dve", "pool", "sp")

    def __init__(self, nc, es):
        self.nc = nc
        self.ops = {e: [] for e in self.ENGS}
        self.esem = {e: es.enter_context(nc.semaphore("sem_" + e)) for e in self.ENGS}
        self.dsems = []
        self.es = es
        self.nops = 0

    def new_dsem(self, name):
        d = DSem(self.es.enter_context(self.nc.semaphore("d_" + name)))
        self.dsems.append(d)
        return d

    def op(self, eng, fn, reads=(), writes=(), dsem=None):
        o = Op()
        o.eng = eng
        o.fn = fn
        o.dsem = dsem
        o.need_inc = False
        o.cnt = None
        o.val = None
        if dsem is not None:
            dsem.total += 16
            o.val = dsem.total
        raw = set()
        waw = set()
        war = set()
        deps = set()
        for r in reads:
            if r.w is not None:
                deps.add(r.w)
                raw.add(r.w)
        for w in writes:
            if w.w is not None:
                deps.add(w.w)
                waw.add(w.w)
            deps.update(w.r_eng.values())
            deps.update(w.r_dma)
            war.update(w.r_eng.values())
            war.update(w.r_dma)
        final = []
        for d in deps:
            if d is o:
                continue
            if d.dsem is None and d.eng == eng:
                if eng == "pe":
                    continue
                if d not in raw and dsem is None and not STRICT_SAME_ENG:
                    continue
            if d.dsem is not None and d.dsem is dsem and d in waw and d not in raw and d not in war:
                continue
            if d.dsem is None:
                d.need_inc = True
            final.append(d)
        o.deps = final
        for r in reads:
            if dsem is not None:
                r.r_dma.append(o)
            else:
                r.r_eng[eng] = o
        for w in writes:
            w.w = o
            w.r_eng = {}
            w.r_dma = []
        self.ops[eng].append(o)
        self.nops += 1
        return o

    def emit(self, block):
        nc = self.nc
        for eng in self.ENGS:
            c = 0
            for o in self.ops[eng]:
                if o.dsem is None and o.need_inc:
                    c += 1
                    o.cnt = c

        def run(eng, e):
            waited = {}
            for o in self.ops[eng]:
                need = {}
                for d in o.deps:
                    if d.dsem is not None:
                        sem, val = d.dsem.sem, d.val
                    else:
                        sem, val = self.esem[d.eng], d.cnt
                    k = id(sem)
                    if waited.get(k, 0) >= val:
                        continue
                    if k not in need or need[k][1] < val:
                        need[k] = (sem, val)
                ws = list(need.values())
                for k, (sem, val) in need.items():
                    waited[k] = val
                for sem, val in ws[1:]:
                    e.wait_ge(sem, val)
                inst = o.fn(e)
                if ws:
                    inst._wait_ge(ws[0][0], ws[0][1])
                if o.dsem is not None:
                    inst.then_inc(o.dsem.sem, 16)
                elif o.need_inc:
                    inst.then_inc(self.esem[eng], 1)
            if eng == "sp":
                for d in self.dsems:
                    if d.total > 0:
                        e.wait_ge(d.sem, d.total)
                for en in self.ENGS:
                    if en == "sp":
                        continue
                    last = None
                    for o in self.ops[en]:
                        if o.cnt is not None:
                            last = o.cnt
                    if last:
                        e.wait_ge(self.esem[en], last)

        @block.tensor
        def _(e):
            run("pe", e)

        @block.scalar
        def _(e):
            run("act", e)

        @block.vector
        def _(e):
            run("dve", e)

        @block.gpsimd
        def _(e):
            run("pool", e)

        @block.sync
        def _(e):
            run("sp", e)


class Ring:
    def __init__(self, items):
        self.items = items
        self.i = 0

    def next(self):
        x = self.items[self.i % len(self.items)]
        self.i += 1
        return x


def AP(t, off, dims):
    return bass.AP(t, off, [list(d) for d in dims])


def _rope_tab(pos):
    half = 8
    inv = (np.float32(500000.0) ** (-2.0 * np.arange(half, dtype=np.float32) / np.float32(16))).astype(np.float32)
    ang = (pos.astype(np.float32)[:, None] * inv[None, :]).astype(np.float32)
    return np.cos(ang).astype(np.float32), np.sin(ang).astype(np.float32)


def _block_positions(order, blk):
    p = np.arange(128)
    t, b = divmod(blk, 4)
    if order == 0:
        return 128 * blk + p
    if order == 1:
        return 512 * t + b + 4 * p
    return 512 * t + 4 * b + (p // 32) + 16 * (p % 32)


def make_consts():
    tabs = np.zeros((128, 3, 16, 2, 8), np.float32)
    for o in range(3):
        for blk in range(16):
            c, s = _rope_tab(_block_positions(o, blk))
            tabs[:, o, blk, 0] = c
            tabs[:, o, blk, 1] = s
    cs, ss = _rope_tab(np.full((NS,), PAST))
    tabs_s = np.zeros((NS, 2, 8), np.float32)
    tabs_s[:, 0] = cs
    tabs_s[:, 1] = ss
    s = np.arange(128)[:, None]
    q = np.arange(128)[None, :]
    m_prev = (s >= q).astype(np.float32)
    m_cur = (s <= q).astype(np.float32)
    same = ((s % 4) == (q % 4)).astype(np.float32)
    m3_prev = same
    m3_cur = same * (s <= q).astype(np.float32)
    masks = np.stack([m_prev, m_cur, m3_prev, m3_cur], 0)
    masks = np.repeat(masks[:, :, None, :], 3, axis=2)
    masks = np.ascontiguousarray(masks.transpose(1, 0, 2, 3)).reshape(128, 4 * 384)
    ident = np.eye(128, dtype=np.float32)
    return {"c_tabs": tabs.reshape(128, -1), "c_tabs_s": tabs_s.reshape(NS, -1), "c_masks": masks,
            "c_ident": ident}


def build(nseq=NSEQ, ns=NS, depth=DEPTH, do_copy=True, phases=99, ntiles=NT):
    nc = bass.Bass("TRN2", target_bir_lowering=False)
    es = ExitStack()
    with es:
        S = Sched(nc, es)

        def din(name, shape, dt=F32):
            return nc.dram_tensor(name, list(shape), dt, kind="ExternalInput").ap()

        def dout(name, shape, dt=F32):
            return nc.dram_tensor(name, list(shape), dt, kind="ExternalOutput").ap()

        nsq = max(nseq, 1)
        nsa = max(ns, 1)
        x_p = din("x_prompt", [nsq, SEQ, D])
        x_s = din("x_sample", [nsa, D])
        caches = [din("cache_a", [depth, nsa, 2, 128, 128]), din("cache_b1", [depth, nsa, 2, 128, 128]),
                  din("cache_b2", [depth, nsa, 2, 512, 128]), din("cache_b3", [depth, nsa, 2, 2048, 128])]
        state_c = din("state_c", [depth, nsa, 2, 384])
        ln1 = din("ln1", [depth, D])
        w_in = din("w_in", [depth, D, NIN])
        qn_a = din("q_norm_a", [depth, 64])
        kn_a = din("k_norm_a", [depth, 64])
        sink_a = din("sink_a", [depth, 6])
        qn_b = din("q_norm_b", [depth, 3, 64])
        kn_b = din("k_norm_b", [depth, 3, 64])
        conv_c = din("conv_c", [depth, 3, 384])
        ws_d = din("ws_d", [depth, 6, 128, 128])
        bs_d = din("bs_d", [depth, 6, 128])
        w_gate = din("w_gate", [depth, 4, D, D])
        b_gate = din("b_gate", [depth, 4, D])
        w_branch = din("w_branch", [depth, 4, 384, D])
        w_o = din("w_o", [depth, D, D])
        ln2 = din("ln2", [depth, D])
        w_fi = din("w_ffn_in", [depth, D, 2 * DFF])
        w_fo = din("w_ffn_out", [depth, DFF, D])
        c_tabs = din("c_tabs", [128, 3 * 16 * 16])
        c_tabs_s = din("c_tabs_s", [NS, 16])
        c_masks = din("c_masks", [128, 4 * 384])
        c_ident = din("c_ident", [128, 128])

        y_p = dout("y_prompt", [nsq, SEQ, D])
        y_s = dout("y_sample", [nsa, D])
        new_p = [dout("new_a_prompt", [depth, nsq, 2, 128, 128]), dout("new_b1_prompt", [depth, nsq, 2, 128, 128]),
                 dout("new_b2_prompt", [depth, nsq, 2, 512, 128]), dout("new_b3_prompt", [depth, nsq, 2, 2048, 128])]
        new_s = [dout("new_a_sample", [depth, nsa, 2, 128, 128]), dout("new_b1_sample", [depth, nsa, 2, 128, 128]),
                 dout("new_b2_sample", [depth, nsa, 2, 512, 128]), dout("new_b3_sample", [depth, nsa, 2, 2048, 128])]
        new_c_p = dout("new_c_prompt", [depth, nsq, 2, 384])
        new_c_s = dout("new_c_sample", [depth, nsa, 2, 384])
        new_d_s = dout("new_d_sample", [depth, nsa, 384])
        q_scr = nc.dram_tensor("q_scr", [4, NS, 384], F32, kind="Internal").ap()

        def sb(name, shape, dt):
            return es.enter_context(nc.sbuf_tensor(name, list(shape), dt))

        xb = [sb(f"xb{i}", [128, D], F32) for i in range(4)]
        xb_r = [Res(f"xb{i}") for i in range(4)]
        xb_ds = [S.new_dsem(f"xb{i}") for i in range(4)]
        hT = sb("hT", [128, 8, T], BF16)
        hT_r = Res("hT")
        oT = [sb(f"oT{i}", [128, 3, T], BF16) for i in range(4)]
        oT_r = [Res(f"oT{i}") for i in range(4)]
        bacc = sb("bacc", [128, 2 * 3 * T], F32)
        bacc_r = Res("bacc")
        actT = bacc[:].bitcast(BF16)
        mT = sb("mT", [128, 8, T], BF16)
        mT_r = Res("mT")
        KT = [sb(f"KT{g}", [128, (8 if g < 3 else 16) * 128], BF16) for g in range(4)]
        KT_r = [[Res(f"KT{g}_{i}") for i in range(8 if g < 3 else 16)] for g in range(4)]
        Vt = [sb(f"Vt{g}", [128, (8 if g < 3 else 16), 192], BF16) for g in range(4)]
        Vt_r = [[Res(f"Vt{g}_{i}") for i in range(8 if g < 3 else 16)] for g in range(4)]
        QT = sb("QT", [128, 4, 384], BF16)
        QT_r = [Res(f"QT{b}") for b in range(4)]
        wsl = [sb(f"wsl{i}", [128, WSLOT], BF16) for i in range(NWS)]
        wsl_r = [Res(f"wsl{i}") for i in range(NWS)]
        wsl_ds = [S.new_dsem(f"wsl{i}") for i in range(NWS)]
        g_bc = sb("g_bc", [128, D], F32)
        g_bc_r = Res("g_bc")
        gqk = sb("gqk", [128, 4, 2, 64], F32)
        tabs = sb("tabs", [128, 3, 16, 2, 8], F32)
        tabs_s = sb("tabs_s", [NS, 2, 8], F32)
        masks = sb("masks", [128, 4, 384], BF16)
        ident = sb("ident", [128, 128], BF16)
        identf = sb("identf", [128, 128], F32)
        wmT = sb("wmT", [128, 6, 128], BF16)
        wsnat = sb("wsnat", [128, 6, 128], F32)
        lrow = sb("lrow", [48, 128], F32)
        lcol = sb("lcol", [128, 48], F32)
        es_t = sb("es_t", [128, 6], F32)
        es_rows = sb("es_rows", [1, 2, 3, 128], BF16)
        sel = sb("sel", [1, 2, 128], BF16)
        epst = sb("epst", [128, 1], F32)
        lay_r = Res("layer_consts")
        const_r = Res("consts")
        xn = [sb(f"xn{i}", [128, D], BF16) for i in range(2)]
        xn_ring = Ring([(xn[i], Res(f"xn{i}")) for i in range(2)])
        junk = sb("junk", [128, D], BF16)
        junk_r = Res("junk")
        ssq = sb("ssq", [128, 8], F32)
        ssq_ring = Ring([(ssq[:, 2 * i:2 * i + 2], Res(f"ssq{i}")) for i in range(4)])
        qk = [sb(f"qk{i}", [128, 8, 64], F32) for i in range(2)]
        qk_ring = Ring([(qk[i], Res(f"qk{i}"), S.new_dsem(f"qk{i}")) for i in range(2)])
        vst = [sb(f"vst{i}", [128, 128], F32) for i in range(2)]
        vst_ring = Ring([(vst[i], Res(f"vst{i}"), S.new_dsem(f"vst{i}")) for i in range(2)])
        ss8 = sb("ss8", [128, 4, 8], F32)
        ss8_ring = Ring([(ss8[:, i, :], Res(f"ss8_{i}")) for i in range(4)])
        ropet = sb("ropet", [128, 2, 2, 8, 16], F32)
        rope_ring = Ring([(ropet[:, i], Res(f"rope{i}")) for i in range(2)])
        qbf = sb("qbf", [128, 3, 3, 128], BF16)
        kbf = sb("kbf", [128, 3, 128], BF16)
        qbf_ring = Ring([(qbf[:, i], kbf[:, i], Res(f"qbf{i}")) for i in range(3)])
        pT = sb("pT", [128, 4, 384], BF16)
        pT_ring = Ring([(pT[:, i], Res(f"pT{i}")) for i in range(4)])
        ftmp = sb("ftmp", [128, 5, T], F32)
        f_ring = Ring([(ftmp[:, i], Res(f"ftmp{i}")) for i in range(5)])
        zc = sb("zc", [128, 3, T + 2], F32)
        zc_r = [Res(f"zc{c}") for c in range(3)]
        zc_ds = [S.new_dsem(f"zc{c}") for c in range(3)]
        vbf = sb("vbf", [128, 2, 384], BF16)
        obf = sb("obf", [128, 2, 384], BF16)
        vbf_ring = Ring([(vbf[:, i], Res(f"vbf{i}")) for i in range(2)])
        obf_ring = Ring([(obf[:, i], Res(f"obf{i}")) for i in range(2)])

        hTs = sb("hTs", [128, 8, NS], BF16)
        hTs_r = Res("hTs")
        oTs = [sb(f"oTs{i}", [128, 3, NS], BF16) for i in range(4)]
        oTs_r = [Res(f"oTs{i}") for i in range(4)]
        mTs = sb("mTs", [128, 8, NS], BF16)
        mTs_r = Res("mTs")
        actTs = sb("actTs", [128, 11 * NS], BF16)
        actTs_r = Res("actTs")
        selfT = sb("selfT", [64, 10 * NS], F32)
        self_r = Res("selfT")
        nacc = sb("nacc", [64, 2, 96], F32)
        nacc_r = Res("nacc")
        s4t = sb("s4t", [128, 2, 24], F32)
        s4_ring = Ring([(s4t[:, i, :], Res(f"s4_{i}")) for i in range(2)])
        sm6 = sb("sm6", [NS, 2, 6], F32)
        sm_r = Res("sm6")
        sm_ds = S.new_dsem("sm6")
        onesf = sb("onesf", [128, 64], F32)
        kc_ds = S.new_dsem("kc4")
        qb_ds = S.new_dsem("qb4")
        qscr_r = [Res(f"qscr{g}") for g in range(4)]
        gqk_s = sb("gqk_s", [128, 4, 2, 64], F32)
        lrow_s = sb("lrow_s", [48, 128], F32)
        lcol_s = sb("lcol_s", [128, 48], F32)
        es_t_s = sb("es_t_s", [128, 6], F32)
        lds_ = S.new_dsem("layc")
        LC_P = {"gqk": gqk, "lrow": lrow, "lcol": lcol, "es_t": es_t, "r": lay_r, "ds": lds_, "full": True}
        LC_S = {"gqk": gqk_s, "lrow": lrow_s, "lcol": lcol_s, "es_t": es_t_s, "r": Res("lay_s"),
                "ds": S.new_dsem("layc_s"), "full": False}
        sscr = sb("sscr", [128, 2048], F32)
        sscr_r = Res("sscr")
        sscr_ds = S.new_dsem("sscr")
        Kc4 = sscr[:, 0:256]
        Vc4 = sscr[:, 256:512]
        zv_t = sb("zv_t", [NS, 768], F32)
        zv_r = Res("zv_t")
        zv_ds = S.new_dsem("zv_t")
        xs_t = sb("xs_t", [NS, D], F32)
        xs_r_ = Res("xs_t")
        xs_ds = S.new_dsem("xs_t")

        psb = [es.enter_context(nc.psum_tensor(f"ps{i}", [128, 512], F32)) for i in range(8)]
        ps_ring = Ring([(psb[i], Res(f"ps{i}")) for i in range(8)])

        class _TpRing:
            def next(self):
                t_, r_ = ps_ring.next()
                return t_[:].bitcast(BF16)[:, 0:512], r_
        tp_ring = _TpRing()

        def mm(out, lhsT, rhs, start, stop, reads, writes):
            S.op("pe", lambda e: e.matmul(out, lhsT, rhs, start=start, stop=stop), reads, writes)

        def tr(out, in_, idn, reads, writes):
            S.op("pe", lambda e: e.transpose(out, in_, idn), reads, writes)

        def act(out, in_, func, reads, writes, bias=None, scale=None, accum=None):
            kw = {}
            if bias is not None:
                kw["bias"] = bias
            if scale is not None:
                kw["scale"] = scale
            if accum is not None:
                kw["accum_out"] = accum
            S.op("act", lambda e: e.activation(out=out, in_=in_, func=func, **kw), reads, writes)

        def tt(eng, out, in0, in1, op, reads, writes):
            S.op(eng, lambda e: e.tensor_tensor(out=out, in0=in0, in1=in1, op=op), reads, writes)

        def ts(eng, out, in0, s1, s2, op0, op1, reads, writes):
            if op1 is None:
                S.op(eng, lambda e: e.tensor_scalar(out=out, in0=in0, scalar1=s1, scalar2=None, op0=op0), reads, writes)
            else:
                S.op(eng, lambda e: e.tensor_scalar(out=out, in0=in0, scalar1=s1, scalar2=s2, op0=op0, op1=op1),
                     reads, writes)

        def stt(eng, out, in0, scalar, in1, op0, op1, reads, writes):
            S.op(eng, lambda e: e.scalar_tensor_tensor(out=out, in0=in0, scalar=scalar, in1=in1, op0=op0, op1=op1),
                 reads, writes)

        def cp(eng, out, in_, reads, writes):
            if eng == "act":
                act(out, in_, AF.Copy, reads, writes)
            else:
                S.op(eng, lambda e: e.tensor_copy(out=out, in_=in_), reads, writes)

        def red(out, in_, reads, writes):
            S.op("dve", lambda e: e.tensor_reduce(out=out, in_=in_, axis=AX.X, op=ALU.add), reads, writes)

        def recip(out, in_, reads, writes, use_act=True):
            if use_act:
                act(out, in_, AF.Ln, reads, writes)
                act(out, out, AF.Exp, writes, writes, scale=-1.0)
            else:
                S.op("dve", lambda e: e.reciprocal(out=out, in_=in_), reads, writes)

        def memset(eng, ap, val, writes):
            S.op(eng, lambda e: e.memset(ap, val), (), writes)

        def dma(q, out, in_, dsem, reads, writes, nonc=False):
            if nonc:
                S.op(q, lambda e: e.dma_start(out=out, in_=in_, allow_slow_non_contiguous=True), reads, writes, dsem=dsem)
            else:
                S.op(q, lambda e: e.dma_start(out=out, in_=in_), reads, writes, dsem=dsem)

        def bcast_rows(ap1d, n, parts=128):
            return AP(ap1d.tensor, ap1d.offset, [[0, parts], [1, n]])

        cds = S.new_dsem("consts")
        gds = S.new_dsem("gbc")
        dma("sp", tabs[:].rearrange("p a b c d -> p (a b c d)"), c_tabs[:, :], cds, (), (const_r,))
        dma("sp", tabs_s[:].rearrange("p c d -> p (c d)"), c_tabs_s[:, :], cds, (), (const_r,))
        dma("sp", identf[:], c_ident[:, :], cds, (), (const_r,))
        cds2 = S.new_dsem("consts2")
        const2_r = Res("consts2")
        dma("pool", masks[:].rearrange("p a b -> p (a b)"), c_masks[:, :], cds2, (), (const2_r,))
        dma("pool", ident[:], c_ident[:, :], cds2, (), (const2_r,))
        S.op("dve", lambda e: e.memset(epst[:], EPS), (const2_r,), (const_r,))
        memset("dve", sel[:], 1.0, (const_r,))
        memset("pool", onesf[:], 1.0, (const_r,))
        memset("dve", sel[0:1, 0, 0:64], 0.0, (const_r,))
        memset("dve", sel[0:1, 1, 64:128], 0.0, (const_r,))
        for g in range(4):
            memset("pool", Vt[g][:, :, 64:128], 1.0, [r for r in Vt_r[g]])
        dram_y_r = {}

        wq_state = {"i": 0}

        wl_marks = []

        def wload(parts):
            i = wq_state["i"] % NWS
            wq_state["i"] += 1
            t, r, ds = wsl[i], wsl_r[i], wsl_ds[i]
            wl_marks.append((len(S.ops["pool"]), len(parts)))
            for (off, dims, npart, p0, src) in parts:
                dst = AP(t, p0 * WSLOT + off, [[WSLOT, npart]] + dims)
                dma("pool", dst, src, ds, (), (r,))
            return t, r

        def hoist_wloads(dist):
            lst = S.ops["pool"]
            groups = {}
            skip = set()
            for k, (idx, n) in enumerate(wl_marks):
                tgt = wl_marks[k - dist][0] if k >= dist else idx
                groups.setdefault(tgt, []).extend(lst[idx:idx + n])
                skip.update(range(idx, idx + n))
            new = []
            for i, o in enumerate(lst):
                if i in groups:
                    new.extend(groups[i])
                if i not in skip:
                    new.append(o)
            assert len(new) == len(lst)
            S.ops["pool"] = new

        def wsrc(w2d, r0, nk, c0, ncol):
            rs = w2d.ap[0][0]
            return AP(w2d.tensor, w2d.offset + r0 * rs + c0, [[rs, 128], [128 * rs, nk], [1, ncol]])

        def load_layer_consts(l, lc):
            lds = lc["ds"]
            lr = lc["r"]
            W = (lr,)
            g_, lrow_, lcol_, es_ = lc["gqk"], lc["lrow"], lc["lcol"], lc["es_t"]
            dma("sp", g_[:, 0, 0, :], bcast_rows(qn_a[l, :], 64), lds, (), W)
            dma("sp", g_[:, 0, 1, :], bcast_rows(kn_a[l, :], 64), lds, (), W)
            for g in range(3):
                dma("sp", g_[:, 1 + g, 0, :], bcast_rows(qn_b[l, g, :], 64), lds, (), W)
                dma("sp", g_[:, 1 + g, 1, :], bcast_rows(kn_b[l, g, :], 64), lds, (), W)
            dma("sp", es_[:], bcast_rows(sink_a[l, :], 6), lds, (), W)
            dma("sp", lrow_[0:32, :], b_gate[l].rearrange("n (m p) -> (n m) p", p=128), lds, (), W)
            dma("sp", lrow_[32:41, :], conv_c[l].rearrange("j (c p) -> (j c) p", p=128), lds, (), W)
            dma("sp", lrow_[41:47, :], bs_d[l], lds, (), W)
            if lc["full"]:
                dma("sp", wsnat[:], ws_d[l].rearrange("g i j -> i g j"), lds, (), W)
            act(es_[:], es_[:], AF.Exp, (lr,), (lr,))
            if lc["full"]:
                cp("dve", es_rows[0:1].rearrange("o k g q -> o (k g) q"),
                   AP(es_, 0, [[6, 1], [1, 6], [0, 128]]), (lr,), (lr,))
            pt, pr = ps_ring.next()
            tr(pt[:, 0:47], lrow_[0:47, :], identf[0:47, 0:47], (lr, const_r), (pr,))
            cp("dve", lcol_[:, 0:47], pt[:, 0:47], (pr,), (lr,))
            if lc["full"]:
                for g in range(6):
                    pt, pr = ps_ring.next()
                    tr(pt[:, 0:128], wsnat[:, g, :], identf[:], (lr, const_r), (pr,))
                    tt("dve", wmT[:, g, :], pt[:, 0:128], masks[:, 1, 0:128], ALU.mult, (pr, const_r), (lr,))

        def do_norm(P, nblk, xsrc, gsrc_ap, hdst):
            dma("sp", g_bc[:], bcast_rows(gsrc_ap, D), gds, (), (g_bc_r,))
            ht, hr = hdst
            for b in range(nblk):
                xa, xr = xsrc[b]
                ssa, ssr = ssq_ring.next()
                memset("dve", ssa[0:P, :], 0.0, (ssr,))
                act(junk[0:P, :], xa, AF.Square, (xr,), (junk_r, ssr), accum=ssa[0:P, 0:1])
                act(ssa[0:P, 1:2], ssa[0:P, 0:1], AF.Ln, (ssr, const_r), (ssr,), scale=1.0 / D, bias=epst[0:P, :])
                act(ssa[0:P, 1:2], ssa[0:P, 1:2], AF.Exp, (ssr,), (ssr,), scale=-0.5)
                xt, xtr = xn_ring.next()
                stt("dve", xt[0:P, :], xa, ssa[0:P, 1:2], g_bc[0:P, :], ALU.mult, ALU.mult, (xr, ssr, g_bc_r), (xtr,))
                for hh in range(2):
                    tp, tpr = tp_ring.next()
                    for c in range(4):
                        cc = hh * 4 + c
                        tr(tp[:, c * 128:c * 128 + P], xt[0:P, cc * 128:(cc + 1) * 128], ident[0:P, 0:P],
                           (xtr, const_r), (tpr,))
                    src = AP(tp.tensor, tp.offset, [list(tp.ap[0]), [128, 4], [1, P]])
                    dst = ht[:, hh * 4:hh * 4 + 4, b * 128:b * 128 + P]
                    cp("act", dst, src, (tpr,), (hr,))

        def qk_norm_rope(P, qa, qr, g, cos_ap, sin_ap, lc, srcq, srck, mid=None):
            (qps, qpr), (kps, kpr) = srcq, srck
            s8, s8r = ss8_ring.next()
            fq, fqr = f_ring.next()
            act(fq[0:P, 0:384].rearrange("p (h d) -> p h d", d=64), qps, AF.Square, (qpr,), (fqr,))
            act(fq[0:P, 384:512].rearrange("p (h d) -> p h d", d=64), kps, AF.Square, (kpr,), (fqr,))
            red(s8[0:P, :], fq[0:P, :].rearrange("p (h d) -> p h d", d=64), (fqr,), (s8r,))
            if mid is not None:
                mid()
            act(s8[0:P, :], s8[0:P, :], AF.Ln, (s8r, const_r), (s8r,), scale=1.0 / 64, bias=epst[0:P, :])
            act(s8[0:P, :], s8[0:P, :], AF.Exp, (s8r,), (s8r,), scale=-0.5)
            s8q = AP(s8.tensor, s8.offset, [[s8.ap[0][0], P], [1, 6], [0, 64]])
            s8k = AP(s8.tensor, s8.offset + 6, [[s8.ap[0][0], P], [1, 2], [0, 64]])
            tt("dve", qa[0:P, 0:6, :], qps, s8q, ALU.mult, (qpr, s8r), (qr,))
            tt("dve", qa[0:P, 6:8, :], kps, s8k, ALU.mult, (kpr, s8r), (qr,))
            gq = AP(lc["gqk"], (g * 2 + 0) * 64, [[512, P], [0, 6], [1, 64]])
            gk = AP(lc["gqk"], (g * 2 + 1) * 64, [[512, P], [0, 2], [1, 64]])
            tt(EW2, qa[0:P, 0:6, :], qa[0:P, 0:6, :], gq, ALU.mult, (qr, lc["r"]), (qr,))
            tt(EW2, qa[0:P, 6:8, :], qa[0:P, 6:8, :], gk, ALU.mult, (qr, lc["r"]), (qr,))
            rt, rr = rope_ring.next()
            tA = rt[0:P, 0]
            tB = rt[0:P, 1]
            y16 = qa[0:P, :, 0:16].rearrange("p h (a d) -> p h a d", a=2)
            tA4 = tA.rearrange("p h (a d) -> p h a d", a=2)
            tB4 = tB.rearrange("p h (a d) -> p h a d", a=2)
            tt(EW2, tA4, y16, cos_ap, ALU.mult, (qr, const_r), (rr,))
            tt(EW2, tB4, y16, sin_ap, ALU.mult, (qr, const_r), (rr,))
            tt(EW2, qa[0:P, :, 0:8], tA[:, :, 0:8], tB[:, :, 8:16], ALU.subtract, (rr,), (qr,))
            tt(EW2, qa[0:P, :, 8:16], tA[:, :, 8:16], tB[:, :, 0:8], ALU.add, (rr,), (qr,))

        def tab_aps(P, order, blk):
            base = (order * 16 + blk) * 16
            c = AP(tabs, base, [[768, P], [0, 8], [0, 2], [1, 8]])
            s = AP(tabs, base + 8, [[768, P], [0, 8], [0, 2], [1, 8]])
            return c, s

        def colpat(order, b):
            if order == 0:
                return b * 128, [[1, 128]]
            if order == 1:
                return b, [[4, 128]]
            return 4 * b, [[1, 4], [16, 32]]


        def dense_tail(cx, l):
            P, NTK, nblk = cx["P"], cx["ntok"], cx["nblk"]
            c_hT, c_hT_r = cx["hT"]
            c_oT, c_oT_r = cx["oT"]
            c_mT, c_mT_r = cx["mT"]
            c_act, c_act_r = cx["act"]
            xbl = cx["x"]
            for m in range(8 if phases >= 5 else 0):
                parts = [(n * 128, [[512, 8], [1, 128]], 128, 0,
                          AP(w_gate.tensor, w_gate[l, n].offset + m * 128, [[D, 128], [128 * D, 8], [1, 128]]))
                         for n in range(4)]
                wbo = 4096
                for n in range(4):
                    base = w_branch[l, n]
                    if n < 2:
                        for hh in range(2):
                            parts.append((wbo + n * 384, [[128, 3], [1, 128]], 64, 64 * hh,
                                          AP(base.tensor, base.offset + hh * 192 * D + m * 128, [[D, 64], [64 * D, 3], [1, 128]])))
                    else:
                        parts.append((wbo + n * 384, [[128, 3], [1, 128]], 128, 0,
                                      AP(base.tensor, base.offset + m * 128, [[D, 128], [128 * D, 3], [1, 128]])))
                wt, wr = wload(parts)
                macc, maccr = f_ring.next()
                for n in range(4):
                    pg, pgr = ps_ring.next()
                    pp, ppr = ps_ring.next()
                    for kc in range(8):
                        mm(pg[:, 0:NTK], AP(wt, kc * 512 + n * 128, [[WSLOT, 128], [1, 128]]), c_hT[:, kc, :], kc == 0, kc == 7,
                           (c_hT_r, wr), (pgr,))
                    for c in range(3):
                        mm(pp[:, 0:NTK], AP(wt, wbo + n * 384 + c * 128, [[WSLOT, 128], [1, 128]]), c_oT[n][:, c, :], c == 0, c == 2,
                           (c_oT_r[n], wr), (ppr,))
                    sg, sgr = f_ring.next()
                    act(sg[:, 0:NTK], pg[:, 0:NTK], AF.Sigmoid, (pgr, cx["lc"]["r"]), (sgr,),
                        bias=cx["lc"]["lcol"][:, n * 8 + m:n * 8 + m + 1])
                    if n == 0:
                        tt("dve", macc[:, 0:NTK], sg[:, 0:NTK], pp[:, 0:NTK], ALU.mult, (sgr, ppr), (maccr,))
                    else:
                        tt("dve", sg[:, 0:NTK], sg[:, 0:NTK], pp[:, 0:NTK], ALU.mult, (sgr, ppr), (sgr,))
                        if n < 3:
                            tt(EW2, macc[:, 0:NTK], macc[:, 0:NTK], sg[:, 0:NTK], ALU.add, (maccr, sgr), (maccr,))
                        else:
                            tt(EW2, c_mT[:, m, :], macc[:, 0:NTK], sg[:, 0:NTK], ALU.add, (maccr, sgr), (c_mT_r,))
                if m % 2 == 1:
                    yield

            for ch in range(2 if phases >= 6 else 0):
                wt, wr = wload([(0, [[512, 8], [1, 512]], 128, 0, wsrc(w_o[l], 0, 8, ch * 512, 512))])
                for b in range(nblk):
                    px, pxr = ps_ring.next()
                    xa, xr = xbl[b]
                    for kc in range(8):
                        mm(px[0:P, :], c_mT[:, kc, b * 128:b * 128 + P], AP(wt, kc * 512, [[WSLOT, 128], [1, 512]]),
                           kc == 0, kc == 7, (c_mT_r, wr), (pxr,))
                    tt("dve", xa[:, ch * 512:(ch + 1) * 512], xa[:, ch * 512:(ch + 1) * 512], px[0:P, :], ALU.add,
                       (xr, pxr), (xr,))

            yield
            if cx.get("prefetch") is not None:
                cx["prefetch"]()
            if phases >= 7:
                do_norm(P, nblk, xbl, ln2[l, :], (c_hT, c_hT_r))
            yield

            for dh in range(2 if phases >= 8 else 0):
                f0 = dh * 11
                jj = 0
                while jj < 11:
                    nj = 2 if jj + 2 <= 11 else 1
                    c0 = (f0 + jj) * 128
                    parts = [(0, [[2 * nj * 128, 8], [1, nj * 128]], 128, 0, wsrc(w_fi[l], 0, 8, c0, nj * 128)),
                             (nj * 128, [[2 * nj * 128, 8], [1, nj * 128]], 128, 0, wsrc(w_fi[l], 0, 8, DFF + c0, nj * 128))]
                    wt, wr = wload(parts)
                    for j in range(nj):
                        pg, pgr = ps_ring.next()
                        pu, pur = ps_ring.next()
                        for kc in range(8):
                            mm(pg[:, 0:NTK], AP(wt, kc * 2 * nj * 128 + j * 128, [[WSLOT, 128], [1, 128]]), c_hT[:, kc, :],
                               kc == 0, kc == 7, (c_hT_r, wr), (pgr,))
                        for kc in range(8):
                            mm(pu[:, 0:NTK], AP(wt, kc * 2 * nj * 128 + nj * 128 + j * 128, [[WSLOT, 128], [1, 128]]), c_hT[:, kc, :],
                               kc == 0, kc == 7, (c_hT_r, wr), (pur,))
                        sg, sgr = f_ring.next()
                        act(sg[:, 0:NTK], pg[:, 0:NTK], AF.Silu, (pgr,), (sgr,))
                        tt("dve", c_act[:, (jj + j) * NTK:(jj + j + 1) * NTK], sg[:, 0:NTK], pu[:, 0:NTK], ALU.mult, (sgr, pur), (c_act_r,))
                    jj += nj
                    if jj % 4 == 0:
                        yield
                yield
                if dh == 1 and cx.get("done") is not None:
                    wts = [wload([(0, [[512, 11], [1, 512]], 128, 0, wsrc(w_fo[l], f0 * 128, 11, ch * 512, 512))])
                           for ch in range(2)]
                    for b in range(nblk):
                        xa, xr = xbl[b]
                        for ch in range(2):
                            wt, wr = wts[ch]
                            px, pxr = ps_ring.next()
                            for kc in range(11):
                                mm(px[0:P, :], c_act[:, kc * NTK + b * 128:kc * NTK + b * 128 + P],
                                   AP(wt, kc * 512, [[WSLOT, 128], [1, 512]]), kc == 0, kc == 10, (c_act_r, wr), (pxr,))
                            tt("dve", xa[:, ch * 512:(ch + 1) * 512], xa[:, ch * 512:(ch + 1) * 512], px[0:P, :], ALU.add,
                               (xr, pxr), (xr,))
                        cx["done"](b)
                    continue
                for ch in range(2):
                    wt, wr = wload([(0, [[512, 11], [1, 512]], 128, 0, wsrc(w_fo[l], f0 * 128, 11, ch * 512, 512))])
                    for b in range(nblk):
                        px, pxr = ps_ring.next()
                        xa, xr = xbl[b]
                        for kc in range(11):
                            mm(px[0:P, :], c_act[:, kc * NTK + b * 128:kc * NTK + b * 128 + P],
                               AP(wt, kc * 512, [[WSLOT, 128], [1, 512]]), kc == 0, kc == 10, (c_act_r, wr), (pxr,))
                        tt("dve", xa[:, ch * 512:(ch + 1) * 512], xa[:, ch * 512:(ch + 1) * 512], px[0:P, :], ALU.add,
                           (xr, pxr), (xr,))

        PCX = {"lc": LC_P, "P": 128, "ntok": T, "nblk": 4, "hT": (hT, hT_r), "oT": (oT, oT_r), "mT": (mT, mT_r),
               "act": (actT, bacc_r), "x": [(xb[b][:], xb_r[b]) for b in range(4)]}


        SCX = {"lc": LC_S, "P": NS, "ntok": NS, "nblk": 1, "hT": (hTs, hTs_r), "oT": (oTs, oTs_r), "mT": (mTs, mTs_r),
               "act": (actTs[:], actTs_r), "x": [(xs_t[0:NS, :], xs_r_)]}

        def sample_tile(l):
            P = NS
            load_layer_consts(l, LC_S)
            xs, xs_r = SCX["x"][0]
            if l == 0:
                dma("sp", xs, x_s[:, :], xs_ds, (), (xs_r,))
            yield
            do_norm(P, 1, [(xs, xs_r)], ln1[l, :], (hTs, hTs_r))
            yield
            cos_ap = AP(tabs_s, 0, [[16, P], [0, 8], [0, 2], [1, 8]])
            sin_ap = AP(tabs_s, 8, [[16, P], [0, 8], [0, 2], [1, 8]])
            fpart = 5 * T
            for g in range(4):
                R = CROWS[g]
                dil = DILS[g]
                wt, wr = wload([(0, [[640, 8], [1, 640]], 128, 0, wsrc(w_in[l], 0, 8, 640 * g, 640))])
                pq, pqr = ps_ring.next()
                pkv, pkvr = ps_ring.next()
                for kc in range(8):
                    lh = hTs[:, kc, :]
                    mm(pq[0:P, 0:384], lh, AP(wt, kc * 640, [[WSLOT, 128], [1, 384]]), kc == 0, kc == 7, (hTs_r, wr), (pqr,))
                    mm(pkv[0:P, 0:256], lh, AP(wt, kc * 640 + 384, [[WSLOT, 128], [1, 256]]), kc == 0, kc == 7,
                       (hTs_r, wr), (pkvr,))
                qa, qr, qds = qk_ring.next()
                va, vr, vds = vst_ring.next()
                cp("act", va[0:P, :], pkv[0:P, 128:256], (pkvr,), (vr,))
                qk_norm_rope(P, qa, qr, g, cos_ap, sin_ap, LC_S,
                             (pq[0:P, 0:384].rearrange("p (h d) -> p h d", d=64), pqr),
                             (pkv[0:P, 0:128].rearrange("p (h d) -> p h d", d=64), pkvr))
                base = new_s[g][l]
                dK = AP(base.tensor, base.offset + (R - 1) * 128, [[2 * R * 128, P], [1, 128]])
                dV = AP(base.tensor, base.offset + R * 128 + (R - 1) * 128, [[2 * R * 128, P], [1, 128]])
                dma("sp", dK, qa[0:P, 6:8, :].rearrange("p h d -> p (h d)"), qds, (qr,), ())
                dma("sp", dV, va[0:P, :], vds, (vr,), ())
                dma("sp", q_scr[g, :, :], qa[0:P, 0:6, :].rearrange("p h d -> p (h d)"), qds, (qr,), (qscr_r[g],))
                pso, psor = ps_ring.next()
                psd, psdr = ps_ring.next()
                cbase = caches[g][l]
                for cb in range(8):
                    b0 = cb * 2
                    srcK = AP(cbase.tensor, cbase.offset + b0 * 2 * R * 128, [[dil * 128, 128], [2 * R * 128, 2], [1, 128]])
                    srcV = AP(cbase.tensor, cbase.offset + b0 * 2 * R * 128 + R * 128,
                              [[dil * 128, 128], [2 * R * 128, 2], [1, 128]])
                    dma("sp", Kc4.rearrange("p (b f) -> p b f", b=2), srcK, sscr_ds, (), (sscr_r,))
                    dma("sp", Vc4.rearrange("p (b f) -> p b f", b=2), srcV, sscr_ds, (), (sscr_r,))
                    dma("sp", sscr[:, 512:1280], AP(q_scr.tensor, q_scr[g, b0].offset, [[0, 128], [1, 768]]), sscr_ds,
                        (qscr_r[g],), (sscr_r,))
                    kin = AP(sscr, 0, [[2048, 128], [64, 4], [0, 3], [1, 64]])
                    qin = AP(sscr, 512, [[2048, 128], [192, 4], [64, 3], [1, 64]])
                    pout = AP(sscr, 1280, [[2048, 128], [192, 4], [64, 3], [1, 64]])
                    tt("dve", pout, kin, qin, ALU.mult, (sscr_r,), (sscr_r,))
                    s4, s4r = s4_ring.next()
                    red(s4[:, 0:12], AP(sscr, 1280, [[2048, 128], [64, 12], [1, 64]]), (sscr_r,), (s4r,))
                    act(s4[:, 0:12], s4[:, 0:12], AF.Exp, (s4r,), (s4r,), scale=0.125)
                    for bb in range(2):
                        for kv in range(2):
                            col = ((b0 + bb) * 2 + kv) * 3
                            mm(pso[0:64, col:col + 3], Vc4[:, bb * 128 + kv * 64:bb * 128 + kv * 64 + 64],
                               s4[:, (bb * 2 + kv) * 3:(bb * 2 + kv) * 3 + 3], True, True, (sscr_r, s4r), (psor,))
                    mm(psd[0:64, b0 * 6:(b0 + 2) * 6], onesf[:, 0:64], s4[:, 0:12], True, True, (const_r, s4r), (psdr,))

                if g <= 1:
                    cp("dve", nacc[0:64, 0, :], pso[0:64, 0:96], (psor,), (nacc_r,))
                    cp("dve", nacc[0:64, 1, :], psd[0:64, 0:96], (psdr,), (nacc_r,))
                else:
                    tt("dve", nacc[0:64, 0, :], nacc[0:64, 0, :], pso[0:64, 0:96], ALU.add, (psor, nacc_r), (nacc_r,))
                    tt("dve", nacc[0:64, 1, :], nacc[0:64, 1, :], psd[0:64, 0:96], ALU.add, (psdr, nacc_r), (nacc_r,))
                pst, pstr = ps_ring.next()
                for h in range(8):
                    tr(pst[0:64, h * 16:h * 16 + P], qa[0:P, h, :], identf[0:P, 0:P], (qr, const_r), (pstr,))
                for kv in range(2):
                    tr(pst[0:64, (8 + kv) * 16:(8 + kv) * 16 + P], va[0:P, kv * 64:(kv + 1) * 64], identf[0:P, 0:P],
                       (vr, const_r), (pstr,))
                cp("act", selfT[0:64, :], pst[0:64, 0:160], (pstr,), (self_r,))
                f1, fr1 = f_ring.next()
                o1 = AP(f1.tensor, f1.offset, [[fpart, 64], [6, 16], [3, 2], [1, 3]])
                tt("dve", o1, AP(selfT, 0, [[160, 64], [1, 16], [48, 2], [16, 3]]),
                   AP(selfT, 96, [[160, 64], [1, 16], [16, 2], [0, 3]]), ALU.mult, (self_r,), (fr1,))
                pss, pssr = ps_ring.next()
                mm(pss[0:64, 0:96], onesf[0:64, 0:64], f1[0:64, 0:96], True, True, (const_r, fr1), (pssr,))
                f2, fr2 = f_ring.next()
                act(f2[0:64, 0:96], pss[0:64, 0:96], AF.Exp, (pssr,), (fr2,), scale=0.125)
                f3, fr3 = f_ring.next()
                o3 = AP(f3.tensor, f3.offset, [[fpart, 64], [6, 16], [3, 2], [1, 3]])
                i2 = AP(f2.tensor, f2.offset, [[fpart, 64], [6, 16], [3, 2], [1, 3]])
                tt("dve", o3, i2, AP(selfT, 128, [[160, 64], [1, 16], [16, 2], [0, 3]]), ALU.mult, (fr2, self_r), (fr3,))
                tt("dve", nacc[0:64, 0, :], nacc[0:64, 0, :], f3[0:64, 0:96], ALU.add, (nacc_r, fr3), (nacc_r,))
                tt("dve", nacc[0:64, 1, :], nacc[0:64, 1, :], f2[0:64, 0:96], ALU.add, (nacc_r, fr2), (nacc_r,))
                if g == 0:
                    dn = AP(nacc, 96, [[192, 64], [6, 16], [1, 6]])
                    tt("dve", dn, dn, AP(es_t_s, 0, [[6, 64], [0, 16], [1, 6]]), ALU.add, (nacc_r, LC_S["r"]), (nacc_r,))
                if g == 0 or g == 3:
                    n = 0 if g == 0 else 1
                    f4, fr4 = f_ring.next()
                    recip(f4[0:64, 0:96], nacc[0:64, 1, :], (nacc_r,), (fr4,))
                    f5, fr5 = f_ring.next()
                    tt("dve", f5[0:64, 0:96], nacc[0:64, 0, :], f4[0:64, 0:96], ALU.mult, (nacc_r, fr4), (fr5,))
                    for kv in range(2):
                        cp("dve", AP(oTs[n], 64 * kv * 48, [[48, 64], [1, 16], [16, 3]]),
                           AP(f5.tensor, f5.offset + kv * 3, [[fpart, 64], [6, 16], [1, 3]]), (fr5,), (oTs_r[n],))
                yield

            yield
            st_t = sscr[0:P, 1152:1920]
            cc_bc = sscr[0:P, 0:1152]
            zcn = zv_t[0:P, 384:768]
            vdt = zv_t[0:P, 0:384]
            dma("sp", st_t, state_c[l].rearrange("b j f -> b (j f)"), sscr_ds, (), (sscr_r,))
            dma("sp", cc_bc, AP(conv_c.tensor, conv_c[l].offset, [[0, P], [1, 1152]]), sscr_ds, (), (sscr_r,))
            pcs = []
            for j in range(3):
                wt, wr = wload([(0, [[384, 8], [1, 384]], 128, 0, wsrc(w_in[l], 0, 8, 2560 + 384 * j, 384))])
                pc, pcr = ps_ring.next()
                for kc in range(8):
                    mm(pc[0:P, 0:384], hTs[:, kc, :], AP(wt, kc * 384, [[WSLOT, 128], [1, 384]]), kc == 0, kc == 7,
                       (hTs_r, wr), (pcr,))
                pcs.append((pc, pcr))
            f1, fr1 = f_ring.next()
            cp("act", f1[0:P, 0:384], pcs[1][0][0:P, 0:384], (pcs[1][1],), (fr1,))
            tt("dve", zcn, f1[0:P, 0:384], pcs[2][0][0:P, 0:384], ALU.mult, (fr1, pcs[2][1]), (zv_r,))
            f2, fr2 = f_ring.next()
            f3, fr3 = f_ring.next()
            tt("dve", f2[0:P, 0:384], st_t[:, 0:384], cc_bc[:, 0:384], ALU.mult, (sscr_r,), (fr2,))
            tt("dve", f3[0:P, 0:384], st_t[:, 384:768], cc_bc[:, 384:768], ALU.mult, (sscr_r,), (fr3,))
            tt("dve", f2[0:P, 0:384], f2[0:P, 0:384], f3[0:P, 0:384], ALU.add, (fr2, fr3), (fr2,))
            tt("dve", f3[0:P, 0:384], zcn, cc_bc[:, 768:1152], ALU.mult, (zv_r, sscr_r), (fr3,))
            tt("dve", f2[0:P, 0:384], f2[0:P, 0:384], f3[0:P, 0:384], ALU.add, (fr2, fr3), (fr2,))
            ob_, obr = obf_ring.next()
            tt("dve", ob_[0:P, :], pcs[0][0][0:P, 0:384], f2[0:P, 0:384], ALU.mult, (pcs[0][1], fr2), (obr,))
            tp, tpr = tp_ring.next()
            for c in range(3):
                tr(tp[:, c * 128:c * 128 + P], ob_[0:P, c * 128:(c + 1) * 128], ident[0:P, 0:P], (obr, const_r), (tpr,))
            cp("act", oTs[2][:, :, :], AP(tp.tensor, tp.offset, [list(tp.ap[0]), [128, 3], [1, P]]), (tpr,), (oTs_r[2],))
            dma("sp", new_c_s[l, :, 0, :], st_t[:, 384:768], sscr_ds, (sscr_r,), ())
            dma("sp", new_c_s[l, :, 1, :], zcn, zv_ds, (zv_r,), ())

            yield
            dma("sp", sm6[0:P, 0, :], AP(ws_d.tensor, ws_d[l].offset, [[0, P], [16384, 6]]), sm_ds, (), (sm_r,), nonc=True)
            dma("sp", sm6[0:P, 1, :], AP(bs_d.tensor, bs_d[l].offset, [[0, P], [128, 6]]), sm_ds, (), (sm_r,), nonc=True)
            wtu, wru = wload([(0, [[384, 8], [1, 384]], 128, 0, wsrc(w_in[l], 0, 8, 3712, 384))])
            wtv, wrv = wload([(0, [[384, 8], [1, 384]], 128, 0, wsrc(w_in[l], 0, 8, 4096, 384))])
            pu, pur = ps_ring.next()
            pv, pvr = ps_ring.next()
            for kc in range(8):
                mm(pu[0:P, 0:384], hTs[:, kc, :], AP(wtu, kc * 384, [[WSLOT, 128], [1, 384]]), kc == 0, kc == 7, (hTs_r, wru), (pur,))
            for kc in range(8):
                mm(pv[0:P, 0:384], hTs[:, kc, :], AP(wtv, kc * 384, [[WSLOT, 128], [1, 384]]), kc == 0, kc == 7, (hTs_r, wrv), (pvr,))
            cp("act", vdt, pv[0:P, 0:384], (pvr,), (zv_r,))
            dma("sp", new_d_s[l, :, :], vdt, zv_ds, (zv_r,), ())
            f1, fr1 = f_ring.next()
            f1v = f1[0:P, 0:384].rearrange("p (g e) -> p g e", g=6)
            tt("dve", f1v, vdt.rearrange("p (g e) -> p g e", g=6), AP(sm6, 0, [[12, P], [1, 6], [0, 64]]), ALU.mult,
               (zv_r, sm_r), (fr1,))
            tt("dve", f1v, f1v, AP(sm6, 6, [[12, P], [1, 6], [0, 64]]), ALU.add, (fr1, sm_r), (fr1,))
            ob_, obr = obf_ring.next()
            tt("dve", ob_[0:P, :], pu[0:P, 0:384], f1[0:P, 0:384], ALU.mult, (pur, fr1), (obr,))
            tp, tpr = tp_ring.next()
            for c in range(3):
                tr(tp[:, c * 128:c * 128 + P], ob_[0:P, c * 128:(c + 1) * 128], ident[0:P, 0:P], (obr, const_r), (tpr,))
            cp("act", oTs[3][:, :, :], AP(tp.tensor, tp.offset, [list(tp.ap[0]), [128, 3], [1, P]]), (tpr,), (oTs_r[3],))

            yield
            yield from dense_tail(SCX, l)
            if l == depth - 1:
                dma("sp", y_s[:, :], xs, xs_ds, (xs_r,), ())

        xloaded = set()
        mT_f32 = mT[:].rearrange("p a b -> p (a b)").bitcast(F32)
        xpre = [mT_f32[:, b * D:(b + 1) * D] for b in range(2)]
        xpre_ds = [S.new_dsem(f"xpre{b}") for b in range(2)]
        xpref = set()

        def x_prefetch(s, l, t):
            nxt = next_tile(s, l, t)
            if nxt is None:
                return
            s2, l2, t2 = nxt
            src = x_p if l2 == 0 else y_p
            for b in range(2):
                blk = t2 * 4 + b
                rd = (dram_y_r[(s2, blk)],) if l2 > 0 else ()
                dma("sp", xpre[b], src[s2, blk * 128:(blk + 1) * 128, :], xpre_ds[b], rd, (mT_r,))
            xpref.add(nxt)

        def x_load(s, l, t, b):
            if (s, l, t, b) in xloaded:
                return
            xloaded.add((s, l, t, b))
            src = x_p if l == 0 else y_p
            blk = t * 4 + b
            rd = ()
            if l > 0:
                rd = (dram_y_r[(s, blk)],)
            dma("sp", xb[b][:], src[s, blk * 128:(blk + 1) * 128, :], xb_ds[b], rd, (xb_r[b],))

        def next_tile(s, l, t):
            if t + 1 < ntiles:
                return (s, l, t + 1)
            if l + 1 < depth:
                return (s, l + 1, 0)
            if s + 1 < nseq:
                return (s + 1, 0, 0)
            return None

        def prompt_tile(s, l, t):
            src = x_p if l == 0 else y_p
            if t == 0:
                load_layer_consts(l, LC_P)
            for b in range(4):
                x_load(s, l, t, b)

            def _done(b, s=s, l=l, t=t):
                blk = t * 4 + b
                r = dram_y_r.setdefault((s, blk), Res(f"y{s}_{blk}"))
                dma("sp", y_p[s, blk * 128:(blk + 1) * 128, :], xb[b][:], xb_ds[b], (xb_r[b],), (r,))
                nxt = next_tile(s, l, t)
                if nxt is not None:
                    x_load(nxt[0], nxt[1], nxt[2], b)
            PCX["done"] = _done if phases >= 8 else None
            xsrc1 = [(xb[b][:], xb_r[b]) for b in range(4)]
            if (s, l, t) in xpref:
                for b in range(2):
                    xsrc1[b] = (xpre[b], mT_r)
            do_norm(128, 4, xsrc1, ln1[l, :], (hT, hT_r))
            PCX["prefetch"] = (lambda s=s, l=l, t=t: x_prefetch(s, l, t)) if phases >= 8 else None

            gw = {}
            stt_ = {}

            def stageA(g, b):
                order = (0, 0, 1, 1)[g]
                nkt = 8 if g < 3 else 16
                if b == 0:
                    gw[g] = wload([(0, [[640, 8], [1, 640]], 128, 0, wsrc(w_in[l], 0, 8, 640 * g, 640))])
                wt, wr = gw[g]
                win_rows = CROWS[g]
                bid = t * 4 + b
                slot = bid % nkt
                c0, cdims = colpat(order, b)
                pq, pqr = ps_ring.next()
                pkv, pkvr = ps_ring.next()
                for kc in range(8):
                    lh = AP(hT, kc * T + c0, [[8 * T, 128]] + cdims)
                    mm(pq[:, 0:384], lh, AP(wt, kc * 640, [[WSLOT, 128], [1, 384]]), kc == 0, kc == 7,
                       (hT_r, wr), (pqr,))
                    mm(pkv[:, 0:256], lh, AP(wt, kc * 640 + 384, [[WSLOT, 128], [1, 256]]), kc == 0, kc == 7,
                       (hT_r, wr), (pkvr,))
                qa, qr, qds = qk_ring.next()
                vdst = AP(Vt[g], slot * 192, [[nkt * 192, 128], [128, 2], [1, 64]])
                first_row = SEQ - win_rows
                if order == 0:
                    need_out = bid * 128 >= first_row
                else:
                    need_out = (t * T + T) > first_row
                if need_out:
                    va, vr, vds = vst_ring.next()

                def _mid():
                    cp("act", vdst, pkv[:, 128:256].rearrange("p (k d) -> p k d", d=64), (pkvr,), (Vt_r[g][slot],))
                    if need_out:
                        cp("act", va[:], pkv[:, 128:256], (pkvr,), (vr,))
                cos_ap, sin_ap = tab_aps(128, order, bid)
                qk_norm_rope(128, qa, qr, g, cos_ap, sin_ap, LC_P,
                             (pq[:, 0:384].rearrange("p (h d) -> p h d", d=64), pqr),
                             (pkv[:, 0:128].rearrange("p (h d) -> p h d", d=64), pkvr), mid=_mid)
                if need_out:
                    def rows_dst(kvsel):
                        base = new_p[g][l, s, kvsel]
                        if order == 0:
                            r0 = bid * 128 - first_row
                            return AP(base.tensor, base.offset + r0 * 128, [[128, 128], [1, 128]])
                        r0 = t * T + b - first_row
                        return AP(base.tensor, base.offset + r0 * 128, [[512, 128], [1, 128]])
                    dma("sp", rows_dst(0), qa[:, 6:8, :].rearrange("p h d -> p (h d)"), qds, (qr,), ())
                    dma("sp", rows_dst(1), va[:, :], vds, (vr,), ())
                qb, kb, qbr = qbf_ring.next()
                cp("dve", qb.rearrange("p g (k d) -> p g k d", k=2),
                   qa[:, 0:6, :].rearrange("p (k g) d -> p g k d", k=2), (qr,), (qbr,))
                cp("dve", kb, qa[:, 6:8, :].rearrange("p h d -> p (h d)"), (qr,), (qbr,))
                stt_[(g, b)] = (qb, kb, qbr)

            def stageB(g, b):
                nkt = 8 if g < 3 else 16
                slot = (t * 4 + b) % nkt
                qb, kb, qbr = stt_[(g, b)]
                tp, tpr = tp_ring.next()
                for gg in range(3):
                    tr(tp[:, gg * 128:(gg + 1) * 128], qb[:, gg, :], ident[:], (qbr, const_r), (tpr,))
                tr(tp[:, 384:512], kb, ident[:], (qbr, const_r), (tpr,))
                cp("act", QT[:, b, :], tp[:, 0:384], (tpr,), (QT_r[b],))
                cp("act", KT[g][:, slot * 128:(slot + 1) * 128], tp[:, 384:512], (tpr,), (KT_r[g][slot],))

            cst_ = {}

            def c_keyblocks(g, b):
                order = (0, 0, 1, 1)[g]
                nkt = 8 if g < 3 else 16
                bid = t * 4 + b
                slot = bid % nkt
                if g < 3:
                    prev = bid - (1 if order == 0 else 4)
                    return ([(prev % nkt, 0)] if prev >= 0 else []) + [(slot, 1)]
                return [((tt_ * 4 + b), 2) for tt_ in range(t)] + [(slot, 3)]

            def stageC1(g, b, kvs=(0, 1)):
                kbs = c_keyblocks(g, b)
                for kv in kvs:
                    pss = []
                    for ki, (ks, mi) in enumerate(kbs):
                        pS, pSr = ps_ring.next()
                        mm(pS[:, 0:384], KT[g][64 * kv:64 * kv + 64, ks * 128:(ks + 1) * 128],
                           QT[64 * kv:64 * kv + 64, b, :], True, True, (KT_r[g][ks], QT_r[b]), (pSr,))
                        pss.append((pS, pSr, ks, mi))
                    pts = []
                    for (pS, pSr, ks, mi) in pss:
                        pt_, ptr = pT_ring.next()
                        act(pt_, pS[:, 0:384], AF.Exp, (pSr,), (ptr,), scale=0.125)
                        pts.append((pt_, ptr, ks, mi))
                    for (pt_, ptr, ks, mi) in pts:
                        tt(EW2, pt_, pt_, masks[:, mi, :], ALU.mult, (ptr, const_r), (ptr,))
                    cst_[(g, b, kv)] = pts

            def stageC2(g, b, kvs=(0, 1)):
                order = (0, 0, 1, 1)[g]
                pos = []
                for kv in kvs:
                    pts = cst_.pop((g, b, kv))
                    po, por = ps_ring.next()
                    nkb = len(pts)
                    for ki, (pt_, ptr, ks, mi) in enumerate(pts):
                        last = (ki == nkb - 1) and g != 0
                        mm(po[:, 0:384], Vt[g][:, ks, 64 * kv:64 * kv + 128], pt_, ki == 0, last,
                           (Vt_r[g][ks], ptr), (por,))
                    if g == 0:
                        mm(po[:, 0:384], sel[0:1, kv, :], es_rows[0:1, kv].rearrange("o g q -> o (g q)"),
                           False, True, (lay_r, const_r), (por,))
                    pos.append((kv, po, por))
                oc0, ocd = colpat(order, b)
                if g == 0:
                    fs = []
                    for (kv, po, por) in pos:
                        nlo, dlo = (0, 64) if kv == 0 else (64, 0)
                        f1, fr1 = f_ring.next()
                        recip(f1[nlo:nlo + 64, 0:384], po[dlo:dlo + 64, 0:384], (por,), (fr1,))
                        fs.append((f1, fr1))
                    for (kv, po, por), (f1, fr1) in zip(pos, fs):
                        nlo, dlo = (0, 64) if kv == 0 else (64, 0)
                        odst = AP(oT[0], nlo * 3 * T + oc0, [[3 * T, 64], [T, 3]] + ocd)
                        tt("dve", odst, po[nlo:nlo + 64, 0:384].rearrange("p (g q) -> p g q", g=3),
                           f1[nlo:nlo + 64, 0:384].rearrange("p (g q) -> p g q", g=3), ALU.mult,
                           (por, fr1), (oT_r[0],))
                else:
                    for (kv, po, por) in pos:
                        adst = AP(bacc, kv * 3 * T + oc0, [[6 * T, 128], [T, 3]] + ocd)
                        pa_ = po[:]
                        psrc = AP(pa_.tensor, pa_.offset, [list(pa_.ap[0]), [128, 3]] + [[1, 128]])
                        if g == 1:
                            cp("act", adst, psrc, (por,), (bacc_r,))
                        else:
                            tt("dve", adst, adst, psrc, ALU.add, (por, bacc_r), (bacc_r,))
                if g == 3 and b == 3 and kvs[-1] == 1:
                    for kv in range(2):
                        nlo, dlo = (0, 64) if kv == 0 else (64, 0)
                        for gg in range(3):
                            f1, fr1 = f_ring.next()
                            off = (kv * 3 + gg) * T
                            recip(f1[nlo:nlo + 64, :], bacc[dlo:dlo + 64, off:off + T], (bacc_r,), (fr1,))
                            tt("dve", oT[1][nlo:nlo + 64, gg, :], bacc[nlo:nlo + 64, off:off + T], f1[nlo:nlo + 64, :],
                               ALU.mult, (bacc_r, fr1), (oT_r[1],))

            def big(g, b):
                return len(c_keyblocks(g, b)) > 2

            items = [(g, b) for g in range(4 if phases >= 2 else 0) for b in range(4)]
            ni = len(items)
            if tl_count[0] >= COPY_START_TL or nseq * depth * ntiles <= COPY_START_TL:
                issue_copies(3)
            tl_count[0] += 1
            hook()
            for i in range(ni + 4):
                if i < ni:
                    stageA(*items[i])
                if 0 <= i - 2 < ni:
                    stageB(*items[i - 2])
                if 0 <= i - 4 < ni and not big(*items[i - 4]):
                    stageC2(*items[i - 4])
                if 0 <= i - 3 < ni:
                    it = items[i - 3]
                    if big(*it):
                        for kv in range(2):
                            stageC1(*it, kvs=(kv,))
                            stageC2(*it, kvs=(kv,))
                    else:
                        stageC1(*it)
                if i % 4 == 3:
                    hook()

            if t == 0:
                for c in range(3):
                    memset("dve", zc[:, c, 0:2], 0.0, (zc_r[c],))
            for c in range(3 if phases >= 3 else 0):
                parts = []
                for j in range(3):
                    parts.append((j * 128, [[384, 8], [1, 128]], 128, 0,
                                  wsrc(w_in[l], 0, 8, 2560 + 384 * j + 128 * c, 128)))
                wt, wr = wload(parts)
                pz = [ps_ring.next() for _ in range(3)]
                for j in range(3):
                    for kc in range(8):
                        mm(pz[j][0][:, :], AP(wt, kc * 384 + j * 128, [[WSLOT, 128], [1, 128]]), hT[:, kc, :],
                           kc == 0, kc == 7, (hT_r, wr), (pz[j][1],))
                f1, fr1 = f_ring.next()
                cp("act", f1[:, :], pz[1][0][:, :], (pz[1][1],), (fr1,))
                tt("dve", zc[:, c, 2:T + 2], f1[:, :], pz[2][0][:, :], ALU.mult, (fr1, pz[2][1]), (zc_r[c],))
                f2, fr2 = f_ring.next()
                cb = 32 + c
                act(f2[:, :], zc[:, c, 2:T + 2], AF.Copy, (zc_r[c], lay_r), (fr2,),
                    scale=lcol[:, 32 + 2 * 3 + c:32 + 2 * 3 + c + 1])
                stt("dve", f2[:, :], zc[:, c, 1:T + 1], lcol[:, 32 + 3 + c:32 + 3 + c + 1], f2[:, :], ALU.mult, ALU.add,
                    (zc_r[c], lay_r, fr2), (fr2,))
                stt("dve", f2[:, :], zc[:, c, 0:T], lcol[:, 32 + c:32 + c + 1], f2[:, :], ALU.mult, ALU.add,
                    (zc_r[c], lay_r, fr2), (fr2,))
                tt("dve", oT[2][:, c, :], pz[0][0][:, :], f2[:, :], ALU.mult, (pz[0][1], fr2), (oT_r[2],))
                if t == NT - 1:
                    dstc = AP(new_c_p.tensor, new_c_p[l, s].offset + c * 128, [[1, 128], [384, 2]])
                    dma("sp", dstc, zc[:, c, T:T + 2], zc_ds[c], (zc_r[c],), (), nonc=True)
                else:
                    cp("act", zc[:, c, 0:2], zc[:, c, T:T + 2], (zc_r[c],), (zc_r[c],))

            if phases >= 4:
                wtu, wru = wload([(0, [[384, 8], [1, 384]], 128, 0, wsrc(w_in[l], 0, 8, 3712, 384))])
                wtv, wrv = wload([(0, [[384, 8], [1, 384]], 128, 0, wsrc(w_in[l], 0, 8, 4096, 384))])
            dst_ = {}

            def dA(b):
                pu, pur = ps_ring.next()
                pv, pvr = ps_ring.next()
                for kc in range(8):
                    lh = hT[:, kc, b * 128:(b + 1) * 128]
                    mm(pv[:, 0:384], lh, AP(wtv, kc * 384, [[WSLOT, 128], [1, 384]]), kc == 0, kc == 7, (hT_r, wrv), (pvr,))
                for kc in range(8):
                    lh = hT[:, kc, b * 128:(b + 1) * 128]
                    mm(pu[:, 0:384], lh, AP(wtu, kc * 384, [[WSLOT, 128], [1, 384]]), kc == 0, kc == 7, (hT_r, wru), (pur,))
                vb_, vbr = vbf_ring.next()
                cp("act", vb_, pv[:, 0:384], (pvr,), (vbr,))
                dst_[b] = [pu, pur, vb_, vbr]

            def dB(b):
                pu, pur, vb_, vbr = dst_[b]
                psv, psvr = ps_ring.next()
                for gg in range(6):
                    mm(psv[:, gg * 64:(gg + 1) * 64], wmT[:, gg, :], vb_[:, gg * 64:(gg + 1) * 64], True, True,
                       (lay_r, vbr), (psvr,))
                f1, fr1 = f_ring.next()
                bsb = AP(lcol, 41, [[48, 128], [1, 6], [0, 64]])
                tt("dve", f1[:, 0:384].rearrange("p (g e) -> p g e", g=6), psv[:, 0:384].rearrange("p (g e) -> p g e", g=6),
                   bsb, ALU.add, (psvr, lay_r), (fr1,))
                ob_, obr = obf_ring.next()
                tt("dve", ob_, pu[:, 0:384], f1[:, 0:384], ALU.mult, (pur, fr1), (obr,))
                dst_[b] = [ob_, obr]

            def dC(b):
                ob_, obr = dst_[b]
                tp, tpr = tp_ring.next()
                for c in range(3):
                    tr(tp[:, c * 128:(c + 1) * 128], ob_[:, c * 128:(c + 1) * 128], ident[:], (obr, const_r), (tpr,))
                cp("act", oT[3][:, :, b * 128:(b + 1) * 128], tp[:, 0:384].rearrange("p (c q) -> p c q", c=3),
                   (tpr,), (oT_r[3],))

            nb_ = 4 if phases >= 4 else 0
            for i in range(nb_ + 2 if nb_ else 0):
                if i < nb_:
                    dA(i)
                if 0 <= i - 1 < nb_:
                    dB(i - 1)
                if 0 <= i - 2 < nb_:
                    dC(i - 2)

            for _ in dense_tail(PCX, l):
                hook()

            if PCX["done"] is None:
                for b in range(4):
                    _done(b)

        copy_jobs = []
        if do_copy and ns > 0:
            cpd = S.new_dsem("cachecopy")
            for g in (3, 2, 1, 0):
                R = CROWS[g]
                for l in range(depth):
                    for b0 in range(0, ns, 4):
                        copy_jobs.append((new_s[g][l, b0:b0 + 4, :, 0:R - 1, :], caches[g][l, b0:b0 + 4, :, 1:R, :]))

        def issue_copies(n):
            for _ in range(n):
                if copy_jobs:
                    o_, i_ = copy_jobs.pop(0)
                    dma("act", o_, i_, cpd, (), ())

        tl_count = [0]

        def _sample_all():
            for l_ in range(depth):
                yield from sample_tile(l_)
        sgen = [_sample_all() if ns > 0 else None]

        def hook():
            if sgen[0] is not None:
                try:
                    next(sgen[0])
                except StopIteration:
                    sgen[0] = None
        if nseq == 0 or ntiles == 0:
            while sgen[0] is not None:
                hook()
        for s in range(nseq):
            for l in range(depth):
                for t in range(ntiles):
                    prompt_tile(s, l, t)

        while sgen[0] is not None:
            hook()
        issue_copies(len(copy_jobs))
        hoist_wloads(NWS - 1)
        with nc.Block() as block:
            S.emit(block)
    return nc


_W_NAMES = ["ln1", "w_in", "q_norm_a", "k_norm_a", "sink_a", "q_norm_b", "k_norm_b", "conv_c", "ws_d", "bs_d",
            "w_gate", "b_gate", "w_branch", "w_o", "ln2", "w_ffn_in", "w_ffn_out"]


def make_in_maps(inputs, ncores=8, nseq=NSEQ, ns=NS):
    consts = make_consts()
    f = lambda a: np.ascontiguousarray(np.asarray(a, dtype=np.float32))
    maps = []
    for c in range(ncores):
        m = dict(consts)
        m["x_prompt"] = f(inputs["x_prompt"][c * nseq:(c + 1) * nseq])
        m["x_sample"] = f(inputs["x_sample"][c * ns:(c + 1) * ns, 0, :])
        for k in ("cache_a", "cache_b1", "cache_b2", "cache_b3"):
            a = np.asarray(inputs[k])[:, c * ns:(c + 1) * ns]
            m[k] = f(a.reshape(a.shape[0], ns, 2, a.shape[3], 128))
        m["state_c"] = f(np.asarray(inputs["state_c"])[:, c * ns:(c + 1) * ns])
        for k in _W_NAMES:
            m[k] = f(inputs[k])
        maps.append(m)
    return maps


def kernel(**inputs):
    nc = build()
    maps = make_in_maps(inputs)
    res = run_bass_kernel_spmd(nc, maps, core_ids=list(range(8)))
    R = res.results
    cat = lambda k, ax: np.concatenate([np.asarray(r[k]) for r in R], axis=ax)
    outs = []
    outs.append(cat("y_prompt", 0))
    outs.append(cat("y_sample", 0).reshape(8 * NS, 1, D))
    for nm, rows in (("a", 128), ("b1", 128), ("b2", 512), ("b3", 2048)):
        p = cat(f"new_{nm}_prompt", 1)
        outs.append(p.reshape(DEPTH, 8 * NSEQ, 2, rows, 2, 64))
        s_ = cat(f"new_{nm}_sample", 1)
        outs.append(s_.reshape(DEPTH, 8 * NS, 2, rows, 2, 64))
    outs.append(cat("new_c_prompt", 1))
    outs.append(cat("new_c_sample", 1))
    outs.append(cat("new_d_sample", 1).reshape(DEPTH, 8 * NS, 1, 384))
    return tuple(np.ascontiguousarray(o, dtype=np.float32) for o in outs)
```

```python
import math
from contextlib import ExitStack
from functools import partial

import numpy as np
import concourse.bass as bass
import concourse.mybir as mybir
from concourse.bass_utils import run_bass_kernel_spmd

F32 = mybir.dt.float32
BF16 = mybir.dt.bfloat16
AF = mybir.ActivationFunctionType
ALU = mybir.AluOpType
AX = mybir.AxisListType

D = 1024
SEQ = 2048
DEPTH = 2
NSEQ = 2
NS = 16
PAST = 16384
T = 512
NT = SEQ // T
NIN = 4480
DFF = 2816
EPS = 1e-6
WSLOT = 5632
NWS = 4
COPY_START_TL = 4
STRICT_SAME_ENG = True
EW2 = "dve"
DILS = (1, 1, 4, 16)
CROWS = (128, 128, 512, 2048)


class Res:
    __slots__ = ("name", "w", "r_eng", "r_dma")

    def __init__(self, name):
        self.name = name
        self.w = None
        self.r_eng = {}
        self.r_dma = []


class DSem:
    __slots__ = ("sem", "total")

    def __init__(self, sem):
        self.sem = sem
        self.total = 0


class Op:
    __slots__ = ("eng", "fn", "deps", "dsem", "val", "need_inc", "cnt")


class Sched:
    ENGS = ("pe", "act", "dve", "pool", "sp")

    def __init__(self, nc, es):
        self.nc = nc
        self.ops = {e: [] for e in self.ENGS}
        self.esem = {e: es.enter_context(nc.semaphore("sem_" + e)) for e in self.ENGS}
        self.dsems = []
        self.es = es
        self.nops = 0

    def new_dsem(self, name):
        d = DSem(self.es.enter_context(self.nc.semaphore("d_" + name)))
        self.dsems.append(d)
        return d

    def op(self, eng, fn, reads=(), writes=(), dsem=None):
        o = Op()
        o.eng = eng
        o.fn = fn
        o.dsem = dsem
        o.need_inc = False
        o.cnt = None
        o.val = None
        if dsem is not None:
            dsem.total += 16
            o.val = dsem.total
        raw = set()
        waw = set()
        war = set()
        deps = set()
        for r in reads:
            if r.w is not None:
                deps.add(r.w)
                raw.add(r.w)
        for w in writes:
            if w.w is not None:
                deps.add(w.w)
                waw.add(w.w)
            deps.update(w.r_eng.values())
            deps.update(w.r_dma)
            war.update(w.r_eng.values())
            war.update(w.r_dma)
        final = []
        for d in deps:
            if d is o:
                continue
            if d.dsem is None and d.eng == eng:
                if eng == "pe":
                    continue
                if d not in raw and dsem is None and not STRICT_SAME_ENG:
                    continue
            if d.dsem is not None and d.dsem is dsem and d in waw and d not in raw and d not in war:
                continue
            if d.dsem is None:
                d.need_inc = True
            final.append(d)
        o.deps = final
        for r in reads:
            if dsem is not None:
                r.r_dma.append(o)
            else:
                r.r_eng[eng] = o
        for w in writes:
            w.w = o
            w.r_eng = {}
            w.r_dma = []
        self.ops[eng].append(o)
        self.nops += 1
        return o

    def emit(self, block):
        nc = self.nc
        for eng in self.ENGS:
            c = 0
            for o in self.ops[eng]:
                if o.dsem is None and o.need_inc:
                    c += 1
                    o.cnt = c

        def run(eng, e):
            waited = {}
            for o in self.ops[eng]:
                need = {}
                for d in o.deps:
                    if d.dsem is not None:
                        sem, val = d.dsem.sem, d.val
                    else:
                        sem, val = self.esem[d.eng], d.cnt
                    k = id(sem)
                    if waited.get(k, 0) >= val:
                        continue
                    if k not in need or need[k][1] < val:
                        need[k] = (sem, val)
                ws = list(need.values())
                for k, (sem, val) in need.items():
                    waited[k] = val
                for sem, val in ws[1:]:
                    e.wait_ge(sem, val)
                inst = o.fn(e)
                if ws:
                    inst._wait_ge(ws[0][0], ws[0][1])
                if o.dsem is not None:
                    inst.then_inc(o.dsem.sem, 16)
                elif o.need_inc:
                    inst.then_inc(self.esem[eng], 1)
            if eng == "sp":
                for d in self.dsems:
                    if d.total > 0:
                        e.wait_ge(d.sem, d.total)
                for en in self.ENGS:
                    if en == "sp":
                        continue
                    last = None
                    for o in self.ops[en]:
                        if o.cnt is not None:
                            last = o.cnt
                    if last:
                        e.wait_ge(self.esem[en], last)

        @block.tensor
        def _(e):
            run("pe", e)

        @block.scalar
        def _(e):
            run("act", e)

        @block.vector
        def _(e):
            run("dve", e)

        @block.gpsimd
        def _(e):
            run("pool", e)

        @block.sync
        def _(e):
            run("sp", e)


class Ring:
    def __init__(self, items):
        self.items = items
        self.i = 0

    def next(self):
        x = self.items[self.i % len(self.items)]
        self.i += 1
        return x


def AP(t, off, dims):
    return bass.AP(t, off, [list(d) for d in dims])


def _rope_tab(pos):
    half = 8
    inv = (np.float32(500000.0) ** (-2.0 * np.arange(half, dtype=np.float32) / np.float32(16))).astype(np.float32)
    ang = (pos.astype(np.float32)[:, None] * inv[None, :]).astype(np.float32)
    return np.cos(ang).astype(np.float32), np.sin(ang).astype(np.float32)


def _block_positions(order, blk):
    p = np.arange(128)
    t, b = divmod(blk, 4)
    if order == 0:
        return 128 * blk + p
    if order == 1:
        return 512 * t + b + 4 * p
    return 512 * t + 4 * b + (p // 32) + 16 * (p % 32)


def make_consts():
    tabs = np.zeros((128, 3, 16, 2, 8), np.float32)
    for o in range(3):
        for blk in range(16):
            c, s = _rope_tab(_block_positions(o, blk))
            tabs[:, o, blk, 0] = c
            tabs[:, o, blk, 1] = s
    cs, ss = _rope_tab(np.full((NS,), PAST))
    tabs_s = np.zeros((NS, 2, 8), np.float32)
    tabs_s[:, 0] = cs
    tabs_s[:, 1] = ss
    s = np.arange(128)[:, None]
    q = np.arange(128)[None, :]
    m_prev = (s >= q).astype(np.float32)
    m_cur = (s <= q).astype(np.float32)
    same = ((s % 4) == (q % 4)).astype(np.float32)
    m3_prev = same
    m3_cur = same * (s <= q).astype(np.float32)
    masks = np.stack([m_prev, m_cur, m3_prev, m3_cur], 0)
    masks = np.repeat(masks[:, :, None, :], 3, axis=2)
    masks = np.ascontiguousarray(masks.transpose(1, 0, 2, 3)).reshape(128, 4 * 384)
    ident = np.eye(128, dtype=np.float32)
    return {"c_tabs": tabs.reshape(128, -1), "c_tabs_s": tabs_s.reshape(NS, -1), "c_masks": masks,
            "c_ident": ident}


def build(nseq=NSEQ, ns=NS, depth=DEPTH, do_copy=True, phases=99, ntiles=NT):
    nc = bass.Bass("TRN2", target_bir_lowering=False)
    es = ExitStack()
    with es:
        S = Sched(nc, es)

        def din(name, shape, dt=F32):
            return nc.dram_tensor(name, list(shape), dt, kind="ExternalInput").ap()

        def dout(name, shape, dt=F32):
            return nc.dram_tensor(name, list(shape), dt, kind="ExternalOutput").ap()

        nsq = max(nseq, 1)
        nsa = max(ns, 1)
        x_p = din("x_prompt", [nsq, SEQ, D])
        x_s = din("x_sample", [nsa, D])
        caches = [din("cache_a", [depth, nsa, 2, 128, 128]), din("cache_b1", [depth, nsa, 2, 128, 128]),
                  din("cache_b2", [depth, nsa, 2, 512, 128]), din("cache_b3", [depth, nsa, 2, 2048, 128])]
        state_c = din("state_c", [depth, nsa, 2, 384])
        ln1 = din("ln1", [depth, D])
        w_in = din("w_in", [depth, D, NIN])
        qn_a = din("q_norm_a", [depth, 64])
        kn_a = din("k_norm_a", [depth, 64])
        sink_a = din("sink_a", [depth, 6])
        qn_b = din("q_norm_b", [depth, 3, 64])
        kn_b = din("k_norm_b", [depth, 3, 64])
        conv_c = din("conv_c", [depth, 3, 384])
        ws_d = din("ws_d", [depth, 6, 128, 128])
        bs_d = din("bs_d", [depth, 6, 128])
        w_gate = din("w_gate", [depth, 4, D, D])
        b_gate = din("b_gate", [depth, 4, D])
        w_branch = din("w_branch", [depth, 4, 384, D])
        w_o = din("w_o", [depth, D, D])
        ln2 = din("ln2", [depth, D])
        w_fi = din("w_ffn_in", [depth, D, 2 * DFF])
        w_fo = din("w_ffn_out", [depth, DFF, D])
        c_tabs = din("c_tabs", [128, 3 * 16 * 16])
        c_tabs_s = din("c_tabs_s", [NS, 16])
        c_masks = din("c_masks", [128, 4 * 384])
        c_ident = din("c_ident", [128, 128])

        y_p = dout("y_prompt", [nsq, SEQ, D])
        y_s = dout("y_sample", [nsa, D])
        new_p = [dout("new_a_prompt", [depth, nsq, 2, 128, 128]), dout("new_b1_prompt", [depth, nsq, 2, 128, 128]),
                 dout("new_b2_prompt", [depth, nsq, 2, 512, 128]), dout("new_b3_prompt", [depth, nsq, 2, 2048, 128])]
        new_s = [dout("new_a_sample", [depth, nsa, 2, 128, 128]), dout("new_b1_sample", [depth, nsa, 2, 128, 128]),
                 dout("new_b2_sample", [depth, nsa, 2, 512, 128]), dout("new_b3_sample", [depth, nsa, 2, 2048, 128])]
        new_c_p = dout("new_c_prompt", [depth, nsq, 2, 384])
        new_c_s = dout("new_c_sample", [depth, nsa, 2, 384])
        new_d_s = dout("new_d_sample", [depth, nsa, 384])
        q_scr = nc.dram_tensor("q_scr", [4, NS, 384], F32, kind="Internal").ap()

        def sb(name, shape, dt):
            return es.enter_context(nc.sbuf_tensor(name, list(shape), dt))

        xb = [sb(f"xb{i}", [128, D], F32) for i in range(4)]
        xb_r = [Res(f"xb{i}") for i in range(4)]
        xb_ds = [S.new_dsem(f"xb{i}") for i in range(4)]
        hT = sb("hT", [128, 8, T], BF16)
        hT_r = Res("hT")
        oT = [sb(f"oT{i}", [128, 3, T], BF16) for i in range(4)]
        oT_r = [Res(f"oT{i}") for i in range(4)]
        bacc = sb("bacc", [128, 2 * 3 * T], F32)
        bacc_r = Res("bacc")
        actT = bacc[:].bitcast(BF16)
        mT = sb("mT", [128, 8, T], BF16)
        mT_r = Res("mT")
        KT = [sb(f"KT{g}", [128, (8 if g < 3 else 16) * 128], BF16) for g in range(4)]
        KT_r = [[Res(f"KT{g}_{i}") for i in range(8 if g < 3 else 16)] for g in range(4)]
        Vt = [sb(f"Vt{g}", [128, (8 if g < 3 else 16), 192], BF16) for g in range(4)]
        Vt_r = [[Res(f"Vt{g}_{i}") for i in range(8 if g < 3 else 16)] for g in range(4)]
        QT = sb("QT", [128, 4, 384], BF16)
        QT_r = [Res(f"QT{b}") for b in range(4)]
        wsl = [sb(f"wsl{i}", [128, WSLOT], BF16) for i in range(NWS)]
        wsl_r = [Res(f"wsl{i}") for i in range(NWS)]
        wsl_ds = [S.new_dsem(f"wsl{i}") for i in range(NWS)]
        g_bc = sb("g_bc", [128, D], F32)
        g_bc_r = Res("g_bc")
        gqk = sb("gqk", [128, 4, 2, 64], F32)
        tabs = sb("tabs", [128, 3, 16, 2, 8], F32)
        tabs_s = sb("tabs_s", [NS, 2, 8], F32)
        masks = sb("masks", [128, 4, 384], BF16)
        ident = sb("ident", [128, 128], BF16)
        identf = sb("identf", [128, 128], F32)
        wmT = sb("wmT", [128, 6, 128], BF16)
        wsnat = sb("wsnat", [128, 6, 128], F32)
        lrow = sb("lrow", [48, 128], F32)
        lcol = sb("lcol", [128, 48], F32)
        es_t = sb("es_t", [128, 6], F32)
        es_rows = sb("es_rows", [1, 2, 3, 128], BF16)
        sel = sb("sel", [1, 2, 128], BF16)
        epst = sb("epst", [128, 1], F32)
        lay_r = Res("layer_consts")
        const_r = Res("consts")
        xn = [sb(f"xn{i}", [128, D], BF16) for i in range(2)]
        xn_ring = Ring([(xn[i], Res(f"xn{i}")) for i in range(2)])
        junk = sb("junk", [128, D], BF16)
        junk_r = Res("junk")
        ssq = sb("ssq", [128, 8], F32)
        ssq_ring = Ring([(ssq[:, 2 * i:2 * i + 2], Res(f"ssq{i}")) for i in range(4)])
        qk = [sb(f"qk{i}", [128, 8, 64], F32) for i in range(2)]
        qk_ring = Ring([(qk[i], Res(f"qk{i}"), S.new_dsem(f"qk{i}")) for i in range(2)])
        vst = [sb(f"vst{i}", [128, 128], F32) for i in range(2)]
        vst_ring = Ring([(vst[i], Res(f"vst{i}"), S.new_dsem(f"vst{i}")) for i in range(2)])
        ss8 = sb("ss8", [128, 4, 8], F32)
        ss8_ring = Ring([(ss8[:, i, :], Res(f"ss8_{i}")) for i in range(4)])
        ropet = sb("ropet", [128, 2, 2, 8, 16], F32)
        rope_ring = Ring([(ropet[:, i], Res(f"rope{i}")) for i in range(2)])
        qbf = sb("qbf", [128, 3, 3, 128], BF16)
        kbf = sb("kbf", [128, 3, 128], BF16)
        qbf_ring = Ring([(qbf[:, i], kbf[:, i], Res(f"qbf{i}")) for i in range(3)])
        pT = sb("pT", [128, 4, 384], BF16)
        pT_ring = Ring([(pT[:, i], Res(f"pT{i}")) for i in range(4)])
        ftmp = sb("ftmp", [128, 5, T], F32)
        f_ring = Ring([(ftmp[:, i], Res(f"ftmp{i}")) for i in range(5)])
        zc = sb("zc", [128, 3, T + 2], F32)
        zc_r = [Res(f"zc{c}") for c in range(3)]
        zc_ds = [S.new_dsem(f"zc{c}") for c in range(3)]
        vbf = sb("vbf", [128, 2, 384], BF16)
        obf = sb("obf", [128, 2, 384], BF16)
        vbf_ring = Ring([(vbf[:, i], Res(f"vbf{i}")) for i in range(2)])
        obf_ring = Ring([(obf[:, i], Res(f"obf{i}")) for i in range(2)])

        hTs = sb("hTs", [128, 8, NS], BF16)
        hTs_r = Res("hTs")
        oTs = [sb(f"oTs{i}", [128, 3, NS], BF16) for i in range(4)]
        oTs_r = [Res(f"oTs{i}") for i in range(4)]
        mTs = sb("mTs", [128, 8, NS], BF16)
        mTs_r = Res("mTs")
        actTs = sb("actTs", [128, 11 * NS], BF16)
        actTs_r = Res("actTs")
        selfT = sb("selfT", [64, 10 * NS], F32)
        self_r = Res("selfT")
        nacc = sb("nacc", [64, 2, 96], F32)
        nacc_r = Res("nacc")
        s4t = sb("s4t", [128, 2, 24], F32)
        s4_ring = Ring([(s4t[:, i, :], Res(f"s4_{i}")) for i in range(2)])
        sm6 = sb("sm6", [NS, 2, 6], F32)
        sm_r = Res("sm6")
        sm_ds = S.new_dsem("sm6")
        onesf = sb("onesf", [128, 64], F32)
        kc_ds = S.new_dsem("kc4")
        qb_ds = S.new_dsem("qb4")
        qscr_r = [Res(f"qscr{g}") for g in range(4)]
        gqk_s = sb("gqk_s", [128, 4, 2, 64], F32)
        lrow_s = sb("lrow_s", [48, 128], F32)
        lcol_s = sb("lcol_s", [128, 48], F32)
        es_t_s = sb("es_t_s", [128, 6], F32)
        lds_ = S.new_dsem("layc")
        LC_P = {"gqk": gqk, "lrow": lrow, "lcol": lcol, "es_t": es_t, "r": lay_r, "ds": lds_, "full": True}
        LC_S = {"gqk": gqk_s, "lrow": lrow_s, "lcol": lcol_s, "es_t": es_t_s, "r": Res("lay_s"),
                "ds": S.new_dsem("layc_s"), "full": False}
        sscr = sb("sscr", [128, 2048], F32)
        sscr_r = Res("sscr")
        sscr_ds = S.new_dsem("sscr")
        Kc4 = sscr[:, 0:256]
        Vc4 = sscr[:, 256:512]
        zv_t = sb("zv_t", [NS, 768], F32)
        zv_r = Res("zv_t")
        zv_ds = S.new_dsem("zv_t")
        xs_t = sb("xs_t", [NS, D], F32)
        xs_r_ = Res("xs_t")
        xs_ds = S.new_dsem("xs_t")

        psb = [es.enter_context(nc.psum_tensor(f"ps{i}", [128, 512], F32)) for i in range(8)]
        ps_ring = Ring([(psb[i], Res(f"ps{i}")) for i in range(8)])

        class _TpRing:
            def next(self):
                t_, r_ = ps_ring.next()
                return t_[:].bitcast(BF16)[:, 0:512], r_
        tp_ring = _TpRing()

        def mm(out, lhsT, rhs, start, stop, reads, writes):
            S.op("pe", lambda e: e.matmul(out, lhsT, rhs, start=start, stop=stop), reads, writes)

        def tr(out, in_, idn, reads, writes):
            S.op("pe", lambda e: e.transpose(out, in_, idn), reads, writes)

        def act(out, in_, func, reads, writes, bias=None, scale=None, accum=None):
            kw = {}
            if bias is not None:
                kw["bias"] = bias
            if scale is not None:
                kw["scale"] = scale
            if accum is not None:
                kw["accum_out"] = accum
            S.op("act", lambda e: e.activation(out=out, in_=in_, func=func, **kw), reads, writes)

        def tt(eng, out, in0, in1, op, reads, writes):
            S.op(eng, lambda e: e.tensor_tensor(out=out, in0=in0, in1=in1, op=op), reads, writes)

        def ts(eng, out, in0, s1, s2, op0, op1, reads, writes):
            if op1 is None:
                S.op(eng, lambda e: e.tensor_scalar(out=out, in0=in0, scalar1=s1, scalar2=None, op0=op0), reads, writes)
            else:
                S.op(eng, lambda e: e.tensor_scalar(out=out, in0=in0, scalar1=s1, scalar2=s2, op0=op0, op1=op1),
                     reads, writes)

        def stt(eng, out, in0, scalar, in1, op0, op1, reads, writes):
            S.op(eng, lambda e: e.scalar_tensor_tensor(out=out, in0=in0, scalar=scalar, in1=in1, op0=op0, op1=op1),
                 reads, writes)

        def cp(eng, out, in_, reads, writes):
            if eng == "act":
                act(out, in_, AF.Copy, reads, writes)
            else:
                S.op(eng, lambda e: e.tensor_copy(out=out, in_=in_), reads, writes)

        def red(out, in_, reads, writes):
            S.op("dve", lambda e: e.tensor_reduce(out=out, in_=in_, axis=AX.X, op=ALU.add), reads, writes)

        def recip(out, in_, reads, writes, use_act=True):
            if use_act:
                act(out, in_, AF.Ln, reads, writes)
                act(out, out, AF.Exp, writes, writes, scale=-1.0)
            else:
                S.op("dve", lambda e: e.reciprocal(out=out, in_=in_), reads, writes)

        def memset(eng, ap, val, writes):
            S.op(eng, lambda e: e.memset(ap, val), (), writes)

        def dma(q, out, in_, dsem, reads, writes, nonc=False):
            if nonc:
                S.op(q, lambda e: e.dma_start(out=out, in_=in_, allow_slow_non_contiguous=True), reads, writes, dsem=dsem)
            else:
                S.op(q, lambda e: e.dma_start(out=out, in_=in_), reads, writes, dsem=dsem)

        def bcast_rows(ap1d, n, parts=128):
            return AP(ap1d.tensor, ap1d.offset, [[0, parts], [1, n]])

        cds = S.new_dsem("consts")
        gds = S.new_dsem("gbc")
        dma("sp", tabs[:].rearrange("p a b c d -> p (a b c d)"), c_tabs[:, :], cds, (), (const_r,))
        dma("sp", tabs_s[:].rearrange("p c d -> p (c d)"), c_tabs_s[:, :], cds, (), (const_r,))
        dma("sp", identf[:], c_ident[:, :], cds, (), (const_r,))
        cds2 = S.new_dsem("consts2")
        const2_r = Res("consts2")
        dma("pool", masks[:].rearrange("p a b -> p (a b)"), c_masks[:, :], cds2, (), (const2_r,))
        dma("pool", ident[:], c_ident[:, :], cds2, (), (const2_r,))
        S.op("dve", lambda e: e.memset(epst[:], EPS), (const2_r,), (const_r,))
        memset("dve", sel[:], 1.0, (const_r,))
        memset("pool", onesf[:], 1.0, (const_r,))
        memset("dve", sel[0:1, 0, 0:64], 0.0, (const_r,))
        memset("dve", sel[0:1, 1, 64:128], 0.0, (const_r,))
        for g in range(4):
            memset("pool", Vt[g][:, :, 64:128], 1.0, [r for r in Vt_r[g]])
        dram_y_r = {}

        wq_state = {"i": 0}

        wl_marks = []

        def wload(parts):
            i = wq_state["i"] % NWS
            wq_state["i"] += 1
            t, r, ds = wsl[i], wsl_r[i], wsl_ds[i]
            wl_marks.append((len(S.ops["pool"]), len(parts)))
            for (off, dims, npart, p0, src) in parts:
                dst = AP(t, p0 * WSLOT + off, [[WSLOT, npart]] + dims)
                dma("pool", dst, src, ds, (), (r,))
            return t, r

        def hoist_wloads(dist):
            lst = S.ops["pool"]
            groups = {}
            skip = set()
            for k, (idx, n) in enumerate(wl_marks):
                tgt = wl_marks[k - dist][0] if k >= dist else idx
                groups.setdefault(tgt, []).extend(lst[idx:idx + n])
                skip.update(range(idx, idx + n))
            new = []
            for i, o in enumerate(lst):
                if i in groups:
                    new.extend(groups[i])
                if i not in skip:
                    new.append(o)
            assert len(new) == len(lst)
            S.ops["pool"] = new

        def wsrc(w2d, r0, nk, c0, ncol):
            rs = w2d.ap[0][0]
            return AP(w2d.tensor, w2d.offset + r0 * rs + c0, [[rs, 128], [128 * rs, nk], [1, ncol]])

        def load_layer_consts(l, lc):
            lds = lc["ds"]
            lr = lc["r"]
            W = (lr,)
            g_, lrow_, lcol_, es_ = lc["gqk"], lc["lrow"], lc["lcol"], lc["es_t"]
            dma("sp", g_[:, 0, 0, :], bcast_rows(qn_a[l, :], 64), lds, (), W)
            dma("sp", g_[:, 0, 1, :], bcast_rows(kn_a[l, :], 64), lds, (), W)
            for g in range(3):
                dma("sp", g_[:, 1 + g, 0, :], bcast_rows(qn_b[l, g, :], 64), lds, (), W)
                dma("sp", g_[:, 1 + g, 1, :], bcast_rows(kn_b[l, g, :], 64), lds, (), W)
            dma("sp", es_[:], bcast_rows(sink_a[l, :], 6), lds, (), W)
            dma("sp", lrow_[0:32, :], b_gate[l].rearrange("n (m p) -> (n m) p", p=128), lds, (), W)
            dma("sp", lrow_[32:41, :], conv_c[l].rearrange("j (c p) -> (j c) p", p=128), lds, (), W)
            dma("sp", lrow_[41:47, :], bs_d[l], lds, (), W)
            if lc["full"]:
                dma("sp", wsnat[:], ws_d[l].rearrange("g i j -> i g j"), lds, (), W)
            act(es_[:], es_[:], AF.Exp, (lr,), (lr,))
            if lc["full"]:
                cp("dve", es_rows[0:1].rearrange("o k g q -> o (k g) q"),
                   AP(es_, 0, [[6, 1], [1, 6], [0, 128]]), (lr,), (lr,))
            pt, pr = ps_ring.next()
            tr(pt[:, 0:47], lrow_[0:47, :], identf[0:47, 0:47], (lr, const_r), (pr,))
            cp("dve", lcol_[:, 0:47], pt[:, 0:47], (pr,), (lr,))
            if lc["full"]:
                for g in range(6):
                    pt, pr = ps_ring.next()
                    tr(pt[:, 0:128], wsnat[:, g, :], identf[:], (lr, const_r), (pr,))
                    tt("dve", wmT[:, g, :], pt[:, 0:128], masks[:, 1, 0:128], ALU.mult, (pr, const_r), (lr,))

        gbc_pre = [None]

        def do_norm(P, nblk, xsrc, gsrc_ap, hdst, gkey=None):
            if gkey is not None and gbc_pre[0] == gkey:
                gbc_pre[0] = None
            else:
                gbc_pre[0] = None
                dma("sp", g_bc[:], bcast_rows(gsrc_ap, D), gds, (), (g_bc_r,))
            ht, hr = hdst
            for b in range(nblk):
                xa, xr = xsrc[b]
                ssa, ssr = ssq_ring.next()
                memset("dve", ssa[0:P, :], 0.0, (ssr,))
                act(junk[0:P, :], xa, AF.Square, (xr,), (junk_r, ssr), accum=ssa[0:P, 0:1])
                act(ssa[0:P, 1:2], ssa[0:P, 0:1], AF.Ln, (ssr, const_r), (ssr,), scale=1.0 / D, bias=epst[0:P, :])
                act(ssa[0:P, 1:2], ssa[0:P, 1:2], AF.Exp, (ssr,), (ssr,), scale=-0.5)
                xt, xtr = xn_ring.next()
                stt("dve", xt[0:P, :], xa, ssa[0:P, 1:2], g_bc[0:P, :], ALU.mult, ALU.mult, (xr, ssr, g_bc_r), (xtr,))
                for hh in range(2):
                    tp, tpr = tp_ring.next()
                    for c in range(4):
                        cc = hh * 4 + c
                        tr(tp[:, c * 128:c * 128 + P], xt[0:P, cc * 128:(cc + 1) * 128], ident[0:P, 0:P],
                           (xtr, const_r), (tpr,))
                    src = AP(tp.tensor, tp.offset, [list(tp.ap[0]), [128, 4], [1, P]])
                    dst = ht[:, hh * 4:hh * 4 + 4, b * 128:b * 128 + P]
                    cp("act", dst, src, (tpr,), (hr,))

        def qk_norm_rope(P, qa, qr, g, cos_ap, sin_ap, lc, srcq, srck, mid=None):
            (qps, qpr), (kps, kpr) = srcq, srck
            s8, s8r = ss8_ring.next()
            fq, fqr = f_ring.next()
            act(fq[0:P, 0:384].rearrange("p (h d) -> p h d", d=64), qps, AF.Square, (qpr,), (fqr,))
            act(fq[0:P, 384:512].rearrange("p (h d) -> p h d", d=64), kps, AF.Square, (kpr,), (fqr,))
            red(s8[0:P, :], fq[0:P, :].rearrange("p (h d) -> p h d", d=64), (fqr,), (s8r,))
            if mid is not None:
                mid()
            act(s8[0:P, :], s8[0:P, :], AF.Ln, (s8r, const_r), (s8r,), scale=1.0 / 64, bias=epst[0:P, :])
            act(s8[0:P, :], s8[0:P, :], AF.Exp, (s8r,), (s8r,), scale=-0.5)
            s8q = AP(s8.tensor, s8.offset, [[s8.ap[0][0], P], [1, 6], [0, 64]])
            s8k = AP(s8.tensor, s8.offset + 6, [[s8.ap[0][0], P], [1, 2], [0, 64]])
            tt("dve", qa[0:P, 0:6, :], qps, s8q, ALU.mult, (qpr, s8r), (qr,))
            tt("dve", qa[0:P, 6:8, :], kps, s8k, ALU.mult, (kpr, s8r), (qr,))
            gq = AP(lc["gqk"], (g * 2 + 0) * 64, [[512, P], [0, 6], [1, 64]])
            gk = AP(lc["gqk"], (g * 2 + 1) * 64, [[512, P], [0, 2], [1, 64]])
            tt(EW2, qa[0:P, 0:6, :], qa[0:P, 0:6, :], gq, ALU.mult, (qr, lc["r"]), (qr,))
            tt(EW2, qa[0:P, 6:8, :], qa[0:P, 6:8, :], gk, ALU.mult, (qr, lc["r"]), (qr,))
            rt, rr = rope_ring.next()
            tA = rt[0:P, 0]
            tB = rt[0:P, 1]
            y16 = qa[0:P, :, 0:16].rearrange("p h (a d) -> p h a d", a=2)
            tA4 = tA.rearrange("p h (a d) -> p h a d", a=2)
            tB4 = tB.rearrange("p h (a d) -> p h a d", a=2)
            tt(EW2, tA4, y16, cos_ap, ALU.mult, (qr, const_r), (rr,))
            tt(EW2, tB4, y16, sin_ap, ALU.mult, (qr, const_r), (rr,))
            tt(EW2, qa[0:P, :, 0:8], tA[:, :, 0:8], tB[:, :, 8:16], ALU.subtract, (rr,), (qr,))
            tt(EW2, qa[0:P, :, 8:16], tA[:, :, 8:16], tB[:, :, 0:8], ALU.add, (rr,), (qr,))

        def tab_aps(P, order, blk):
            base = (order * 16 + blk) * 16
            c = AP(tabs, base, [[768, P], [0, 8], [0, 2], [1, 8]])
            s = AP(tabs, base + 8, [[768, P], [0, 8], [0, 2], [1, 8]])
            return c, s

        def colpat(order, b):
            if order == 0:
                return b * 128, [[1, 128]]
            if order == 1:
                return b, [[4, 128]]
            return 4 * b, [[1, 4], [16, 32]]


        def dense_tail(cx, l):
            P, NTK, nblk = cx["P"], cx["ntok"], cx["nblk"]
            c_hT, c_hT_r = cx["hT"]
            c_oT, c_oT_r = cx["oT"]
            c_mT, c_mT_r = cx["mT"]
            c_act, c_act_r = cx["act"]
            xbl = cx["x"]
            for m in range(8 if phases >= 5 else 0):
                parts = [(n * 128, [[512, 8], [1, 128]], 128, 0,
                          AP(w_gate.tensor, w_gate[l, n].offset + m * 128, [[D, 128], [128 * D, 8], [1, 128]]))
                         for n in range(4)]
                wbo = 4096
                for n in range(4):
                    base = w_branch[l, n]
                    if n < 2:
                        for hh in range(2):
                            parts.append((wbo + n * 384, [[128, 3], [1, 128]], 64, 64 * hh,
                                          AP(base.tensor, base.offset + hh * 192 * D + m * 128, [[D, 64], [64 * D, 3], [1, 128]])))
                    else:
                        parts.append((wbo + n * 384, [[128, 3], [1, 128]], 128, 0,
                                      AP(base.tensor, base.offset + m * 128, [[D, 128], [128 * D, 3], [1, 128]])))
                wt, wr = wload(parts)
                macc, maccr = f_ring.next()
                for n in range(4):
                    pg, pgr = ps_ring.next()
                    pp, ppr = ps_ring.next()
                    for kc in range(8):
                        mm(pg[:, 0:NTK], AP(wt, kc * 512 + n * 128, [[WSLOT, 128], [1, 128]]), c_hT[:, kc, :], kc == 0, kc == 7,
                           (c_hT_r, wr), (pgr,))
                    for c in range(3):
                        mm(pp[:, 0:NTK], AP(wt, wbo + n * 384 + c * 128, [[WSLOT, 128], [1, 128]]), c_oT[n][:, c, :], c == 0, c == 2,
                           (c_oT_r[n], wr), (ppr,))
                    sg, sgr = f_ring.next()
                    act(sg[:, 0:NTK], pg[:, 0:NTK], AF.Sigmoid, (pgr, cx["lc"]["r"]), (sgr,),
                        bias=cx["lc"]["lcol"][:, n * 8 + m:n * 8 + m + 1])
                    if n == 0:
                        tt("dve", macc[:, 0:NTK], sg[:, 0:NTK], pp[:, 0:NTK], ALU.mult, (sgr, ppr), (maccr,))
                    else:
                        tt("dve", sg[:, 0:NTK], sg[:, 0:NTK], pp[:, 0:NTK], ALU.mult, (sgr, ppr), (sgr,))
                        if n < 3:
                            tt(EW2, macc[:, 0:NTK], macc[:, 0:NTK], sg[:, 0:NTK], ALU.add, (maccr, sgr), (maccr,))
                        else:
                            tt(EW2, c_mT[:, m, :], macc[:, 0:NTK], sg[:, 0:NTK], ALU.add, (maccr, sgr), (c_mT_r,))
                if m % 2 == 1:
                    yield

            for ch in range(2 if phases >= 6 else 0):
                wt, wr = wload([(0, [[512, 8], [1, 512]], 128, 0, wsrc(w_o[l], 0, 8, ch * 512, 512))])
                for b in range(nblk):
                    px, pxr = ps_ring.next()
                    xa, xr = xbl[b]
                    for kc in range(8):
                        mm(px[0:P, :], c_mT[:, kc, b * 128:b * 128 + P], AP(wt, kc * 512, [[WSLOT, 128], [1, 512]]),
                           kc == 0, kc == 7, (c_mT_r, wr), (pxr,))
                    tt("dve", xa[:, ch * 512:(ch + 1) * 512], xa[:, ch * 512:(ch + 1) * 512], px[0:P, :], ALU.add,
                       (xr, pxr), (xr,))

            yield
            if phases >= 7:
                do_norm(P, nblk, xbl, ln2[l, :], (c_hT, c_hT_r))
                if cx.get("after_norm2") is not None:
                    cx["after_norm2"]()
            yield

            for dh in range(2 if phases >= 8 else 0):
                f0 = dh * 11
                jj = 0
                while jj < 11:
                    nj = 2 if jj + 2 <= 11 else 1
                    c0 = (f0 + jj) * 128
                    parts = [(0, [[2 * nj * 128, 8], [1, nj * 128]], 128, 0, wsrc(w_fi[l], 0, 8, c0, nj * 128)),
                             (nj * 128, [[2 * nj * 128, 8], [1, nj * 128]], 128, 0, wsrc(w_fi[l], 0, 8, DFF + c0, nj * 128))]
                    wt, wr = wload(parts)
                    for j in range(nj):
                        pg, pgr = ps_ring.next()
                        pu, pur = ps_ring.next()
                        for kc in range(8):
                            mm(pg[:, 0:NTK], AP(wt, kc * 2 * nj * 128 + j * 128, [[WSLOT, 128], [1, 128]]), c_hT[:, kc, :],
                               kc == 0, kc == 7, (c_hT_r, wr), (pgr,))
                        for kc in range(8):
                            mm(pu[:, 0:NTK], AP(wt, kc * 2 * nj * 128 + nj * 128 + j * 128, [[WSLOT, 128], [1, 128]]), c_hT[:, kc, :],
                               kc == 0, kc == 7, (c_hT_r, wr), (pur,))
                        sg, sgr = f_ring.next()
                        act(sg[:, 0:NTK], pg[:, 0:NTK], AF.Silu, (pgr,), (sgr,))
                        tt("dve", c_act[:, (jj + j) * NTK:(jj + j + 1) * NTK], sg[:, 0:NTK], pu[:, 0:NTK], ALU.mult, (sgr, pur), (c_act_r,))
                    jj += nj
                    if jj % 4 == 0:
                        yield
                yield
                if dh == 1 and cx.get("done") is not None:
                    wts = [wload([(0, [[512, 11], [1, 512]], 128, 0, wsrc(w_fo[l], f0 * 128, 11, ch * 512, 512))])
                           for ch in range(2)]
                    for b in range(nblk):
                        xa, xr = xbl[b]
                        for ch in range(2):
                            wt, wr = wts[ch]
                            px, pxr = ps_ring.next()
                            for kc in range(11):
                                mm(px[0:P, :], c_act[:, kc * NTK + b * 128:kc * NTK + b * 128 + P],
                                   AP(wt, kc * 512, [[WSLOT, 128], [1, 512]]), kc == 0, kc == 10, (c_act_r, wr), (pxr,))
                            tt("dve", xa[:, ch * 512:(ch + 1) * 512], xa[:, ch * 512:(ch + 1) * 512], px[0:P, :], ALU.add,
                               (xr, pxr), (xr,))
                        cx["done"](b)
                    continue
                for ch in range(2):
                    wt, wr = wload([(0, [[512, 11], [1, 512]], 128, 0, wsrc(w_fo[l], f0 * 128, 11, ch * 512, 512))])
                    for b in range(nblk):
                        px, pxr = ps_ring.next()
                        xa, xr = xbl[b]
                        for kc in range(11):
                            mm(px[0:P, :], c_act[:, kc * NTK + b * 128:kc * NTK + b * 128 + P],
                               AP(wt, kc * 512, [[WSLOT, 128], [1, 512]]), kc == 0, kc == 10, (c_act_r, wr), (pxr,))
                        tt("dve", xa[:, ch * 512:(ch + 1) * 512], xa[:, ch * 512:(ch + 1) * 512], px[0:P, :], ALU.add,
                           (xr, pxr), (xr,))

        PCX = {"lc": LC_P, "P": 128, "ntok": T, "nblk": 4, "hT": (hT, hT_r), "oT": (oT, oT_r), "mT": (mT, mT_r),
               "act": (actT, bacc_r), "x": [(xb[b][:], xb_r[b]) for b in range(4)]}


        SCX = {"lc": LC_S, "P": NS, "ntok": NS, "nblk": 1, "hT": (hTs, hTs_r), "oT": (oTs, oTs_r), "mT": (mTs, mTs_r),
               "act": (actTs[:], actTs_r), "x": [(xs_t[0:NS, :], xs_r_)]}

        def sample_tile(l):
            P = NS
            load_layer_consts(l, LC_S)
            xs, xs_r = SCX["x"][0]
            if l == 0:
                dma("sp", xs, x_s[:, :], xs_ds, (), (xs_r,))
            yield
            do_norm(P, 1, [(xs, xs_r)], ln1[l, :], (hTs, hTs_r))
            yield
            cos_ap = AP(tabs_s, 0, [[16, P], [0, 8], [0, 2], [1, 8]])
            sin_ap = AP(tabs_s, 8, [[16, P], [0, 8], [0, 2], [1, 8]])
            fpart = 5 * T
            for g in range(4):
                R = CROWS[g]
                dil = DILS[g]
                wt, wr = wload([(0, [[640, 8], [1, 640]], 128, 0, wsrc(w_in[l], 0, 8, 640 * g, 640))])
                pq, pqr = ps_ring.next()
                pkv, pkvr = ps_ring.next()
                for kc in range(8):
                    lh = hTs[:, kc, :]
                    mm(pq[0:P, 0:384], lh, AP(wt, kc * 640, [[WSLOT, 128], [1, 384]]), kc == 0, kc == 7, (hTs_r, wr), (pqr,))
                    mm(pkv[0:P, 0:256], lh, AP(wt, kc * 640 + 384, [[WSLOT, 128], [1, 256]]), kc == 0, kc == 7,
                       (hTs_r, wr), (pkvr,))
                qa, qr, qds = qk_ring.next()
                va, vr, vds = vst_ring.next()
                cp("act", va[0:P, :], pkv[0:P, 128:256], (pkvr,), (vr,))
                qk_norm_rope(P, qa, qr, g, cos_ap, sin_ap, LC_S,
                             (pq[0:P, 0:384].rearrange("p (h d) -> p h d", d=64), pqr),
                             (pkv[0:P, 0:128].rearrange("p (h d) -> p h d", d=64), pkvr))
                base = new_s[g][l]
                dK = AP(base.tensor, base.offset + (R - 1) * 128, [[2 * R * 128, P], [1, 128]])
                dV = AP(base.tensor, base.offset + R * 128 + (R - 1) * 128, [[2 * R * 128, P], [1, 128]])
                dma("sp", dK, qa[0:P, 6:8, :].rearrange("p h d -> p (h d)"), qds, (qr,), ())
                dma("sp", dV, va[0:P, :], vds, (vr,), ())
                dma("sp", q_scr[g, :, :], qa[0:P, 0:6, :].rearrange("p h d -> p (h d)"), qds, (qr,), (qscr_r[g],))
                pso, psor = ps_ring.next()
                psd, psdr = ps_ring.next()
                cbase = caches[g][l]
                for cb in range(8):
                    b0 = cb * 2
                    srcK = AP(cbase.tensor, cbase.offset + b0 * 2 * R * 128, [[dil * 128, 128], [2 * R * 128, 2], [1, 128]])
                    srcV = AP(cbase.tensor, cbase.offset + b0 * 2 * R * 128 + R * 128,
                              [[dil * 128, 128], [2 * R * 128, 2], [1, 128]])
                    dma("sp", Kc4.rearrange("p (b f) -> p b f", b=2), srcK, sscr_ds, (), (sscr_r,))
                    dma("sp", Vc4.rearrange("p (b f) -> p b f", b=2), srcV, sscr_ds, (), (sscr_r,))
                    dma("sp", sscr[:, 512:1280], AP(q_scr.tensor, q_scr[g, b0].offset, [[0, 128], [1, 768]]), sscr_ds,
                        (qscr_r[g],), (sscr_r,))
                    kin = AP(sscr, 0, [[2048, 128], [64, 4], [0, 3], [1, 64]])
                    qin = AP(sscr, 512, [[2048, 128], [192, 4], [64, 3], [1, 64]])
                    pout = AP(sscr, 1280, [[2048, 128], [192, 4], [64, 3], [1, 64]])
                    tt("dve", pout, kin, qin, ALU.mult, (sscr_r,), (sscr_r,))
                    s4, s4r = s4_ring.next()
                    red(s4[:, 0:12], AP(sscr, 1280, [[2048, 128], [64, 12], [1, 64]]), (sscr_r,), (s4r,))
                    act(s4[:, 0:12], s4[:, 0:12], AF.Exp, (s4r,), (s4r,), scale=0.125)
                    for bb in range(2):
                        for kv in range(2):
                            col = ((b0 + bb) * 2 + kv) * 3
                            mm(pso[0:64, col:col + 3], Vc4[:, bb * 128 + kv * 64:bb * 128 + kv * 64 + 64],
                               s4[:, (bb * 2 + kv) * 3:(bb * 2 + kv) * 3 + 3], True, True, (sscr_r, s4r), (psor,))
                    mm(psd[0:64, b0 * 6:(b0 + 2) * 6], onesf[:, 0:64], s4[:, 0:12], True, True, (const_r, s4r), (psdr,))

                if g <= 1:
                    cp("dve", nacc[0:64, 0, :], pso[0:64, 0:96], (psor,), (nacc_r,))
                    cp("dve", nacc[0:64, 1, :], psd[0:64, 0:96], (psdr,), (nacc_r,))
                else:
                    tt("dve", nacc[0:64, 0, :], nacc[0:64, 0, :], pso[0:64, 0:96], ALU.add, (psor, nacc_r), (nacc_r,))
                    tt("dve", nacc[0:64, 1, :], nacc[0:64, 1, :], psd[0:64, 0:96], ALU.add, (psdr, nacc_r), (nacc_r,))
                pst, pstr = ps_ring.next()
                for h in range(8):
                    tr(pst[0:64, h * 16:h * 16 + P], qa[0:P, h, :], identf[0:P, 0:P], (qr, const_r), (pstr,))
                for kv in range(2):
                    tr(pst[0:64, (8 + kv) * 16:(8 + kv) * 16 + P], va[0:P, kv * 64:(kv + 1) * 64], identf[0:P, 0:P],
                       (vr, const_r), (pstr,))
                cp("act", selfT[0:64, :], pst[0:64, 0:160], (pstr,), (self_r,))
                f1, fr1 = f_ring.next()
                o1 = AP(f1.tensor, f1.offset, [[fpart, 64], [6, 16], [3, 2], [1, 3]])
                tt("dve", o1, AP(selfT, 0, [[160, 64], [1, 16], [48, 2], [16, 3]]),
                   AP(selfT, 96, [[160, 64], [1, 16], [16, 2], [0, 3]]), ALU.mult, (self_r,), (fr1,))
                pss, pssr = ps_ring.next()
                mm(pss[0:64, 0:96], onesf[0:64, 0:64], f1[0:64, 0:96], True, True, (const_r, fr1), (pssr,))
                f2, fr2 = f_ring.next()
                act(f2[0:64, 0:96], pss[0:64, 0:96], AF.Exp, (pssr,), (fr2,), scale=0.125)
                f3, fr3 = f_ring.next()
                o3 = AP(f3.tensor, f3.offset, [[fpart, 64], [6, 16], [3, 2], [1, 3]])
                i2 = AP(f2.tensor, f2.offset, [[fpart, 64], [6, 16], [3, 2], [1, 3]])
                tt("dve", o3, i2, AP(selfT, 128, [[160, 64], [1, 16], [16, 2], [0, 3]]), ALU.mult, (fr2, self_r), (fr3,))
                tt("dve", nacc[0:64, 0, :], nacc[0:64, 0, :], f3[0:64, 0:96], ALU.add, (nacc_r, fr3), (nacc_r,))
                tt("dve", nacc[0:64, 1, :], nacc[0:64, 1, :], f2[0:64, 0:96], ALU.add, (nacc_r, fr2), (nacc_r,))
                if g == 0:
                    dn = AP(nacc, 96, [[192, 64], [6, 16], [1, 6]])
                    tt("dve", dn, dn, AP(es_t_s, 0, [[6, 64], [0, 16], [1, 6]]), ALU.add, (nacc_r, LC_S["r"]), (nacc_r,))
                if g == 0 or g == 3:
                    n = 0 if g == 0 else 1
                    f4, fr4 = f_ring.next()
                    recip(f4[0:64, 0:96], nacc[0:64, 1, :], (nacc_r,), (fr4,))
                    f5, fr5 = f_ring.next()
                    tt("dve", f5[0:64, 0:96], nacc[0:64, 0, :], f4[0:64, 0:96], ALU.mult, (nacc_r, fr4), (fr5,))
                    for kv in range(2):
                        cp("dve", AP(oTs[n], 64 * kv * 48, [[48, 64], [1, 16], [16, 3]]),
                           AP(f5.tensor, f5.offset + kv * 3, [[fpart, 64], [6, 16], [1, 3]]), (fr5,), (oTs_r[n],))
                yield

            yield
            st_t = sscr[0:P, 1152:1920]
            cc_bc = sscr[0:P, 0:1152]
            zcn = zv_t[0:P, 384:768]
            vdt = zv_t[0:P, 0:384]
            dma("sp", st_t, state_c[l].rearrange("b j f -> b (j f)"), sscr_ds, (), (sscr_r,))
            dma("sp", cc_bc, AP(conv_c.tensor, conv_c[l].offset, [[0, P], [1, 1152]]), sscr_ds, (), (sscr_r,))
            pcs = []
            for j in range(3):
                wt, wr = wload([(0, [[384, 8], [1, 384]], 128, 0, wsrc(w_in[l], 0, 8, 2560 + 384 * j, 384))])
                pc, pcr = ps_ring.next()
                for kc in range(8):
                    mm(pc[0:P, 0:384], hTs[:, kc, :], AP(wt, kc * 384, [[WSLOT, 128], [1, 384]]), kc == 0, kc == 7,
                       (hTs_r, wr), (pcr,))
                pcs.append((pc, pcr))
            f1, fr1 = f_ring.next()
            cp("act", f1[0:P, 0:384], pcs[1][0][0:P, 0:384], (pcs[1][1],), (fr1,))
            tt("dve", zcn, f1[0:P, 0:384], pcs[2][0][0:P, 0:384], ALU.mult, (fr1, pcs[2][1]), (zv_r,))
            f2, fr2 = f_ring.next()
            f3, fr3 = f_ring.next()
            tt("dve", f2[0:P, 0:384], st_t[:, 0:384], cc_bc[:, 0:384], ALU.mult, (sscr_r,), (fr2,))
            tt("dve", f3[0:P, 0:384], st_t[:, 384:768], cc_bc[:, 384:768], ALU.mult, (sscr_r,), (fr3,))
            tt("dve", f2[0:P, 0:384], f2[0:P, 0:384], f3[0:P, 0:384], ALU.add, (fr2, fr3), (fr2,))
            tt("dve", f3[0:P, 0:384], zcn, cc_bc[:, 768:1152], ALU.mult, (zv_r, sscr_r), (fr3,))
            tt("dve", f2[0:P, 0:384], f2[0:P, 0:384], f3[0:P, 0:384], ALU.add, (fr2, fr3), (fr2,))
            ob_, obr = obf_ring.next()
            tt("dve", ob_[0:P, :], pcs[0][0][0:P, 0:384], f2[0:P, 0:384], ALU.mult, (pcs[0][1], fr2), (obr,))
            tp, tpr = tp_ring.next()
            for c in range(3):
                tr(tp[:, c * 128:c * 128 + P], ob_[0:P, c * 128:(c + 1) * 128], ident[0:P, 0:P], (obr, const_r), (tpr,))
            cp("act", oTs[2][:, :, :], AP(tp.tensor, tp.offset, [list(tp.ap[0]), [128, 3], [1, P]]), (tpr,), (oTs_r[2],))
            dma("sp", new_c_s[l, :, 0, :], st_t[:, 384:768], sscr_ds, (sscr_r,), ())
            dma("sp", new_c_s[l, :, 1, :], zcn, zv_ds, (zv_r,), ())

            yield
            dma("sp", sm6[0:P, 0, :], AP(ws_d.tensor, ws_d[l].offset, [[0, P], [16384, 6]]), sm_ds, (), (sm_r,), nonc=True)
            dma("sp", sm6[0:P, 1, :], AP(bs_d.tensor, bs_d[l].offset, [[0, P], [128, 6]]), sm_ds, (), (sm_r,), nonc=True)
            wtu, wru = wload([(0, [[384, 8], [1, 384]], 128, 0, wsrc(w_in[l], 0, 8, 3712, 384))])
            wtv, wrv = wload([(0, [[384, 8], [1, 384]], 128, 0, wsrc(w_in[l], 0, 8, 4096, 384))])
            pu, pur = ps_ring.next()
            pv, pvr = ps_ring.next()
            for kc in range(8):
                mm(pu[0:P, 0:384], hTs[:, kc, :], AP(wtu, kc * 384, [[WSLOT, 128], [1, 384]]), kc == 0, kc == 7, (hTs_r, wru), (pur,))
            for kc in range(8):
                mm(pv[0:P, 0:384], hTs[:, kc, :], AP(wtv, kc * 384, [[WSLOT, 128], [1, 384]]), kc == 0, kc == 7, (hTs_r, wrv), (pvr,))
            cp("act", vdt, pv[0:P, 0:384], (pvr,), (zv_r,))
            dma("sp", new_d_s[l, :, :], vdt, zv_ds, (zv_r,), ())
            f1, fr1 = f_ring.next()
            f1v = f1[0:P, 0:384].rearrange("p (g e) -> p g e", g=6)
            tt("dve", f1v, vdt.rearrange("p (g e) -> p g e", g=6), AP(sm6, 0, [[12, P], [1, 6], [0, 64]]), ALU.mult,
               (zv_r, sm_r), (fr1,))
            tt("dve", f1v, f1v, AP(sm6, 6, [[12, P], [1, 6], [0, 64]]), ALU.add, (fr1, sm_r), (fr1,))
            ob_, obr = obf_ring.next()
            tt("dve", ob_[0:P, :], pu[0:P, 0:384], f1[0:P, 0:384], ALU.mult, (pur, fr1), (obr,))
            tp, tpr = tp_ring.next()
            for c in range(3):
                tr(tp[:, c * 128:c * 128 + P], ob_[0:P, c * 128:(c + 1) * 128], ident[0:P, 0:P], (obr, const_r), (tpr,))
            cp("act", oTs[3][:, :, :], AP(tp.tensor, tp.offset, [list(tp.ap[0]), [128, 3], [1, P]]), (tpr,), (oTs_r[3],))

            yield
            yield from dense_tail(SCX, l)
            if l == depth - 1:
                dma("sp", y_s[:, :], xs, xs_ds, (xs_r,), ())

        xloaded = set()

        def x_load(s, l, t, b):
            if (s, l, t, b) in xloaded:
                return
            xloaded.add((s, l, t, b))
            src = x_p if l == 0 else y_p
            blk = t * 4 + b
            rd = ()
            if l > 0:
                rd = (dram_y_r[(s, blk)],)
            dma("sp", xb[b][:], src[s, blk * 128:(blk + 1) * 128, :], xb_ds[b], rd, (xb_r[b],))

        def next_tile(s, l, t):
            if t + 1 < ntiles:
                return (s, l, t + 1)
            if l + 1 < depth:
                return (s, l + 1, 0)
            if s + 1 < nseq:
                return (s + 1, 0, 0)
            return None

        def prompt_tile(s, l, t):
            src = x_p if l == 0 else y_p
            if t == 0:
                load_layer_consts(l, LC_P)
            for b in range(4):
                x_load(s, l, t, b)

            def _done(b, s=s, l=l, t=t):
                blk = t * 4 + b
                r = dram_y_r.setdefault((s, blk), Res(f"y{s}_{blk}"))
                dma("sp", y_p[s, blk * 128:(blk + 1) * 128, :], xb[b][:], xb_ds[b], (xb_r[b],), (r,))
                nxt = next_tile(s, l, t)
                if nxt is not None:
                    x_load(nxt[0], nxt[1], nxt[2], b)
            PCX["done"] = _done if phases >= 8 else None
            do_norm(128, 4, [(xb[b][:], xb_r[b]) for b in range(4)], ln1[l, :], (hT, hT_r), gkey=("ln1", l))

            def _after_norm2(s=s, l=l, t=t):
                nxt = next_tile(s, l, t)
                if nxt is None or sgen[0] is not None:
                    return
                dma("sp", g_bc[:], bcast_rows(ln1[nxt[1], :], D), gds, (), (g_bc_r,))
                gbc_pre[0] = ("ln1", nxt[1])
            PCX["after_norm2"] = _after_norm2

            gw = {}
            stt_ = {}

            def stageA(g, b):
                order = (0, 0, 1, 1)[g]
                nkt = 8 if g < 3 else 16
                if b == 0:
                    gw[g] = wload([(0, [[640, 8], [1, 640]], 128, 0, wsrc(w_in[l], 0, 8, 640 * g, 640))])
                wt, wr = gw[g]
                win_rows = CROWS[g]
                bid = t * 4 + b
                slot = bid % nkt
                c0, cdims = colpat(order, b)
                pq, pqr = ps_ring.next()
                pkv, pkvr = ps_ring.next()
                for kc in range(8):
                    lh = AP(hT, kc * T + c0, [[8 * T, 128]] + cdims)
                    mm(pq[:, 0:384], lh, AP(wt, kc * 640, [[WSLOT, 128], [1, 384]]), kc == 0, kc == 7,
                       (hT_r, wr), (pqr,))
                    mm(pkv[:, 0:256], lh, AP(wt, kc * 640 + 384, [[WSLOT, 128], [1, 256]]), kc == 0, kc == 7,
                       (hT_r, wr), (pkvr,))
                qa, qr, qds = qk_ring.next()
                vdst = AP(Vt[g], slot * 192, [[nkt * 192, 128], [128, 2], [1, 64]])
                first_row = SEQ - win_rows
                if order == 0:
                    need_out = bid * 128 >= first_row
                else:
                    need_out = (t * T + T) > first_row
                if need_out:
                    va, vr, vds = vst_ring.next()

                def _mid():
                    cp("act", vdst, pkv[:, 128:256].rearrange("p (k d) -> p k d", d=64), (pkvr,), (Vt_r[g][slot],))
                    if need_out:
                        cp("act", va[:], pkv[:, 128:256], (pkvr,), (vr,))
                cos_ap, sin_ap = tab_aps(128, order, bid)
                qk_norm_rope(128, qa, qr, g, cos_ap, sin_ap, LC_P,
                             (pq[:, 0:384].rearrange("p (h d) -> p h d", d=64), pqr),
                             (pkv[:, 0:128].rearrange("p (h d) -> p h d", d=64), pkvr), mid=_mid)
                if need_out:
                    def rows_dst(kvsel):
                        base = new_p[g][l, s, kvsel]
                        if order == 0:
                            r0 = bid * 128 - first_row
                            return AP(base.tensor, base.offset + r0 * 128, [[128, 128], [1, 128]])
                        r0 = t * T + b - first_row
                        return AP(base.tensor, base.offset + r0 * 128, [[512, 128], [1, 128]])
                    dma("sp", rows_dst(0), qa[:, 6:8, :].rearrange("p h d -> p (h d)"), qds, (qr,), ())
                    dma("sp", rows_dst(1), va[:, :], vds, (vr,), ())
                qb, kb, qbr = qbf_ring.next()
                cp("dve", qb.rearrange("p g (k d) -> p g k d", k=2),
                   qa[:, 0:6, :].rearrange("p (k g) d -> p g k d", k=2), (qr,), (qbr,))
                cp("dve", kb, qa[:, 6:8, :].rearrange("p h d -> p (h d)"), (qr,), (qbr,))
                stt_[(g, b)] = (qb, kb, qbr)

            def stageB(g, b):
                nkt = 8 if g < 3 else 16
                slot = (t * 4 + b) % nkt
                qb, kb, qbr = stt_[(g, b)]
                tp, tpr = tp_ring.next()
                for gg in range(3):
                    tr(tp[:, gg * 128:(gg + 1) * 128], qb[:, gg, :], ident[:], (qbr, const_r), (tpr,))
                tr(tp[:, 384:512], kb, ident[:], (qbr, const_r), (tpr,))
                cp("act", QT[:, b, :], tp[:, 0:384], (tpr,), (QT_r[b],))
                cp("act", KT[g][:, slot * 128:(slot + 1) * 128], tp[:, 384:512], (tpr,), (KT_r[g][slot],))

            cst_ = {}

            def c_keyblocks(g, b):
                order = (0, 0, 1, 1)[g]
                nkt = 8 if g < 3 else 16
                bid = t * 4 + b
                slot = bid % nkt
                if g < 3:
                    prev = bid - (1 if order == 0 else 4)
                    return ([(prev % nkt, 0)] if prev >= 0 else []) + [(slot, 1)]
                return [((tt_ * 4 + b), 2) for tt_ in range(t)] + [(slot, 3)]

            def stageC1(g, b, kvs=(0, 1)):
                kbs = c_keyblocks(g, b)
                for kv in kvs:
                    pss = []
                    for ki, (ks, mi) in enumerate(kbs):
                        pS, pSr = ps_ring.next()
                        mm(pS[:, 0:384], KT[g][64 * kv:64 * kv + 64, ks * 128:(ks + 1) * 128],
                           QT[64 * kv:64 * kv + 64, b, :], True, True, (KT_r[g][ks], QT_r[b]), (pSr,))
                        pss.append((pS, pSr, ks, mi))
                    pts = []
                    for (pS, pSr, ks, mi) in pss:
                        pt_, ptr = pT_ring.next()
                        act(pt_, pS[:, 0:384], AF.Exp, (pSr,), (ptr,), scale=0.125)
                        pts.append((pt_, ptr, ks, mi))
                    for (pt_, ptr, ks, mi) in pts:
                        tt(EW2, pt_, pt_, masks[:, mi, :], ALU.mult, (ptr, const_r), (ptr,))
                    cst_[(g, b, kv)] = pts

            def stageC2(g, b, kvs=(0, 1)):
                order = (0, 0, 1, 1)[g]
                pos = []
                for kv in kvs:
                    pts = cst_.pop((g, b, kv))
                    po, por = ps_ring.next()
                    nkb = len(pts)
                    for ki, (pt_, ptr, ks, mi) in enumerate(pts):
                        last = (ki == nkb - 1) and g != 0
                        mm(po[:, 0:384], Vt[g][:, ks, 64 * kv:64 * kv + 128], pt_, ki == 0, last,
                           (Vt_r[g][ks], ptr), (por,))
                    if g == 0:
                        mm(po[:, 0:384], sel[0:1, kv, :], es_rows[0:1, kv].rearrange("o g q -> o (g q)"),
                           False, True, (lay_r, const_r), (por,))
                    pos.append((kv, po, por))
                oc0, ocd = colpat(order, b)
                if g == 0:
                    fs = []
                    for (kv, po, por) in pos:
                        nlo, dlo = (0, 64) if kv == 0 else (64, 0)
                        f1, fr1 = f_ring.next()
                        recip(f1[nlo:nlo + 64, 0:384], po[dlo:dlo + 64, 0:384], (por,), (fr1,))
                        fs.append((f1, fr1))
                    for (kv, po, por), (f1, fr1) in zip(pos, fs):
                        nlo, dlo = (0, 64) if kv == 0 else (64, 0)
                        odst = AP(oT[0], nlo * 3 * T + oc0, [[3 * T, 64], [T, 3]] + ocd)
                        tt("dve", odst, po[nlo:nlo + 64, 0:384].rearrange("p (g q) -> p g q", g=3),
                           f1[nlo:nlo + 64, 0:384].rearrange("p (g q) -> p g q", g=3), ALU.mult,
                           (por, fr1), (oT_r[0],))
                else:
                    for (kv, po, por) in pos:
                        adst = AP(bacc, kv * 3 * T + oc0, [[6 * T, 128], [T, 3]] + ocd)
                        pa_ = po[:]
                        psrc = AP(pa_.tensor, pa_.offset, [list(pa_.ap[0]), [128, 3]] + [[1, 128]])
                        if g == 1:
                            cp("act", adst, psrc, (por,), (bacc_r,))
                        else:
                            tt("dve", adst, adst, psrc, ALU.add, (por, bacc_r), (bacc_r,))
                if g == 3 and b == 3 and kvs[-1] == 1:
                    for kv in range(2):
                        nlo, dlo = (0, 64) if kv == 0 else (64, 0)
                        for gg in range(3):
                            f1, fr1 = f_ring.next()
                            off = (kv * 3 + gg) * T
                            recip(f1[nlo:nlo + 64, :], bacc[dlo:dlo + 64, off:off + T], (bacc_r,), (fr1,))
                            tt("dve", oT[1][nlo:nlo + 64, gg, :], bacc[nlo:nlo + 64, off:off + T], f1[nlo:nlo + 64, :],
                               ALU.mult, (bacc_r, fr1), (oT_r[1],))

            def big(g, b):
                return len(c_keyblocks(g, b)) > 2

            items = [(g, b) for g in range(4 if phases >= 2 else 0) for b in range(4)]
            ni = len(items)
            if tl_count[0] >= COPY_START_TL or nseq * depth * ntiles <= COPY_START_TL:
                issue_copies(3)
            tl_count[0] += 1
            hook()
            for i in range(ni + 4):
                if i < ni:
                    stageA(*items[i])
                if 0 <= i - 2 < ni:
                    stageB(*items[i - 2])
                if 0 <= i - 4 < ni and not big(*items[i - 4]):
                    stageC2(*items[i - 4])
                if 0 <= i - 3 < ni:
                    it = items[i - 3]
                    if big(*it):
                        for kv in range(2):
                            stageC1(*it, kvs=(kv,))
                            stageC2(*it, kvs=(kv,))
                    else:
                        stageC1(*it)
                if i % 4 == 3:
                    hook()

            if t == 0:
                for c in range(3):
                    memset("dve", zc[:, c, 0:2], 0.0, (zc_r[c],))
            for c in range(3 if phases >= 3 else 0):
                parts = []
                for j in range(3):
                    parts.append((j * 128, [[384, 8], [1, 128]], 128, 0,
                                  wsrc(w_in[l], 0, 8, 2560 + 384 * j + 128 * c, 128)))
                wt, wr = wload(parts)
                pz = [ps_ring.next() for _ in range(3)]
                for j in range(3):
                    for kc in range(8):
                        mm(pz[j][0][:, :], AP(wt, kc * 384 + j * 128, [[WSLOT, 128], [1, 128]]), hT[:, kc, :],
                           kc == 0, kc == 7, (hT_r, wr), (pz[j][1],))
                f1, fr1 = f_ring.next()
                cp("act", f1[:, :], pz[1][0][:, :], (pz[1][1],), (fr1,))
                tt("dve", zc[:, c, 2:T + 2], f1[:, :], pz[2][0][:, :], ALU.mult, (fr1, pz[2][1]), (zc_r[c],))
                f2, fr2 = f_ring.next()
                cb = 32 + c
                act(f2[:, :], zc[:, c, 2:T + 2], AF.Copy, (zc_r[c], lay_r), (fr2,),
                    scale=lcol[:, 32 + 2 * 3 + c:32 + 2 * 3 + c + 1])
                stt("dve", f2[:, :], zc[:, c, 1:T + 1], lcol[:, 32 + 3 + c:32 + 3 + c + 1], f2[:, :], ALU.mult, ALU.add,
                    (zc_r[c], lay_r, fr2), (fr2,))
                stt("dve", f2[:, :], zc[:, c, 0:T], lcol[:, 32 + c:32 + c + 1], f2[:, :], ALU.mult, ALU.add,
                    (zc_r[c], lay_r, fr2), (fr2,))
                tt("dve", oT[2][:, c, :], pz[0][0][:, :], f2[:, :], ALU.mult, (pz[0][1], fr2), (oT_r[2],))
                if t == NT - 1:
                    dstc = AP(new_c_p.tensor, new_c_p[l, s].offset + c * 128, [[1, 128], [384, 2]])
                    dma("sp", dstc, zc[:, c, T:T + 2], zc_ds[c], (zc_r[c],), (), nonc=True)
                else:
                    cp("act", zc[:, c, 0:2], zc[:, c, T:T + 2], (zc_r[c],), (zc_r[c],))

            if phases >= 4:
                wtu, wru = wload([(0, [[384, 8], [1, 384]], 128, 0, wsrc(w_in[l], 0, 8, 3712, 384))])
                wtv, wrv = wload([(0, [[384, 8], [1, 384]], 128, 0, wsrc(w_in[l], 0, 8, 4096, 384))])
            dst_ = {}

            def dA(b):
                pu, pur = ps_ring.next()
                pv, pvr = ps_ring.next()
                for kc in range(8):
                    lh = hT[:, kc, b * 128:(b + 1) * 128]
                    mm(pv[:, 0:384], lh, AP(wtv, kc * 384, [[WSLOT, 128], [1, 384]]), kc == 0, kc == 7, (hT_r, wrv), (pvr,))
                for kc in range(8):
                    lh = hT[:, kc, b * 128:(b + 1) * 128]
                    mm(pu[:, 0:384], lh, AP(wtu, kc * 384, [[WSLOT, 128], [1, 384]]), kc == 0, kc == 7, (hT_r, wru), (pur,))
                vb_, vbr = vbf_ring.next()
                cp("act", vb_, pv[:, 0:384], (pvr,), (vbr,))
                dst_[b] = [pu, pur, vb_, vbr]

            def dB(b):
                pu, pur, vb_, vbr = dst_[b]
                psv, psvr = ps_ring.next()
                for gg in range(6):
                    mm(psv[:, gg * 64:(gg + 1) * 64], wmT[:, gg, :], vb_[:, gg * 64:(gg + 1) * 64], True, True,
                       (lay_r, vbr), (psvr,))
                f1, fr1 = f_ring.next()
                bsb = AP(lcol, 41, [[48, 128], [1, 6], [0, 64]])
                tt("dve", f1[:, 0:384].rearrange("p (g e) -> p g e", g=6), psv[:, 0:384].rearrange("p (g e) -> p g e", g=6),
                   bsb, ALU.add, (psvr, lay_r), (fr1,))
                ob_, obr = obf_ring.next()
                tt("dve", ob_, pu[:, 0:384], f1[:, 0:384], ALU.mult, (pur, fr1), (obr,))
                dst_[b] = [ob_, obr]

            def dC(b):
                ob_, obr = dst_[b]
                tp, tpr = tp_ring.next()
                for c in range(3):
                    tr(tp[:, c * 128:(c + 1) * 128], ob_[:, c * 128:(c + 1) * 128], ident[:], (obr, const_r), (tpr,))
                cp("act", oT[3][:, :, b * 128:(b + 1) * 128], tp[:, 0:384].rearrange("p (c q) -> p c q", c=3),
                   (tpr,), (oT_r[3],))

            nb_ = 4 if phases >= 4 else 0
            for i in range(nb_ + 2 if nb_ else 0):
                if i < nb_:
                    dA(i)
                if 0 <= i - 1 < nb_:
                    dB(i - 1)
                if 0 <= i - 2 < nb_:
                    dC(i - 2)

            for _ in dense_tail(PCX, l):
                hook()

            if PCX["done"] is None:
                for b in range(4):
                    _done(b)

        copy_jobs = []
        if do_copy and ns > 0:
            cpd = S.new_dsem("cachecopy")
            for g in (3, 2, 1, 0):
                R = CROWS[g]
                for l in range(depth):
                    for b0 in range(0, ns, 4):
                        copy_jobs.append((new_s[g][l, b0:b0 + 4, :, 0:R - 1, :], caches[g][l, b0:b0 + 4, :, 1:R, :]))

        def issue_copies(n):
            for _ in range(n):
                if copy_jobs:
                    o_, i_ = copy_jobs.pop(0)
                    dma("act", o_, i_, cpd, (), ())

        tl_count = [0]

        def _sample_all():
            for l_ in range(depth):
                yield from sample_tile(l_)
        sgen = [_sample_all() if ns > 0 else None]

        def hook():
            if sgen[0] is not None:
                try:
                    next(sgen[0])
                except StopIteration:
                    sgen[0] = None
        if nseq == 0 or ntiles == 0:
            while sgen[0] is not None:
                hook()
        for s in range(nseq):
            for l in range(depth):
                for t in range(ntiles):
                    prompt_tile(s, l, t)

        while sgen[0] is not None:
            hook()
        issue_copies(len(copy_jobs))
        hoist_wloads(NWS - 1)
        with nc.Block() as block:
            S.emit(block)
    return nc


_W_NAMES = ["ln1", "w_in", "q_norm_a", "k_norm_a", "sink_a", "q_norm_b", "k_norm_b", "conv_c", "ws_d", "bs_d",
            "w_gate", "b_gate", "w_branch", "w_o", "ln2", "w_ffn_in", "w_ffn_out"]


def make_in_maps(inputs, ncores=8, nseq=NSEQ, ns=NS):
    consts = make_consts()
    f = lambda a: np.ascontiguousarray(np.asarray(a, dtype=np.float32))
    maps = []
    for c in range(ncores):
        m = dict(consts)
        m["x_prompt"] = f(inputs["x_prompt"][c * nseq:(c + 1) * nseq])
        m["x_sample"] = f(inputs["x_sample"][c * ns:(c + 1) * ns, 0, :])
        for k in ("cache_a", "cache_b1", "cache_b2", "cache_b3"):
            a = np.asarray(inputs[k])[:, c * ns:(c + 1) * ns]
            m[k] = f(a.reshape(a.shape[0], ns, 2, a.shape[3], 128))
        m["state_c"] = f(np.asarray(inputs["state_c"])[:, c * ns:(c + 1) * ns])
        for k in _W_NAMES:
            m[k] = f(inputs[k])
        maps.append(m)
    return maps


def kernel(**inputs):
    nc = build()
    maps = make_in_maps(inputs)
    res = run_bass_kernel_spmd(nc, maps, core_ids=list(range(8)))
    R = res.results
    cat = lambda k, ax: np.concatenate([np.asarray(r[k]) for r in R], axis=ax)
    outs = []
    outs.append(cat("y_prompt", 0))
    outs.append(cat("y_sample", 0).reshape(8 * NS, 1, D))
    for nm, rows in (("a", 128), ("b1", 128), ("b2", 512), ("b3", 2048)):
        p = cat(f"new_{nm}_prompt", 1)
        outs.append(p.reshape(DEPTH, 8 * NSEQ, 2, rows, 2, 64))
        s_ = cat(f"new_{nm}_sample", 1)
        outs.append(s_.reshape(DEPTH, 8 * NS, 2, rows, 2, 64))
    outs.append(cat("new_c_prompt", 1))
    outs.append(cat("new_c_sample", 1))
    outs.append(cat("new_d_sample", 1).reshape(DEPTH, 8 * NS, 1, 384))
    return tuple(np.ascontiguousarray(o, dtype=np.float32) for o in outs)
```

```python
import math
from contextlib import ExitStack
from functools import partial

import numpy as np
import concourse.bass as bass
import concourse.mybir as mybir
from concourse.bass_utils import run_bass_kernel_spmd

F32 = mybir.dt.float32
BF16 = mybir.dt.bfloat16
AF = mybir.ActivationFunctionType
ALU = mybir.AluOpType
AX = mybir.AxisListType

D = 1024
SEQ = 2048
DEPTH = 2
NSEQ = 2
NS = 16
PAST = 16384
T = 512
NT = SEQ // T
NIN = 4480
DFF = 2816
EPS = 1e-6
WSLOT = 5632
NWS = 4
COPY_START_TL = 4
STRICT_SAME_ENG = True
EW2 = "dve"
DILS = (1, 1, 4, 16)
CROWS = (128, 128, 512, 2048)


class Res:
    __slots__ = ("name", "w", "r_eng", "r_dma")

    def __init__(self, name):
        self.name = name
        self.w = None
        self.r_eng = {}
        self.r_dma = []


class DSem:
    __slots__ = ("sem", "total")

    def __init__(self, sem):
        self.sem = sem
        self.total = 0


class Op:
    __slots__ = ("eng", "fn", "deps", "dsem", "val", "need_inc", "cnt")


class Sched:
    ENGS = ("pe", "act", "dve", "pool", "sp")

    def __init__(self, nc, es):
        self.nc = nc
        self.ops = {e: [] for e in self.ENGS}
        self.esem = {e: es.enter_context(nc.semaphore("sem_" + e)) for e in self.ENGS}
        self.dsems = []
        self.es = es
        self.nops = 0

    def new_dsem(self, name):
        d = DSem(self.es.enter_context(self.nc.semaphore("d_" + name)))
        self.dsems.append(d)
        return d

    def op(self, eng, fn, reads=(), writes=(), dsem=None):
        o = Op()
        o.eng = eng
        o.fn = fn
        o.dsem = dsem
        o.need_inc = False
        o.cnt = None
        o.val = None
        if dsem is not None:
            dsem.total += 16
            o.val = dsem.total
        raw = set()
        waw = set()
        war = set()
        deps = set()
        for r in reads:
            if r.w is not None:
                deps.add(r.w)
                raw.add(r.w)
        for w in writes:
            if w.w is not None:
                deps.add(w.w)
                waw.add(w.w)
            deps.update(w.r_eng.values())
            deps.update(w.r_dma)
            war.update(w.r_eng.values())
            war.update(w.r_dma)
        final = []
        for d in deps:
            if d is o:
                continue
            if d.dsem is None and d.eng == eng:
                if eng == "pe":
                    continue
                if d not in raw and dsem is None and not STRICT_SAME_ENG:
                    continue
            if d.dsem is not None and d.dsem is dsem and d in waw and d not in raw and d not in war:
                continue
            if d.dsem is None:
                d.need_inc = True
            final.append(d)
        o.deps = final
        for r in reads:
            if dsem is not None:
                r.r_dma.append(o)
            else:
                r.r_eng[eng] = o
        for w in writes:
            w.w = o
            w.r_eng = {}
            w.r_dma = []
        self.ops[eng].append(o)
        self.nops += 1
        return o

    def emit(self, block):
        nc = self.nc
        for eng in self.ENGS:
            c = 0
            for o in self.ops[eng]:
                if o.dsem is None and o.need_inc:
                    c += 1
                    o.cnt = c

        def run(eng, e):
            waited = {}
            for o in self.ops[eng]:
                need = {}
                for d in o.deps:
                    if d.dsem is not None:
                        sem, val = d.dsem.sem, d.val
                    else:
                        sem, val = self.esem[d.eng], d.cnt
                    k = id(sem)
                    if waited.get(k, 0) >= val:
                        continue
                    if k not in need or need[k][1] < val:
                        need[k] = (sem, val)
                ws = list(need.values())
                for k, (sem, val) in need.items():
                    waited[k] = val
                for sem, val in ws[1:]:
                    e.wait_ge(sem, val)
                inst = o.fn(e)
                if ws:
                    inst._wait_ge(ws[0][0], ws[0][1])
                if o.dsem is not None:
                    inst.then_inc(o.dsem.sem, 16)
                elif o.need_inc:
                    inst.then_inc(self.esem[eng], 1)
            if eng == "sp":
                for d in self.dsems:
                    if d.total > 0:
                        e.wait_ge(d.sem, d.total)
                for en in self.ENGS:
                    if en == "sp":
                        continue
                    last = None
                    for o in self.ops[en]:
                        if o.cnt is not None:
                            last = o.cnt
                    if last:
                        e.wait_ge(self.esem[en], last)

        @block.tensor
        def _(e):
            run("pe", e)

        @block.scalar
        def _(e):
            run("act", e)

        @block.vector
        def _(e):
            run("dve", e)

        @block.gpsimd
        def _(e):
            run("pool", e)

        @block.sync
        def _(e):
            run("sp", e)


class Ring:
    def __init__(self, items):
        self.items = items
        self.i = 0

    def next(self):
        x = self.items[self.i % len(self.items)]
        self.i += 1
        return x


def AP(t, off, dims):
    return bass.AP(t, off, [list(d) for d in dims])


def _rope_tab(pos):
    half = 8
    inv = (np.float32(500000.0) ** (-2.0 * np.arange(half, dtype=np.float32) / np.float32(16))).astype(np.float32)
    ang = (pos.astype(np.float32)[:, None] * inv[None, :]).astype(np.float32)
    return np.cos(ang).astype(np.float32), np.sin(ang).astype(np.float32)


def _block_positions(order, blk):
    p = np.arange(128)
    t, b = divmod(blk, 4)
    if order == 0:
        return 128 * blk + p
    if order == 1:
        return 512 * t + b + 4 * p
    return 512 * t + 4 * b + (p // 32) + 16 * (p % 32)


def make_consts():
    tabs = np.zeros((128, 3, 16, 2, 8), np.float32)
    for o in range(3):
        for blk in range(16):
            c, s = _rope_tab(_block_positions(o, blk))
            tabs[:, o, blk, 0] = c
            tabs[:, o, blk, 1] = s
    cs, ss = _rope_tab(np.full((NS,), PAST))
    tabs_s = np.zeros((NS, 2, 8), np.float32)
    tabs_s[:, 0] = cs
    tabs_s[:, 1] = ss
    s = np.arange(128)[:, None]
    q = np.arange(128)[None, :]
    m_prev = (s >= q).astype(np.float32)
    m_cur = (s <= q).astype(np.float32)
    same = ((s % 4) == (q % 4)).astype(np.float32)
    m3_prev = same
    m3_cur = same * (s <= q).astype(np.float32)
    masks = np.stack([m_prev, m_cur, m3_prev, m3_cur], 0)
    masks = np.repeat(masks[:, :, None, :], 3, axis=2)
    masks = np.ascontiguousarray(masks.transpose(1, 0, 2, 3)).reshape(128, 4 * 384)
    ident = np.eye(128, dtype=np.float32)
    return {"c_tabs": tabs.reshape(128, -1), "c_tabs_s": tabs_s.reshape(NS, -1), "c_masks": masks,
            "c_ident": ident}


def build(nseq=NSEQ, ns=NS, depth=DEPTH, do_copy=True, phases=99, ntiles=NT):
    nc = bass.Bass("TRN2", target_bir_lowering=False)
    es = ExitStack()
    with es:
        S = Sched(nc, es)

        def din(name, shape, dt=F32):
            return nc.dram_tensor(name, list(shape), dt, kind="ExternalInput").ap()

        def dout(name, shape, dt=F32):
            return nc.dram_tensor(name, list(shape), dt, kind="ExternalOutput").ap()

        nsq = max(nseq, 1)
        nsa = max(ns, 1)
        x_p = din("x_prompt", [nsq, SEQ, D])
        x_s = din("x_sample", [nsa, D])
        caches = [din("cache_a", [depth, nsa, 2, 128, 128]), din("cache_b1", [depth, nsa, 2, 128, 128]),
                  din("cache_b2", [depth, nsa, 2, 512, 128]), din("cache_b3", [depth, nsa, 2, 2048, 128])]
        state_c = din("state_c", [depth, nsa, 2, 384])
        ln1 = din("ln1", [depth, D])
        w_in = din("w_in", [depth, D, NIN])
        qn_a = din("q_norm_a", [depth, 64])
        kn_a = din("k_norm_a", [depth, 64])
        sink_a = din("sink_a", [depth, 6])
        qn_b = din("q_norm_b", [depth, 3, 64])
        kn_b = din("k_norm_b", [depth, 3, 64])
        conv_c = din("conv_c", [depth, 3, 384])
        ws_d = din("ws_d", [depth, 6, 128, 128])
        bs_d = din("bs_d", [depth, 6, 128])
        w_gate = din("w_gate", [depth, 4, D, D])
        b_gate = din("b_gate", [depth, 4, D])
        w_branch = din("w_branch", [depth, 4, 384, D])
        w_o = din("w_o", [depth, D, D])
        ln2 = din("ln2", [depth, D])
        w_fi = din("w_ffn_in", [depth, D, 2 * DFF])
        w_fo = din("w_ffn_out", [depth, DFF, D])
        c_tabs = din("c_tabs", [128, 3 * 16 * 16])
        c_tabs_s = din("c_tabs_s", [NS, 16])
        c_masks = din("c_masks", [128, 4 * 384])
        c_ident = din("c_ident", [128, 128])

        y_p = dout("y_prompt", [nsq, SEQ, D])
        y_s = dout("y_sample", [nsa, D])
        new_p = [dout("new_a_prompt", [depth, nsq, 2, 128, 128]), dout("new_b1_prompt", [depth, nsq, 2, 128, 128]),
                 dout("new_b2_prompt", [depth, nsq, 2, 512, 128]), dout("new_b3_prompt", [depth, nsq, 2, 2048, 128])]
        new_s = [dout("new_a_sample", [depth, nsa, 2, 128, 128]), dout("new_b1_sample", [depth, nsa, 2, 128, 128]),
                 dout("new_b2_sample", [depth, nsa, 2, 512, 128]), dout("new_b3_sample", [depth, nsa, 2, 2048, 128])]
        new_c_p = dout("new_c_prompt", [depth, nsq, 2, 384])
        new_c_s = dout("new_c_sample", [depth, nsa, 2, 384])
        new_d_s = dout("new_d_sample", [depth, nsa, 384])
        q_scr = nc.dram_tensor("q_scr", [4, NS, 384], F32, kind="Internal").ap()

        def sb(name, shape, dt):
            return es.enter_context(nc.sbuf_tensor(name, list(shape), dt))

        xb = [sb(f"xb{i}", [128, D], F32) for i in range(4)]
        xb_r = [Res(f"xb{i}") for i in range(4)]
        xb_ds = [S.new_dsem(f"xb{i}") for i in range(4)]
        hT = sb("hT", [128, 8, T], BF16)
        hT_r = Res("hT")
        oT = [sb(f"oT{i}", [128, 3, T], BF16) for i in range(4)]
        oT_r = [Res(f"oT{i}") for i in range(4)]
        bacc = sb("bacc", [128, 2 * 3 * T], F32)
        bacc_r = Res("bacc")
        actT = bacc[:].bitcast(BF16)
        mT = sb("mT", [128, 8, T], BF16)
        mT_r = Res("mT")
        KT = [sb(f"KT{g}", [128, (8 if g < 3 else 16) * 128], BF16) for g in range(4)]
        KT_r = [[Res(f"KT{g}_{i}") for i in range(8 if g < 3 else 16)] for g in range(4)]
        Vt = [sb(f"Vt{g}", [128, (8 if g < 3 else 16), 192], BF16) for g in range(4)]
        Vt_r = [[Res(f"Vt{g}_{i}") for i in range(8 if g < 3 else 16)] for g in range(4)]
        QT = sb("QT", [128, 4, 384], BF16)
        QT_r = [Res(f"QT{b}") for b in range(4)]
        wsl = [sb(f"wsl{i}", [128, WSLOT], BF16) for i in range(NWS)]
        wsl_r = [Res(f"wsl{i}") for i in range(NWS)]
        wsl_ds = [S.new_dsem(f"wsl{i}") for i in range(NWS)]
        g_bc = sb("g_bc", [128, D], F32)
        g_bc_r = Res("g_bc")
        gqk = sb("gqk", [128, 4, 2, 64], F32)
        tabs = sb("tabs", [128, 3, 16, 2, 8], F32)
        tabs_s = sb("tabs_s", [NS, 2, 8], F32)
        masks = sb("masks", [128, 4, 384], BF16)
        ident = sb("ident", [128, 128], BF16)
        identf = sb("identf", [128, 128], F32)
        wmT = sb("wmT", [128, 6, 128], BF16)
        wsnat = sb("wsnat", [128, 6, 128], F32)
        lrow = sb("lrow", [48, 128], F32)
        lcol = sb("lcol", [128, 48], F32)
        es_t = sb("es_t", [128, 6], F32)
        es_rows = sb("es_rows", [1, 2, 3, 128], BF16)
        sel = sb("sel", [1, 2, 128], BF16)
        epst = sb("epst", [128, 1], F32)
        lay_r = Res("layer_consts")
        const_r = Res("consts")
        xn = [sb(f"xn{i}", [128, D], BF16) for i in range(2)]
        xn_ring = Ring([(xn[i], Res(f"xn{i}")) for i in range(2)])
        junk = sb("junk", [128, D], BF16)
        junk_r = Res("junk")
        ssq = sb("ssq", [128, 8], F32)
        ssq_ring = Ring([(ssq[:, 2 * i:2 * i + 2], Res(f"ssq{i}")) for i in range(4)])
        qk = [sb(f"qk{i}", [128, 8, 64], F32) for i in range(2)]
        qk_ring = Ring([(qk[i], Res(f"qk{i}"), S.new_dsem(f"qk{i}")) for i in range(2)])
        vst = [sb(f"vst{i}", [128, 128], F32) for i in range(2)]
        vst_ring = Ring([(vst[i], Res(f"vst{i}"), S.new_dsem(f"vst{i}")) for i in range(2)])
        ss8 = sb("ss8", [128, 4, 8], F32)
        ss8_ring = Ring([(ss8[:, i, :], Res(f"ss8_{i}")) for i in range(4)])
        ropet = sb("ropet", [128, 2, 2, 8, 16], F32)
        rope_ring = Ring([(ropet[:, i], Res(f"rope{i}")) for i in range(2)])
        qbf = sb("qbf", [128, 3, 3, 128], BF16)
        kbf = sb("kbf", [128, 3, 128], BF16)
        qbf_ring = Ring([(qbf[:, i], kbf[:, i], Res(f"qbf{i}")) for i in range(3)])
        pT = sb("pT", [128, 4, 384], BF16)
        pT_ring = Ring([(pT[:, i], Res(f"pT{i}")) for i in range(4)])
        ftmp = sb("ftmp", [128, 5, T], F32)
        f_ring = Ring([(ftmp[:, i], Res(f"ftmp{i}")) for i in range(5)])
        zc = sb("zc", [128, 3, T + 2], F32)
        zc_r = [Res(f"zc{c}") for c in range(3)]
        zc_ds = [S.new_dsem(f"zc{c}") for c in range(3)]
        vbf = sb("vbf", [128, 2, 384], BF16)
        obf = sb("obf", [128, 2, 384], BF16)
        vbf_ring = Ring([(vbf[:, i], Res(f"vbf{i}")) for i in range(2)])
        obf_ring = Ring([(obf[:, i], Res(f"obf{i}")) for i in range(2)])

        hTs = sb("hTs", [128, 8, NS], BF16)
        hTs_r = Res("hTs")
        oTs = [sb(f"oTs{i}", [128, 3, NS], BF16) for i in range(4)]
        oTs_r = [Res(f"oTs{i}") for i in range(4)]
        mTs = sb("mTs", [128, 8, NS], BF16)
        mTs_r = Res("mTs")
        actTs = sb("actTs", [128, 11 * NS], BF16)
        actTs_r = Res("actTs")
        selfT = sb("selfT", [64, 10 * NS], F32)
        self_r = Res("selfT")
        nacc = sb("nacc", [64, 2, 96], F32)
        nacc_r = Res("nacc")
        s4t = sb("s4t", [128, 2, 24], F32)
        s4_ring = Ring([(s4t[:, i, :], Res(f"s4_{i}")) for i in range(2)])
        sm6 = sb("sm6", [NS, 2, 6], F32)
        sm_r = Res("sm6")
        sm_ds = S.new_dsem("sm6")
        onesf = sb("onesf", [128, 64], F32)
        kc_ds = S.new_dsem("kc4")
        qb_ds = S.new_dsem("qb4")
        qscr_r = [Res(f"qscr{g}") for g in range(4)]
        gqk_s = sb("gqk_s", [128, 4, 2, 64], F32)
        lrow_s = sb("lrow_s", [48, 128], F32)
        lcol_s = sb("lcol_s", [128, 48], F32)
        es_t_s = sb("es_t_s", [128, 6], F32)
        lds_ = S.new_dsem("layc")
        LC_P = {"gqk": gqk, "lrow": lrow, "lcol": lcol, "es_t": es_t, "r": lay_r, "ds": lds_, "full": True}
        LC_S = {"gqk": gqk_s, "lrow": lrow_s, "lcol": lcol_s, "es_t": es_t_s, "r": Res("lay_s"),
                "ds": S.new_dsem("layc_s"), "full": False}
        sscr = sb("sscr", [128, 2048], F32)
        sscr_r = Res("sscr")
        sscr_ds = S.new_dsem("sscr")
        sb_r = [Res("sscr_b0"), Res("sscr_b1")]
        sb_ds = [S.new_dsem("sscr_b0"), S.new_dsem("sscr_b1")]
        SS = (sscr_r, sb_r[0], sb_r[1])
        Kc4 = sscr[:, 0:256]
        Vc4 = sscr[:, 256:512]
        zv_t = sb("zv_t", [NS, 768], F32)
        zv_r = Res("zv_t")
        zv_ds = S.new_dsem("zv_t")
        xs_t = sb("xs_t", [NS, D], F32)
        xs_r_ = Res("xs_t")
        xs_ds = S.new_dsem("xs_t")

        psb = [es.enter_context(nc.psum_tensor(f"ps{i}", [128, 512], F32)) for i in range(8)]
        ps_ring = Ring([(psb[i], Res(f"ps{i}")) for i in range(8)])

        class _TpRing:
            def next(self):
                t_, r_ = ps_ring.next()
                return t_[:].bitcast(BF16)[:, 0:512], r_
        tp_ring = _TpRing()

        def mm(out, lhsT, rhs, start, stop, reads, writes):
            S.op("pe", lambda e: e.matmul(out, lhsT, rhs, start=start, stop=stop), reads, writes)

        def tr(out, in_, idn, reads, writes):
            S.op("pe", lambda e: e.transpose(out, in_, idn), reads, writes)

        def act(out, in_, func, reads, writes, bias=None, scale=None, accum=None):
            kw = {}
            if bias is not None:
                kw["bias"] = bias
            if scale is not None:
                kw["scale"] = scale
            if accum is not None:
                kw["accum_out"] = accum
            S.op("act", lambda e: e.activation(out=out, in_=in_, func=func, **kw), reads, writes)

        def tt(eng, out, in0, in1, op, reads, writes):
            S.op(eng, lambda e: e.tensor_tensor(out=out, in0=in0, in1=in1, op=op), reads, writes)

        def ts(eng, out, in0, s1, s2, op0, op1, reads, writes):
            if op1 is None:
                S.op(eng, lambda e: e.tensor_scalar(out=out, in0=in0, scalar1=s1, scalar2=None, op0=op0), reads, writes)
            else:
                S.op(eng, lambda e: e.tensor_scalar(out=out, in0=in0, scalar1=s1, scalar2=s2, op0=op0, op1=op1),
                     reads, writes)

        def stt(eng, out, in0, scalar, in1, op0, op1, reads, writes):
            S.op(eng, lambda e: e.scalar_tensor_tensor(out=out, in0=in0, scalar=scalar, in1=in1, op0=op0, op1=op1),
                 reads, writes)

        def cp(eng, out, in_, reads, writes):
            if eng == "act":
                act(out, in_, AF.Copy, reads, writes)
            else:
                S.op(eng, lambda e: e.tensor_copy(out=out, in_=in_), reads, writes)

        def red(out, in_, reads, writes):
            S.op("dve", lambda e: e.tensor_reduce(out=out, in_=in_, axis=AX.X, op=ALU.add), reads, writes)

        def recip(out, in_, reads, writes, use_act=True):
            if use_act:
                act(out, in_, AF.Ln, reads, writes)
                act(out, out, AF.Exp, writes, writes, scale=-1.0)
            else:
                S.op("dve", lambda e: e.reciprocal(out=out, in_=in_), reads, writes)

        def memset(eng, ap, val, writes):
            S.op(eng, lambda e: e.memset(ap, val), (), writes)

        def dma(q, out, in_, dsem, reads, writes, nonc=False):
            if nonc:
                S.op(q, lambda e: e.dma_start(out=out, in_=in_, allow_slow_non_contiguous=True), reads, writes, dsem=dsem)
            else:
                S.op(q, lambda e: e.dma_start(out=out, in_=in_), reads, writes, dsem=dsem)

        def bcast_rows(ap1d, n, parts=128):
            return AP(ap1d.tensor, ap1d.offset, [[0, parts], [1, n]])

        cds = S.new_dsem("consts")
        gds = S.new_dsem("gbc")
        dma("sp", tabs[:].rearrange("p a b c d -> p (a b c d)"), c_tabs[:, :], cds, (), (const_r,))
        dma("sp", tabs_s[:].rearrange("p c d -> p (c d)"), c_tabs_s[:, :], cds, (), (const_r,))
        dma("sp", identf[:], c_ident[:, :], cds, (), (const_r,))
        cds2 = S.new_dsem("consts2")
        const2_r = Res("consts2")
        dma("pool", masks[:].rearrange("p a b -> p (a b)"), c_masks[:, :], cds2, (), (const2_r,))
        dma("pool", ident[:], c_ident[:, :], cds2, (), (const2_r,))
        S.op("dve", lambda e: e.memset(epst[:], EPS), (const2_r,), (const_r,))
        memset("dve", sel[:], 1.0, (const_r,))
        memset("pool", onesf[:], 1.0, (const_r,))
        memset("dve", sel[0:1, 0, 0:64], 0.0, (const_r,))
        memset("dve", sel[0:1, 1, 64:128], 0.0, (const_r,))
        for g in range(4):
            memset("pool", Vt[g][:, :, 64:128], 1.0, [r for r in Vt_r[g]])
        dram_y_r = {}

        wq_state = {"i": 0}

        wl_marks = []

        def wload(parts):
            i = wq_state["i"] % NWS
            wq_state["i"] += 1
            t, r, ds = wsl[i], wsl_r[i], wsl_ds[i]
            wl_marks.append((len(S.ops["pool"]), len(parts)))
            for (off, dims, npart, p0, src) in parts:
                dst = AP(t, p0 * WSLOT + off, [[WSLOT, npart]] + dims)
                dma("pool", dst, src, ds, (), (r,))
            return t, r

        def hoist_wloads(dist):
            lst = S.ops["pool"]
            groups = {}
            skip = set()
            for k, (idx, n) in enumerate(wl_marks):
                tgt = wl_marks[k - dist][0] if k >= dist else idx
                groups.setdefault(tgt, []).extend(lst[idx:idx + n])
                skip.update(range(idx, idx + n))
            new = []
            for i, o in enumerate(lst):
                if i in groups:
                    new.extend(groups[i])
                if i not in skip:
                    new.append(o)
            assert len(new) == len(lst)
            S.ops["pool"] = new

        def wsrc(w2d, r0, nk, c0, ncol):
            rs = w2d.ap[0][0]
            return AP(w2d.tensor, w2d.offset + r0 * rs + c0, [[rs, 128], [128 * rs, nk], [1, ncol]])

        def load_layer_consts(l, lc):
            lds = lc["ds"]
            lr = lc["r"]
            W = (lr,)
            g_, lrow_, lcol_, es_ = lc["gqk"], lc["lrow"], lc["lcol"], lc["es_t"]
            dma("sp", g_[:, 0, 0, :], bcast_rows(qn_a[l, :], 64), lds, (), W)
            dma("sp", g_[:, 0, 1, :], bcast_rows(kn_a[l, :], 64), lds, (), W)
            for g in range(3):
                dma("sp", g_[:, 1 + g, 0, :], bcast_rows(qn_b[l, g, :], 64), lds, (), W)
                dma("sp", g_[:, 1 + g, 1, :], bcast_rows(kn_b[l, g, :], 64), lds, (), W)
            dma("sp", es_[:], bcast_rows(sink_a[l, :], 6), lds, (), W)
            dma("sp", lrow_[0:32, :], b_gate[l].rearrange("n (m p) -> (n m) p", p=128), lds, (), W)
            dma("sp", lrow_[32:41, :], conv_c[l].rearrange("j (c p) -> (j c) p", p=128), lds, (), W)
            dma("sp", lrow_[41:47, :], bs_d[l], lds, (), W)
            if lc["full"]:
                dma("sp", wsnat[:], ws_d[l].rearrange("g i j -> i g j"), lds, (), W)
            act(es_[:], es_[:], AF.Exp, (lr,), (lr,))
            if lc["full"]:
                cp("dve", es_rows[0:1].rearrange("o k g q -> o (k g) q"),
                   AP(es_, 0, [[6, 1], [1, 6], [0, 128]]), (lr,), (lr,))
            pt, pr = ps_ring.next()
            tr(pt[:, 0:47], lrow_[0:47, :], identf[0:47, 0:47], (lr, const_r), (pr,))
            cp("dve", lcol_[:, 0:47], pt[:, 0:47], (pr,), (lr,))
            if lc["full"]:
                for g in range(6):
                    pt, pr = ps_ring.next()
                    tr(pt[:, 0:128], wsnat[:, g, :], identf[:], (lr, const_r), (pr,))
                    tt("dve", wmT[:, g, :], pt[:, 0:128], masks[:, 1, 0:128], ALU.mult, (pr, const_r), (lr,))

        gbc_pre = [None]

        def do_norm(P, nblk, xsrc, gsrc_ap, hdst, gkey=None):
            if gkey is not None and gbc_pre[0] == gkey:
                gbc_pre[0] = None
            else:
                gbc_pre[0] = None
                dma("sp", g_bc[:], bcast_rows(gsrc_ap, D), gds, (), (g_bc_r,))
            ht, hr = hdst
            for b in range(nblk):
                xa, xr = xsrc[b]
                ssa, ssr = ssq_ring.next()
                memset("dve", ssa[0:P, :], 0.0, (ssr,))
                act(junk[0:P, :], xa, AF.Square, (xr,), (junk_r, ssr), accum=ssa[0:P, 0:1])
                act(ssa[0:P, 1:2], ssa[0:P, 0:1], AF.Ln, (ssr, const_r), (ssr,), scale=1.0 / D, bias=epst[0:P, :])
                act(ssa[0:P, 1:2], ssa[0:P, 1:2], AF.Exp, (ssr,), (ssr,), scale=-0.5)
                xt, xtr = xn_ring.next()
                stt("dve", xt[0:P, :], xa, ssa[0:P, 1:2], g_bc[0:P, :], ALU.mult, ALU.mult, (xr, ssr, g_bc_r), (xtr,))
                for hh in range(2):
                    tp, tpr = tp_ring.next()
                    for c in range(4):
                        cc = hh * 4 + c
                        tr(tp[:, c * 128:c * 128 + P], xt[0:P, cc * 128:(cc + 1) * 128], ident[0:P, 0:P],
                           (xtr, const_r), (tpr,))
                    src = AP(tp.tensor, tp.offset, [list(tp.ap[0]), [128, 4], [1, P]])
                    dst = ht[:, hh * 4:hh * 4 + 4, b * 128:b * 128 + P]
                    cp("act", dst, src, (tpr,), (hr,))

        def qk_norm_rope(P, qa, qr, g, cos_ap, sin_ap, lc, srcq, srck, mid=None):
            (qps, qpr), (kps, kpr) = srcq, srck
            s8, s8r = ss8_ring.next()
            fq, fqr = f_ring.next()
            act(fq[0:P, 0:384].rearrange("p (h d) -> p h d", d=64), qps, AF.Square, (qpr,), (fqr,))
            act(fq[0:P, 384:512].rearrange("p (h d) -> p h d", d=64), kps, AF.Square, (kpr,), (fqr,))
            red(s8[0:P, :], fq[0:P, :].rearrange("p (h d) -> p h d", d=64), (fqr,), (s8r,))
            if mid is not None:
                mid()
            act(s8[0:P, :], s8[0:P, :], AF.Ln, (s8r, const_r), (s8r,), scale=1.0 / 64, bias=epst[0:P, :])
            act(s8[0:P, :], s8[0:P, :], AF.Exp, (s8r,), (s8r,), scale=-0.5)
            s8q = AP(s8.tensor, s8.offset, [[s8.ap[0][0], P], [1, 6], [0, 64]])
            s8k = AP(s8.tensor, s8.offset + 6, [[s8.ap[0][0], P], [1, 2], [0, 64]])
            tt("dve", qa[0:P, 0:6, :], qps, s8q, ALU.mult, (qpr, s8r), (qr,))
            tt("dve", qa[0:P, 6:8, :], kps, s8k, ALU.mult, (kpr, s8r), (qr,))
            gq = AP(lc["gqk"], (g * 2 + 0) * 64, [[512, P], [0, 6], [1, 64]])
            gk = AP(lc["gqk"], (g * 2 + 1) * 64, [[512, P], [0, 2], [1, 64]])
            tt(EW2, qa[0:P, 0:6, :], qa[0:P, 0:6, :], gq, ALU.mult, (qr, lc["r"]), (qr,))
            tt(EW2, qa[0:P, 6:8, :], qa[0:P, 6:8, :], gk, ALU.mult, (qr, lc["r"]), (qr,))
            rt, rr = rope_ring.next()
            tA = rt[0:P, 0]
            tB = rt[0:P, 1]
            y16 = qa[0:P, :, 0:16].rearrange("p h (a d) -> p h a d", a=2)
            tA4 = tA.rearrange("p h (a d) -> p h a d", a=2)
            tB4 = tB.rearrange("p h (a d) -> p h a d", a=2)
            tt(EW2, tA4, y16, cos_ap, ALU.mult, (qr, const_r), (rr,))
            tt(EW2, tB4, y16, sin_ap, ALU.mult, (qr, const_r), (rr,))
            tt(EW2, qa[0:P, :, 0:8], tA[:, :, 0:8], tB[:, :, 8:16], ALU.subtract, (rr,), (qr,))
            tt(EW2, qa[0:P, :, 8:16], tA[:, :, 8:16], tB[:, :, 0:8], ALU.add, (rr,), (qr,))

        def tab_aps(P, order, blk):
            base = (order * 16 + blk) * 16
            c = AP(tabs, base, [[768, P], [0, 8], [0, 2], [1, 8]])
            s = AP(tabs, base + 8, [[768, P], [0, 8], [0, 2], [1, 8]])
            return c, s

        def colpat(order, b):
            if order == 0:
                return b * 128, [[1, 128]]
            if order == 1:
                return b, [[4, 128]]
            return 4 * b, [[1, 4], [16, 32]]


        def dense_tail(cx, l):
            P, NTK, nblk = cx["P"], cx["ntok"], cx["nblk"]
            c_hT, c_hT_r = cx["hT"]
            c_oT, c_oT_r = cx["oT"]
            c_mT, c_mT_r = cx["mT"]
            c_act, c_act_r = cx["act"]
            xbl = cx["x"]
            for m in range(8 if phases >= 5 else 0):
                parts = [(n * 128, [[512, 8], [1, 128]], 128, 0,
                          AP(w_gate.tensor, w_gate[l, n].offset + m * 128, [[D, 128], [128 * D, 8], [1, 128]]))
                         for n in range(4)]
                wbo = 4096
                for n in range(4):
                    base = w_branch[l, n]
                    if n < 2:
                        for hh in range(2):
                            parts.append((wbo + n * 384, [[128, 3], [1, 128]], 64, 64 * hh,
                                          AP(base.tensor, base.offset + hh * 192 * D + m * 128, [[D, 64], [64 * D, 3], [1, 128]])))
                    else:
                        parts.append((wbo + n * 384, [[128, 3], [1, 128]], 128, 0,
                                      AP(base.tensor, base.offset + m * 128, [[D, 128], [128 * D, 3], [1, 128]])))
                wt, wr = wload(parts)
                macc, maccr = f_ring.next()
                for n in range(4):
                    pg, pgr = ps_ring.next()
                    pp, ppr = ps_ring.next()
                    for kc in range(8):
                        mm(pg[:, 0:NTK], AP(wt, kc * 512 + n * 128, [[WSLOT, 128], [1, 128]]), c_hT[:, kc, :], kc == 0, kc == 7,
                           (c_hT_r, wr), (pgr,))
                    for c in range(3):
                        mm(pp[:, 0:NTK], AP(wt, wbo + n * 384 + c * 128, [[WSLOT, 128], [1, 128]]), c_oT[n][:, c, :], c == 0, c == 2,
                           (c_oT_r[n], wr), (ppr,))
                    sg, sgr = f_ring.next()
                    act(sg[:, 0:NTK], pg[:, 0:NTK], AF.Sigmoid, (pgr, cx["lc"]["r"]), (sgr,),
                        bias=cx["lc"]["lcol"][:, n * 8 + m:n * 8 + m + 1])
                    if n == 0:
                        tt("dve", macc[:, 0:NTK], sg[:, 0:NTK], pp[:, 0:NTK], ALU.mult, (sgr, ppr), (maccr,))
                    else:
                        tt("dve", sg[:, 0:NTK], sg[:, 0:NTK], pp[:, 0:NTK], ALU.mult, (sgr, ppr), (sgr,))
                        if n < 3:
                            tt(EW2, macc[:, 0:NTK], macc[:, 0:NTK], sg[:, 0:NTK], ALU.add, (maccr, sgr), (maccr,))
                        else:
                            tt(EW2, c_mT[:, m, :], macc[:, 0:NTK], sg[:, 0:NTK], ALU.add, (maccr, sgr), (c_mT_r,))
                if m % 2 == 1:
                    yield

            for ch in range(2 if phases >= 6 else 0):
                wt, wr = wload([(0, [[512, 8], [1, 512]], 128, 0, wsrc(w_o[l], 0, 8, ch * 512, 512))])
                for b in range(nblk):
                    px, pxr = ps_ring.next()
                    xa, xr = xbl[b]
                    for kc in range(8):
                        mm(px[0:P, :], c_mT[:, kc, b * 128:b * 128 + P], AP(wt, kc * 512, [[WSLOT, 128], [1, 512]]),
                           kc == 0, kc == 7, (c_mT_r, wr), (pxr,))
                    tt("dve", xa[:, ch * 512:(ch + 1) * 512], xa[:, ch * 512:(ch + 1) * 512], px[0:P, :], ALU.add,
                       (xr, pxr), (xr,))

            yield
            if phases >= 7:
                do_norm(P, nblk, xbl, ln2[l, :], (c_hT, c_hT_r))
                if cx.get("after_norm2") is not None:
                    cx["after_norm2"]()
            yield

            for dh in range(2 if phases >= 8 else 0):
                f0 = dh * 11
                jj = 0
                while jj < 11:
                    nj = 2 if jj + 2 <= 11 else 1
                    c0 = (f0 + jj) * 128
                    parts = [(0, [[2 * nj * 128, 8], [1, nj * 128]], 128, 0, wsrc(w_fi[l], 0, 8, c0, nj * 128)),
                             (nj * 128, [[2 * nj * 128, 8], [1, nj * 128]], 128, 0, wsrc(w_fi[l], 0, 8, DFF + c0, nj * 128))]
                    wt, wr = wload(parts)
                    for j in range(nj):
                        pg, pgr = ps_ring.next()
                        pu, pur = ps_ring.next()
                        for kc in range(8):
                            mm(pg[:, 0:NTK], AP(wt, kc * 2 * nj * 128 + j * 128, [[WSLOT, 128], [1, 128]]), c_hT[:, kc, :],
                               kc == 0, kc == 7, (c_hT_r, wr), (pgr,))
                        for kc in range(8):
                            mm(pu[:, 0:NTK], AP(wt, kc * 2 * nj * 128 + nj * 128 + j * 128, [[WSLOT, 128], [1, 128]]), c_hT[:, kc, :],
                               kc == 0, kc == 7, (c_hT_r, wr), (pur,))
                        sg, sgr = f_ring.next()
                        act(sg[:, 0:NTK], pg[:, 0:NTK], AF.Silu, (pgr,), (sgr,))
                        tt("dve", c_act[:, (jj + j) * NTK:(jj + j + 1) * NTK], sg[:, 0:NTK], pu[:, 0:NTK], ALU.mult, (sgr, pur), (c_act_r,))
                    jj += nj
                    if jj % 4 == 0:
                        yield
                yield
                if dh == 1 and cx.get("done") is not None:
                    wts = [wload([(0, [[512, 11], [1, 512]], 128, 0, wsrc(w_fo[l], f0 * 128, 11, ch * 512, 512))])
                           for ch in range(2)]
                    for b in range(nblk):
                        xa, xr = xbl[b]
                        for ch in range(2):
                            wt, wr = wts[ch]
                            px, pxr = ps_ring.next()
                            for kc in range(11):
                                mm(px[0:P, :], c_act[:, kc * NTK + b * 128:kc * NTK + b * 128 + P],
                                   AP(wt, kc * 512, [[WSLOT, 128], [1, 512]]), kc == 0, kc == 10, (c_act_r, wr), (pxr,))
                            tt("dve", xa[:, ch * 512:(ch + 1) * 512], xa[:, ch * 512:(ch + 1) * 512], px[0:P, :], ALU.add,
                               (xr, pxr), (xr,))
                        cx["done"](b)
                    continue
                for ch in range(2):
                    wt, wr = wload([(0, [[512, 11], [1, 512]], 128, 0, wsrc(w_fo[l], f0 * 128, 11, ch * 512, 512))])
                    for b in range(nblk):
                        px, pxr = ps_ring.next()
                        xa, xr = xbl[b]
                        for kc in range(11):
                            mm(px[0:P, :], c_act[:, kc * NTK + b * 128:kc * NTK + b * 128 + P],
                               AP(wt, kc * 512, [[WSLOT, 128], [1, 512]]), kc == 0, kc == 10, (c_act_r, wr), (pxr,))
                        tt("dve", xa[:, ch * 512:(ch + 1) * 512], xa[:, ch * 512:(ch + 1) * 512], px[0:P, :], ALU.add,
                           (xr, pxr), (xr,))

        PCX = {"lc": LC_P, "P": 128, "ntok": T, "nblk": 4, "hT": (hT, hT_r), "oT": (oT, oT_r), "mT": (mT, mT_r),
               "act": (actT, bacc_r), "x": [(xb[b][:], xb_r[b]) for b in range(4)]}


        SCX = {"lc": LC_S, "P": NS, "ntok": NS, "nblk": 1, "hT": (hTs, hTs_r), "oT": (oTs, oTs_r), "mT": (mTs, mTs_r),
               "act": (actTs[:], actTs_r), "x": [(xs_t[0:NS, :], xs_r_)]}

        def sample_tile(l):
            P = NS
            load_layer_consts(l, LC_S)
            xs, xs_r = SCX["x"][0]
            if l == 0:
                dma("sp", xs, x_s[:, :], xs_ds, (), (xs_r,))
            yield
            do_norm(P, 1, [(xs, xs_r)], ln1[l, :], (hTs, hTs_r))
            yield
            cos_ap = AP(tabs_s, 0, [[16, P], [0, 8], [0, 2], [1, 8]])
            sin_ap = AP(tabs_s, 8, [[16, P], [0, 8], [0, 2], [1, 8]])
            fpart = 5 * T
            for g in range(4):
                R = CROWS[g]
                dil = DILS[g]
                wt, wr = wload([(0, [[640, 8], [1, 640]], 128, 0, wsrc(w_in[l], 0, 8, 640 * g, 640))])
                pq, pqr = ps_ring.next()
                pkv, pkvr = ps_ring.next()
                for kc in range(8):
                    lh = hTs[:, kc, :]
                    mm(pq[0:P, 0:384], lh, AP(wt, kc * 640, [[WSLOT, 128], [1, 384]]), kc == 0, kc == 7, (hTs_r, wr), (pqr,))
                    mm(pkv[0:P, 0:256], lh, AP(wt, kc * 640 + 384, [[WSLOT, 128], [1, 256]]), kc == 0, kc == 7,
                       (hTs_r, wr), (pkvr,))
                qa, qr, qds = qk_ring.next()
                va, vr, vds = vst_ring.next()
                cp("act", va[0:P, :], pkv[0:P, 128:256], (pkvr,), (vr,))
                qk_norm_rope(P, qa, qr, g, cos_ap, sin_ap, LC_S,
                             (pq[0:P, 0:384].rearrange("p (h d) -> p h d", d=64), pqr),
                             (pkv[0:P, 0:128].rearrange("p (h d) -> p h d", d=64), pkvr))
                base = new_s[g][l]
                dK = AP(base.tensor, base.offset + (R - 1) * 128, [[2 * R * 128, P], [1, 128]])
                dV = AP(base.tensor, base.offset + R * 128 + (R - 1) * 128, [[2 * R * 128, P], [1, 128]])
                dma("sp", dK, qa[0:P, 6:8, :].rearrange("p h d -> p (h d)"), qds, (qr,), ())
                dma("sp", dV, va[0:P, :], vds, (vr,), ())
                dma("sp", q_scr[g, :, :], qa[0:P, 0:6, :].rearrange("p h d -> p (h d)"), qds, (qr,), (qscr_r[g],))
                pso, psor = ps_ring.next()
                psd, psdr = ps_ring.next()
                cbase = caches[g][l]
                def chunk_dma(c):
                    i = c % 2
                    base = i * 1024
                    srcK = AP(cbase.tensor, cbase.offset + c * 2 * R * 128, [[dil * 128, 128], [1, 128]])
                    srcV = AP(cbase.tensor, cbase.offset + c * 2 * R * 128 + R * 128, [[dil * 128, 128], [1, 128]])
                    dma("sp", sscr[:, base:base + 128], srcK, sb_ds[i], (), (sb_r[i],))
                    dma("sp", sscr[:, base + 128:base + 256], srcV, sb_ds[i], (), (sb_r[i],))
                    dma("sp", sscr[:, base + 256:base + 640], AP(q_scr.tensor, q_scr[g, c].offset, [[0, 128], [1, 384]]),
                        sb_ds[i], (qscr_r[g],), (sb_r[i],))

                chunk_dma(0)
                for c in range(NS):
                    if c + 1 < NS:
                        chunk_dma(c + 1)
                    i = c % 2
                    base = i * 1024
                    kin = AP(sscr, base, [[2048, 128], [64, 2], [0, 3], [1, 64]])
                    qin = AP(sscr, base + 256, [[2048, 128], [192, 2], [64, 3], [1, 64]])
                    pout = AP(sscr, base + 640, [[2048, 128], [192, 2], [64, 3], [1, 64]])
                    tt("dve", pout, kin, qin, ALU.mult, (sb_r[i],), (sb_r[i],))
                    s4, s4r = s4_ring.next()
                    red(s4[:, 0:6], AP(sscr, base + 640, [[2048, 128], [64, 6], [1, 64]]), (sb_r[i],), (s4r,))
                    act(s4[:, 0:6], s4[:, 0:6], AF.Exp, (s4r,), (s4r,), scale=0.125)
                    for kv in range(2):
                        col = (c * 2 + kv) * 3
                        mm(pso[0:64, col:col + 3], sscr[:, base + 128 + kv * 64:base + 128 + kv * 64 + 64],
                           s4[:, kv * 3:kv * 3 + 3], True, True, (sb_r[i], s4r), (psor,))
                    mm(psd[0:64, c * 6:(c + 1) * 6], onesf[:, 0:64], s4[:, 0:6], True, True, (const_r, s4r), (psdr,))

                if g <= 1:
                    cp("dve", nacc[0:64, 0, :], pso[0:64, 0:96], (psor,), (nacc_r,))
                    cp("dve", nacc[0:64, 1, :], psd[0:64, 0:96], (psdr,), (nacc_r,))
                else:
                    tt("dve", nacc[0:64, 0, :], nacc[0:64, 0, :], pso[0:64, 0:96], ALU.add, (psor, nacc_r), (nacc_r,))
                    tt("dve", nacc[0:64, 1, :], nacc[0:64, 1, :], psd[0:64, 0:96], ALU.add, (psdr, nacc_r), (nacc_r,))
                pst, pstr = ps_ring.next()
                for h in range(8):
                    tr(pst[0:64, h * 16:h * 16 + P], qa[0:P, h, :], identf[0:P, 0:P], (qr, const_r), (pstr,))
                for kv in range(2):
                    tr(pst[0:64, (8 + kv) * 16:(8 + kv) * 16 + P], va[0:P, kv * 64:(kv + 1) * 64], identf[0:P, 0:P],
                       (vr, const_r), (pstr,))
                cp("act", selfT[0:64, :], pst[0:64, 0:160], (pstr,), (self_r,))
                f1, fr1 = f_ring.next()
                o1 = AP(f1.tensor, f1.offset, [[fpart, 64], [6, 16], [3, 2], [1, 3]])
                tt("dve", o1, AP(selfT, 0, [[160, 64], [1, 16], [48, 2], [16, 3]]),
                   AP(selfT, 96, [[160, 64], [1, 16], [16, 2], [0, 3]]), ALU.mult, (self_r,), (fr1,))
                pss, pssr = ps_ring.next()
                mm(pss[0:64, 0:96], onesf[0:64, 0:64], f1[0:64, 0:96], True, True, (const_r, fr1), (pssr,))
                f2, fr2 = f_ring.next()
                act(f2[0:64, 0:96], pss[0:64, 0:96], AF.Exp, (pssr,), (fr2,), scale=0.125)
                f3, fr3 = f_ring.next()
                o3 = AP(f3.tensor, f3.offset, [[fpart, 64], [6, 16], [3, 2], [1, 3]])
                i2 = AP(f2.tensor, f2.offset, [[fpart, 64], [6, 16], [3, 2], [1, 3]])
                tt("dve", o3, i2, AP(selfT, 128, [[160, 64], [1, 16], [16, 2], [0, 3]]), ALU.mult, (fr2, self_r), (fr3,))
                tt("dve", nacc[0:64, 0, :], nacc[0:64, 0, :], f3[0:64, 0:96], ALU.add, (nacc_r, fr3), (nacc_r,))
                tt("dve", nacc[0:64, 1, :], nacc[0:64, 1, :], f2[0:64, 0:96], ALU.add, (nacc_r, fr2), (nacc_r,))
                if g == 0:
                    dn = AP(nacc, 96, [[192, 64], [6, 16], [1, 6]])
                    tt("dve", dn, dn, AP(es_t_s, 0, [[6, 64], [0, 16], [1, 6]]), ALU.add, (nacc_r, LC_S["r"]), (nacc_r,))
                if g == 0 or g == 3:
                    n = 0 if g == 0 else 1
                    f4, fr4 = f_ring.next()
                    recip(f4[0:64, 0:96], nacc[0:64, 1, :], (nacc_r,), (fr4,))
                    f5, fr5 = f_ring.next()
                    tt("dve", f5[0:64, 0:96], nacc[0:64, 0, :], f4[0:64, 0:96], ALU.mult, (nacc_r, fr4), (fr5,))
                    for kv in range(2):
                        cp("dve", AP(oTs[n], 64 * kv * 48, [[48, 64], [1, 16], [16, 3]]),
                           AP(f5.tensor, f5.offset + kv * 3, [[fpart, 64], [6, 16], [1, 3]]), (fr5,), (oTs_r[n],))
                yield

            yield
            st_t = sscr[0:P, 1152:1920]
            cc_bc = sscr[0:P, 0:1152]
            zcn = zv_t[0:P, 384:768]
            vdt = zv_t[0:P, 0:384]
            dma("sp", st_t, state_c[l].rearrange("b j f -> b (j f)"), sscr_ds, (), SS)
            dma("sp", cc_bc, AP(conv_c.tensor, conv_c[l].offset, [[0, P], [1, 1152]]), sscr_ds, (), SS)
            pcs = []
            for j in range(3):
                wt, wr = wload([(0, [[384, 8], [1, 384]], 128, 0, wsrc(w_in[l], 0, 8, 2560 + 384 * j, 384))])
                pc, pcr = ps_ring.next()
                for kc in range(8):
                    mm(pc[0:P, 0:384], hTs[:, kc, :], AP(wt, kc * 384, [[WSLOT, 128], [1, 384]]), kc == 0, kc == 7,
                       (hTs_r, wr), (pcr,))
                pcs.append((pc, pcr))
            f1, fr1 = f_ring.next()
            cp("act", f1[0:P, 0:384], pcs[1][0][0:P, 0:384], (pcs[1][1],), (fr1,))
            tt("dve", zcn, f1[0:P, 0:384], pcs[2][0][0:P, 0:384], ALU.mult, (fr1, pcs[2][1]), (zv_r,))
            f2, fr2 = f_ring.next()
            f3, fr3 = f_ring.next()
            tt("dve", f2[0:P, 0:384], st_t[:, 0:384], cc_bc[:, 0:384], ALU.mult, SS, (fr2,))
            tt("dve", f3[0:P, 0:384], st_t[:, 384:768], cc_bc[:, 384:768], ALU.mult, SS, (fr3,))
            tt("dve", f2[0:P, 0:384], f2[0:P, 0:384], f3[0:P, 0:384], ALU.add, (fr2, fr3), (fr2,))
            tt("dve", f3[0:P, 0:384], zcn, cc_bc[:, 768:1152], ALU.mult, (zv_r,) + SS, (fr3,))
            tt("dve", f2[0:P, 0:384], f2[0:P, 0:384], f3[0:P, 0:384], ALU.add, (fr2, fr3), (fr2,))
            ob_, obr = obf_ring.next()
            tt("dve", ob_[0:P, :], pcs[0][0][0:P, 0:384], f2[0:P, 0:384], ALU.mult, (pcs[0][1], fr2), (obr,))
            tp, tpr = tp_ring.next()
            for c in range(3):
                tr(tp[:, c * 128:c * 128 + P], ob_[0:P, c * 128:(c + 1) * 128], ident[0:P, 0:P], (obr, const_r), (tpr,))
            cp("act", oTs[2][:, :, :], AP(tp.tensor, tp.offset, [list(tp.ap[0]), [128, 3], [1, P]]), (tpr,), (oTs_r[2],))
            dma("sp", new_c_s[l, :, 0, :], st_t[:, 384:768], sscr_ds, SS, ())
            dma("sp", new_c_s[l, :, 1, :], zcn, zv_ds, (zv_r,), ())

            yield
            dma("sp", sm6[0:P, 0, :], AP(ws_d.tensor, ws_d[l].offset, [[0, P], [16384, 6]]), sm_ds, (), (sm_r,), nonc=True)
            dma("sp", sm6[0:P, 1, :], AP(bs_d.tensor, bs_d[l].offset, [[0, P], [128, 6]]), sm_ds, (), (sm_r,), nonc=True)
            wtu, wru = wload([(0, [[384, 8], [1, 384]], 128, 0, wsrc(w_in[l], 0, 8, 3712, 384))])
            wtv, wrv = wload([(0, [[384, 8], [1, 384]], 128, 0, wsrc(w_in[l], 0, 8, 4096, 384))])
            pu, pur = ps_ring.next()
            pv, pvr = ps_ring.next()
            for kc in range(8):
                mm(pu[0:P, 0:384], hTs[:, kc, :], AP(wtu, kc * 384, [[WSLOT, 128], [1, 384]]), kc == 0, kc == 7, (hTs_r, wru), (pur,))
            for kc in range(8):
                mm(pv[0:P, 0:384], hTs[:, kc, :], AP(wtv, kc * 384, [[WSLOT, 128], [1, 384]]), kc == 0, kc == 7, (hTs_r, wrv), (pvr,))
            cp("act", vdt, pv[0:P, 0:384], (pvr,), (zv_r,))
            dma("sp", new_d_s[l, :, :], vdt, zv_ds, (zv_r,), ())
            f1, fr1 = f_ring.next()
            f1v = f1[0:P, 0:384].rearrange("p (g e) -> p g e", g=6)
            tt("dve", f1v, vdt.rearrange("p (g e) -> p g e", g=6), AP(sm6, 0, [[12, P], [1, 6], [0, 64]]), ALU.mult,
               (zv_r, sm_r), (fr1,))
            tt("dve", f1v, f1v, AP(sm6, 6, [[12, P], [1, 6], [0, 64]]), ALU.add, (fr1, sm_r), (fr1,))
            ob_, obr = obf_ring.next()
            tt("dve", ob_[0:P, :], pu[0:P, 0:384], f1[0:P, 0:384], ALU.mult, (pur, fr1), (obr,))
            tp, tpr = tp_ring.next()
            for c in range(3):
                tr(tp[:, c * 128:c * 128 + P], ob_[0:P, c * 128:(c + 1) * 128], ident[0:P, 0:P], (obr, const_r), (tpr,))
            cp("act", oTs[3][:, :, :], AP(tp.tensor, tp.offset, [list(tp.ap[0]), [128, 3], [1, P]]), (tpr,), (oTs_r[3],))

            yield
            yield from dense_tail(SCX, l)
            if l == depth - 1:
                dma("sp", y_s[:, :], xs, xs_ds, (xs_r,), ())

        xloaded = set()

        def x_load(s, l, t, b):
            if (s, l, t, b) in xloaded:
                return
            xloaded.add((s, l, t, b))
            src = x_p if l == 0 else y_p
            blk = t * 4 + b
            rd = ()
            if l > 0:
                rd = (dram_y_r[(s, blk)],)
            dma("sp", xb[b][:], src[s, blk * 128:(blk + 1) * 128, :], xb_ds[b], rd, (xb_r[b],))

        def next_tile(s, l, t):
            if t + 1 < ntiles:
                return (s, l, t + 1)
            if l + 1 < depth:
                return (s, l + 1, 0)
            if s + 1 < nseq:
                return (s + 1, 0, 0)
            return None

        def prompt_tile(s, l, t):
            src = x_p if l == 0 else y_p
            if t == 0:
                load_layer_consts(l, LC_P)
            for b in range(4):
                x_load(s, l, t, b)

            def _done(b, s=s, l=l, t=t):
                blk = t * 4 + b
                r = dram_y_r.setdefault((s, blk), Res(f"y{s}_{blk}"))
                dma("sp", y_p[s, blk * 128:(blk + 1) * 128, :], xb[b][:], xb_ds[b], (xb_r[b],), (r,))
                nxt = next_tile(s, l, t)
                if nxt is not None:
                    x_load(nxt[0], nxt[1], nxt[2], b)
            PCX["done"] = _done if phases >= 8 else None
            do_norm(128, 4, [(xb[b][:], xb_r[b]) for b in range(4)], ln1[l, :], (hT, hT_r), gkey=("ln1", l))

            def _after_norm2(s=s, l=l, t=t):
                nxt = next_tile(s, l, t)
                if nxt is None or sgen[0] is not None:
                    return
                dma("sp", g_bc[:], bcast_rows(ln1[nxt[1], :], D), gds, (), (g_bc_r,))
                gbc_pre[0] = ("ln1", nxt[1])
            PCX["after_norm2"] = _after_norm2

            gw = {}
            stt_ = {}

            def stageA(g, b):
                order = (0, 0, 1, 1)[g]
                nkt = 8 if g < 3 else 16
                if b == 0:
                    gw[g] = wload([(0, [[640, 8], [1, 640]], 128, 0, wsrc(w_in[l], 0, 8, 640 * g, 640))])
                wt, wr = gw[g]
                win_rows = CROWS[g]
                bid = t * 4 + b
                slot = bid % nkt
                c0, cdims = colpat(order, b)
                pq, pqr = ps_ring.next()
                pkv, pkvr = ps_ring.next()
                for kc in range(8):
                    lh = AP(hT, kc * T + c0, [[8 * T, 128]] + cdims)
                    mm(pq[:, 0:384], lh, AP(wt, kc * 640, [[WSLOT, 128], [1, 384]]), kc == 0, kc == 7,
                       (hT_r, wr), (pqr,))
                    mm(pkv[:, 0:256], lh, AP(wt, kc * 640 + 384, [[WSLOT, 128], [1, 256]]), kc == 0, kc == 7,
                       (hT_r, wr), (pkvr,))
                qa, qr, qds = qk_ring.next()
                vdst = AP(Vt[g], slot * 192, [[nkt * 192, 128], [128, 2], [1, 64]])
                first_row = SEQ - win_rows
                if order == 0:
                    need_out = bid * 128 >= first_row
                else:
                    need_out = (t * T + T) > first_row
                if need_out:
                    va, vr, vds = vst_ring.next()

                def _mid():
                    cp("act", vdst, pkv[:, 128:256].rearrange("p (k d) -> p k d", d=64), (pkvr,), (Vt_r[g][slot],))
                    if need_out:
                        cp("act", va[:], pkv[:, 128:256], (pkvr,), (vr,))
                cos_ap, sin_ap = tab_aps(128, order, bid)
                qk_norm_rope(128, qa, qr, g, cos_ap, sin_ap, LC_P,
                             (pq[:, 0:384].rearrange("p (h d) -> p h d", d=64), pqr),
                             (pkv[:, 0:128].rearrange("p (h d) -> p h d", d=64), pkvr), mid=_mid)
                if need_out:
                    def rows_dst(kvsel):
                        base = new_p[g][l, s, kvsel]
                        if order == 0:
                            r0 = bid * 128 - first_row
                            return AP(base.tensor, base.offset + r0 * 128, [[128, 128], [1, 128]])
                        r0 = t * T + b - first_row
                        return AP(base.tensor, base.offset + r0 * 128, [[512, 128], [1, 128]])
                    dma("sp", rows_dst(0), qa[:, 6:8, :].rearrange("p h d -> p (h d)"), qds, (qr,), ())
                    dma("sp", rows_dst(1), va[:, :], vds, (vr,), ())
                qb, kb, qbr = qbf_ring.next()
                cp("dve", qb.rearrange("p g (k d) -> p g k d", k=2),
                   qa[:, 0:6, :].rearrange("p (k g) d -> p g k d", k=2), (qr,), (qbr,))
                cp("dve", kb, qa[:, 6:8, :].rearrange("p h d -> p (h d)"), (qr,), (qbr,))
                stt_[(g, b)] = (qb, kb, qbr)

            def stageB(g, b):
                nkt = 8 if g < 3 else 16
                slot = (t * 4 + b) % nkt
                qb, kb, qbr = stt_[(g, b)]
                tp, tpr = tp_ring.next()
                for gg in range(3):
                    tr(tp[:, gg * 128:(gg + 1) * 128], qb[:, gg, :], ident[:], (qbr, const_r), (tpr,))
                tr(tp[:, 384:512], kb, ident[:], (qbr, const_r), (tpr,))
                cp("act", QT[:, b, :], tp[:, 0:384], (tpr,), (QT_r[b],))
                cp("act", KT[g][:, slot * 128:(slot + 1) * 128], tp[:, 384:512], (tpr,), (KT_r[g][slot],))

            cst_ = {}

            def c_keyblocks(g, b):
                order = (0, 0, 1, 1)[g]
                nkt = 8 if g < 3 else 16
                bid = t * 4 + b
                slot = bid % nkt
                if g < 3:
                    prev = bid - (1 if order == 0 else 4)
                    return ([(prev % nkt, 0)] if prev >= 0 else []) + [(slot, 1)]
                return [((tt_ * 4 + b), 2) for tt_ in range(t)] + [(slot, 3)]

            def stageC1(g, b, kvs=(0, 1)):
                kbs = c_keyblocks(g, b)
                for kv in kvs:
                    pss = []
                    for ki, (ks, mi) in enumerate(kbs):
                        pS, pSr = ps_ring.next()
                        mm(pS[:, 0:384], KT[g][64 * kv:64 * kv + 64, ks * 128:(ks + 1) * 128],
                           QT[64 * kv:64 * kv + 64, b, :], True, True, (KT_r[g][ks], QT_r[b]), (pSr,))
                        pss.append((pS, pSr, ks, mi))
                    pts = []
                    for (pS, pSr, ks, mi) in pss:
                        pt_, ptr = pT_ring.next()
                        act(pt_, pS[:, 0:384], AF.Exp, (pSr,), (ptr,), scale=0.125)
                        pts.append((pt_, ptr, ks, mi))
                    for (pt_, ptr, ks, mi) in pts:
                        tt(EW2, pt_, pt_, masks[:, mi, :], ALU.mult, (ptr, const_r), (ptr,))
                    cst_[(g, b, kv)] = pts

            def stageC2(g, b, kvs=(0, 1)):
                order = (0, 0, 1, 1)[g]
                pos = []
                for kv in kvs:
                    pts = cst_.pop((g, b, kv))
                    po, por = ps_ring.next()
                    nkb = len(pts)
                    for ki, (pt_, ptr, ks, mi) in enumerate(pts):
                        last = (ki == nkb - 1) and g != 0
                        mm(po[:, 0:384], Vt[g][:, ks, 64 * kv:64 * kv + 128], pt_, ki == 0, last,
                           (Vt_r[g][ks], ptr), (por,))
                    if g == 0:
                        mm(po[:, 0:384], sel[0:1, kv, :], es_rows[0:1, kv].rearrange("o g q -> o (g q)"),
                           False, True, (lay_r, const_r), (por,))
                    pos.append((kv, po, por))
                oc0, ocd = colpat(order, b)
                if g == 0:
                    fs = []
                    for (kv, po, por) in pos:
                        nlo, dlo = (0, 64) if kv == 0 else (64, 0)
                        f1, fr1 = f_ring.next()
                        recip(f1[nlo:nlo + 64, 0:384], po[dlo:dlo + 64, 0:384], (por,), (fr1,))
                        fs.append((f1, fr1))
                    for (kv, po, por), (f1, fr1) in zip(pos, fs):
                        nlo, dlo = (0, 64) if kv == 0 else (64, 0)
                        odst = AP(oT[0], nlo * 3 * T + oc0, [[3 * T, 64], [T, 3]] + ocd)
                        tt("dve", odst, po[nlo:nlo + 64, 0:384].rearrange("p (g q) -> p g q", g=3),
                           f1[nlo:nlo + 64, 0:384].rearrange("p (g q) -> p g q", g=3), ALU.mult,
                           (por, fr1), (oT_r[0],))
                else:
                    for (kv, po, por) in pos:
                        adst = AP(bacc, kv * 3 * T + oc0, [[6 * T, 128], [T, 3]] + ocd)
                        pa_ = po[:]
                        psrc = AP(pa_.tensor, pa_.offset, [list(pa_.ap[0]), [128, 3]] + [[1, 128]])
                        if g == 1:
                            cp("act", adst, psrc, (por,), (bacc_r,))
                        else:
                            tt("dve", adst, adst, psrc, ALU.add, (por, bacc_r), (bacc_r,))
                if g == 3 and b == 3 and kvs[-1] == 1:
                    for kv in range(2):
                        nlo, dlo = (0, 64) if kv == 0 else (64, 0)
                        for gg in range(3):
                            f1, fr1 = f_ring.next()
                            off = (kv * 3 + gg) * T
                            recip(f1[nlo:nlo + 64, :], bacc[dlo:dlo + 64, off:off + T], (bacc_r,), (fr1,))
                            tt("dve", oT[1][nlo:nlo + 64, gg, :], bacc[nlo:nlo + 64, off:off + T], f1[nlo:nlo + 64, :],
                               ALU.mult, (bacc_r, fr1), (oT_r[1],))

            def big(g, b):
                return len(c_keyblocks(g, b)) > 2

            items = [(g, b) for g in range(4 if phases >= 2 else 0) for b in range(4)]
            ni = len(items)
            if tl_count[0] >= COPY_START_TL or nseq * depth * ntiles <= COPY_START_TL:
                issue_copies(3)
            tl_count[0] += 1
            hook()
            for i in range(ni + 4):
                if i < ni:
                    stageA(*items[i])
                if 0 <= i - 2 < ni:
                    stageB(*items[i - 2])
                if 0 <= i - 4 < ni and not big(*items[i - 4]):
                    stageC2(*items[i - 4])
                if 0 <= i - 3 < ni:
                    it = items[i - 3]
                    if big(*it):
                        for kv in range(2):
                            stageC1(*it, kvs=(kv,))
                            stageC2(*it, kvs=(kv,))
                    else:
                        stageC1(*it)
                if i % 4 == 3:
                    hook()

            if t == 0:
                for c in range(3):
                    memset("dve", zc[:, c, 0:2], 0.0, (zc_r[c],))
            for c in range(3 if phases >= 3 else 0):
                parts = []
                for j in range(3):
                    parts.append((j * 128, [[384, 8], [1, 128]], 128, 0,
                                  wsrc(w_in[l], 0, 8, 2560 + 384 * j + 128 * c, 128)))
                wt, wr = wload(parts)
                pz = [ps_ring.next() for _ in range(3)]
                for j in range(3):
                    for kc in range(8):
                        mm(pz[j][0][:, :], AP(wt, kc * 384 + j * 128, [[WSLOT, 128], [1, 128]]), hT[:, kc, :],
                           kc == 0, kc == 7, (hT_r, wr), (pz[j][1],))
                f1, fr1 = f_ring.next()
                cp("act", f1[:, :], pz[1][0][:, :], (pz[1][1],), (fr1,))
                tt("dve", zc[:, c, 2:T + 2], f1[:, :], pz[2][0][:, :], ALU.mult, (fr1, pz[2][1]), (zc_r[c],))
                f2, fr2 = f_ring.next()
                cb = 32 + c
                act(f2[:, :], zc[:, c, 2:T + 2], AF.Copy, (zc_r[c], lay_r), (fr2,),
                    scale=lcol[:, 32 + 2 * 3 + c:32 + 2 * 3 + c + 1])
                stt("dve", f2[:, :], zc[:, c, 1:T + 1], lcol[:, 32 + 3 + c:32 + 3 + c + 1], f2[:, :], ALU.mult, ALU.add,
                    (zc_r[c], lay_r, fr2), (fr2,))
                stt("dve", f2[:, :], zc[:, c, 0:T], lcol[:, 32 + c:32 + c + 1], f2[:, :], ALU.mult, ALU.add,
                    (zc_r[c], lay_r, fr2), (fr2,))
                tt("dve", oT[2][:, c, :], pz[0][0][:, :], f2[:, :], ALU.mult, (pz[0][1], fr2), (oT_r[2],))
                if t == NT - 1:
                    dstc = AP(new_c_p.tensor, new_c_p[l, s].offset + c * 128, [[1, 128], [384, 2]])
                    dma("sp", dstc, zc[:, c, T:T + 2], zc_ds[c], (zc_r[c],), (), nonc=True)
                else:
                    cp("act", zc[:, c, 0:2], zc[:, c, T:T + 2], (zc_r[c],), (zc_r[c],))

            if phases >= 4:
                wtu, wru = wload([(0, [[384, 8], [1, 384]], 128, 0, wsrc(w_in[l], 0, 8, 3712, 384))])
                wtv, wrv = wload([(0, [[384, 8], [1, 384]], 128, 0, wsrc(w_in[l], 0, 8, 4096, 384))])
            dst_ = {}

            def dA(b):
                pu, pur = ps_ring.next()
                pv, pvr = ps_ring.next()
                for kc in range(8):
                    lh = hT[:, kc, b * 128:(b + 1) * 128]
                    mm(pv[:, 0:384], lh, AP(wtv, kc * 384, [[WSLOT, 128], [1, 384]]), kc == 0, kc == 7, (hT_r, wrv), (pvr,))
                for kc in range(8):
                    lh = hT[:, kc, b * 128:(b + 1) * 128]
                    mm(pu[:, 0:384], lh, AP(wtu, kc * 384, [[WSLOT, 128], [1, 384]]), kc == 0, kc == 7, (hT_r, wru), (pur,))
                vb_, vbr = vbf_ring.next()
                cp("act", vb_, pv[:, 0:384], (pvr,), (vbr,))
                dst_[b] = [pu, pur, vb_, vbr]

            def dB(b):
                pu, pur, vb_, vbr = dst_[b]
                psv, psvr = ps_ring.next()
                for gg in range(6):
                    mm(psv[:, gg * 64:(gg + 1) * 64], wmT[:, gg, :], vb_[:, gg * 64:(gg + 1) * 64], True, True,
                       (lay_r, vbr), (psvr,))
                f1, fr1 = f_ring.next()
                bsb = AP(lcol, 41, [[48, 128], [1, 6], [0, 64]])
                tt("dve", f1[:, 0:384].rearrange("p (g e) -> p g e", g=6), psv[:, 0:384].rearrange("p (g e) -> p g e", g=6),
                   bsb, ALU.add, (psvr, lay_r), (fr1,))
                ob_, obr = obf_ring.next()
                tt("dve", ob_, pu[:, 0:384], f1[:, 0:384], ALU.mult, (pur, fr1), (obr,))
                dst_[b] = [ob_, obr]

            def dC(b):
                ob_, obr = dst_[b]
                tp, tpr = tp_ring.next()
                for c in range(3):
                    tr(tp[:, c * 128:(c + 1) * 128], ob_[:, c * 128:(c + 1) * 128], ident[:], (obr, const_r), (tpr,))
                cp("act", oT[3][:, :, b * 128:(b + 1) * 128], tp[:, 0:384].rearrange("p (c q) -> p c q", c=3),
                   (tpr,), (oT_r[3],))

            nb_ = 4 if phases >= 4 else 0
            for i in range(nb_ + 2 if nb_ else 0):
                if i < nb_:
                    dA(i)
                if 0 <= i - 1 < nb_:
                    dB(i - 1)
                if 0 <= i - 2 < nb_:
                    dC(i - 2)

            for _ in dense_tail(PCX, l):
                hook()

            if PCX["done"] is None:
                for b in range(4):
                    _done(b)

        copy_jobs = []
        if do_copy and ns > 0:
            cpd = S.new_dsem("cachecopy")
            for g in (3, 2, 1, 0):
                R = CROWS[g]
                for l in range(depth):
                    for b0 in range(0, ns, 4):
                        copy_jobs.append((new_s[g][l, b0:b0 + 4, :, 0:R - 1, :], caches[g][l, b0:b0 + 4, :, 1:R, :]))

        def issue_copies(n):
            for _ in range(n):
                if copy_jobs:
                    o_, i_ = copy_jobs.pop(0)
                    dma("act", o_, i_, cpd, (), ())

        tl_count = [0]

        def _sample_all():
            for l_ in range(depth):
                yield from sample_tile(l_)
        sgen = [_sample_all() if ns > 0 else None]

        def hook():
            if sgen[0] is not None:
                try:
                    next(sgen[0])
                except StopIteration:
                    sgen[0] = None
        if nseq == 0 or ntiles == 0:
            while sgen[0] is not None:
                hook()
        for s in range(nseq):
            for l in range(depth):
                for t in range(ntiles):
                    prompt_tile(s, l, t)

        while sgen[0] is not None:
            hook()
        issue_copies(len(copy_jobs))
        hoist_wloads(NWS - 1)
        with nc.Block() as block:
            S.emit(block)
    return nc


_W_NAMES = ["ln1", "w_in", "q_norm_a", "k_norm_a", "sink_a", "q_norm_b", "k_norm_b", "conv_c", "ws_d", "bs_d",
            "w_gate", "b_gate", "w_branch", "w_o", "ln2", "w_ffn_in", "w_ffn_out"]


def make_in_maps(inputs, ncores=8, nseq=NSEQ, ns=NS):
    consts = make_consts()
    f = lambda a: np.ascontiguousarray(np.asarray(a, dtype=np.float32))
    maps = []
    for c in range(ncores):
        m = dict(consts)
        m["x_prompt"] = f(inputs["x_prompt"][c * nseq:(c + 1) * nseq])
        m["x_sample"] = f(inputs["x_sample"][c * ns:(c + 1) * ns, 0, :])
        for k in ("cache_a", "cache_b1", "cache_b2", "cache_b3"):
            a = np.asarray(inputs[k])[:, c * ns:(c + 1) * ns]
            m[k] = f(a.reshape(a.shape[0], ns, 2, a.shape[3], 128))
        m["state_c"] = f(np.asarray(inputs["state_c"])[:, c * ns:(c + 1) * ns])
        for k in _W_NAMES:
            m[k] = f(inputs[k])
        maps.append(m)
    return maps


def kernel(**inputs):
    nc = build()
    maps = make_in_maps(inputs)
    res = run_bass_kernel_spmd(nc, maps, core_ids=list(range(8)))
    R = res.results
    cat = lambda k, ax: np.concatenate([np.asarray(r[k]) for r in R], axis=ax)
    outs = []
    outs.append(cat("y_prompt", 0))
    outs.append(cat("y_sample", 0).reshape(8 * NS, 1, D))
    for nm, rows in (("a", 128), ("b1", 128), ("b2", 512), ("b3", 2048)):
        p = cat(f"new_{nm}_prompt", 1)
        outs.append(p.reshape(DEPTH, 8 * NSEQ, 2, rows, 2, 64))
        s_ = cat(f"new_{nm}_sample", 1)
        outs.append(s_.reshape(DEPTH, 8 * NS, 2, rows, 2, 64))
    outs.append(cat("new_c_prompt", 1))
    outs.append(cat("new_c_sample", 1))
    outs.append(cat("new_d_sample", 1).reshape(DEPTH, 8 * NS, 1, 384))
    return tuple(np.ascontiguousarray(o, dtype=np.float32) for o in outs)
```
